# Optimizing a Trainium2 kernel written in Bass

```python
import math
import jax, jax.numpy as jnp
from jax import lax
import numpy as np

D_MODEL = 2048
BATCH = 4
SEQ = 4096
DEPTH = 1

DIFF_HEADS = 8
DIFF_HEAD_DIM = 64
DIFF_V_DIM = 2 * DIFF_HEAD_DIM
DIFF_WIDTH = DIFF_HEADS * DIFF_V_DIM
QK_WIDTH = DIFF_HEADS * 2 * DIFF_HEAD_DIM
CONV_WIDTH = D_MODEL - DIFF_WIDTH
CONV_GROUPS = 8
CONV_KERNEL = 31
MIX_WIDTH = DIFF_WIDTH + CONV_WIDTH
IN_WIDTH = 2 * QK_WIDTH + DIFF_WIDTH + 2 * CONV_WIDTH
D_FF = -(-(8 * D_MODEL) // (3 * 256)) * 256
REL_BUCKETS = 32
REL_MAX_DIST = 128
Q_BLOCK = 128
EPS = 1e-6
MASK_VALUE = -1e30

kernel_name = "hymba_diffattn_conformer_hybrid"


def rmsnorm(x, g):
    xf = x.astype(jnp.float32)
    y = xf * lax.rsqrt(jnp.mean(xf * xf, axis=-1, keepdims=True) + EPS)
    return (y * g.astype(jnp.float32)).astype(x.dtype)


def layernorm(x, g, b):
    xf = x.astype(jnp.float32)
    mu = jnp.mean(xf, axis=-1, keepdims=True)
    var = jnp.mean(jnp.square(xf - mu), axis=-1, keepdims=True)
    y = (xf - mu) * lax.rsqrt(var + EPS)
    return (y * g.astype(jnp.float32) + b.astype(jnp.float32)).astype(x.dtype)


def t5_bucket(q_pos, k_pos):
    n = jnp.maximum(q_pos[:, None] - k_pos[None, :], 0)
    max_exact = REL_BUCKETS // 2
    nf = jnp.maximum(n, 1).astype(jnp.float32)
    large = max_exact + (jnp.log(nf / max_exact) / math.log(REL_MAX_DIST / max_exact)
                         * (REL_BUCKETS - max_exact)).astype(jnp.int32)
    large = jnp.minimum(large, REL_BUCKETS - 1)
    return jnp.where(n < max_exact, n, large)


def diff_attention(q, k, v, rel_table, lam, sub_g, lam_init):
    B, S = q.shape[0], q.shape[1]
    nblk = S // Q_BLOCK
    qb = q.reshape(B, nblk, Q_BLOCK, DIFF_HEADS, 2, DIFF_HEAD_DIM).transpose(1, 0, 2, 3, 4, 5)
    k_pos = jnp.arange(S)
    scale = DIFF_HEAD_DIM ** -0.5

    def one_block(args):
        q_blk, i = args
        q_pos = i * Q_BLOCK + jnp.arange(Q_BLOCK)
        s = jnp.einsum('bqhcd,bkhcd->bhcqk', q_blk, k).astype(jnp.float32) * scale
        bias = rel_table[t5_bucket(q_pos, k_pos)].astype(jnp.float32)
        s = s + jnp.transpose(bias, (2, 0, 1))[None, :, None]
        mask = k_pos[None, :] <= q_pos[:, None]
        s = jnp.where(mask, s, MASK_VALUE)
        p = jax.nn.softmax(s, axis=-1)
        a = p[:, :, 0] - lam * p[:, :, 1]
        return jnp.einsum('bhqk,bkhe->bqhe', a.astype(v.dtype), v)

    o = lax.map(one_block, (qb, jnp.arange(nblk)))
    o = o.transpose(1, 0, 2, 3, 4).reshape(B, S, DIFF_HEADS, DIFF_V_DIM)
    o = rmsnorm(o, sub_g) * (1.0 - lam_init)
    return o.reshape(B, S, DIFF_WIDTH)


def conformer_conv(u, dw_w, dw_b, ln_g, ln_b):
    a, gate = jnp.split(u, 2, axis=-1)
    h = a * jax.nn.sigmoid(gate)
    h = jnp.pad(h, ((0, 0), (CONV_KERNEL - 1, 0), (0, 0)))
    h = lax.conv_general_dilated(h, dw_w[:, None, :].astype(h.dtype), window_strides=(1,),
                                 padding='VALID', dimension_numbers=('NWC', 'WIO', 'NWC'),
                                 feature_group_count=CONV_WIDTH) + dw_b
    h = layernorm(h, ln_g, ln_b)
    return jax.nn.silu(h)


def setup_inputs(seed: int = 0) -> dict:
    key = jax.random.key(seed)
    ks = jax.random.split(key, 24)
    f32 = jnp.float32
    L = DEPTH

    def nrm(k, shape, scale):
        return jax.random.normal(k, shape, f32) * scale

    def gain(k, shape):
        return 1.0 + 0.01 * jax.random.normal(k, shape, f32)

    return {
        "x": jax.random.normal(ks[0], (BATCH, SEQ, D_MODEL), f32),
        "w_in": nrm(ks[1], (L, D_MODEL, IN_WIDTH), D_MODEL ** -0.5),
        "lambda_q1": nrm(ks[2], (L, DIFF_HEAD_DIM), 0.1),
        "lambda_k1": nrm(ks[3], (L, DIFF_HEAD_DIM), 0.1),
        "lambda_q2": nrm(ks[4], (L, DIFF_HEAD_DIM), 0.1),
        "lambda_k2": nrm(ks[5], (L, DIFF_HEAD_DIM), 0.1),
        "subln_g": gain(ks[6], (L, DIFF_V_DIM)),
        "conv_dw_w": nrm(ks[7], (L, CONV_KERNEL, CONV_WIDTH), CONV_KERNEL ** -0.5),
        "conv_dw_b": nrm(ks[8], (L, CONV_WIDTH), 0.02),
        "conv_ln_g": gain(ks[9], (L, CONV_WIDTH)),
        "conv_ln_b": nrm(ks[10], (L, CONV_WIDTH), 0.02),
        "w_out": nrm(ks[11], (L, MIX_WIDTH, D_MODEL), MIX_WIDTH ** -0.5),
        "rel_bias": nrm(ks[12], (REL_BUCKETS, DIFF_HEADS), 0.5),
        "norm_pre_mix": gain(ks[13], (L, D_MODEL)),
        "norm_post_mix": gain(ks[14], (L, D_MODEL)),
        "norm_pre_ffn": gain(ks[15], (L, D_MODEL)),
        "norm_post_ffn": gain(ks[16], (L, D_MODEL)),
        "w_gate": nrm(ks[17], (L, D_MODEL, D_FF), D_MODEL ** -0.5),
        "w_up": nrm(ks[18], (L, D_MODEL, D_FF), D_MODEL ** -0.5),
        "w_down": nrm(ks[19], (L, D_FF, D_MODEL), D_FF ** -0.5),
    }


def reference(x, w_in, lambda_q1, lambda_k1, lambda_q2, lambda_k2, subln_g,
              conv_dw_w, conv_dw_b, conv_ln_g, conv_ln_b, w_out, rel_bias,
              norm_pre_mix, norm_post_mix, norm_pre_ffn, norm_post_ffn,
              w_gate, w_up, w_down):
    B, S = x.shape[0], x.shape[1]
    for l in range(DEPTH):
        lam_init = 0.8 - 0.6 * math.exp(-0.3 * l)
        h = rmsnorm(x, norm_pre_mix[l])
        y = jnp.einsum('bsd,de->bse', h, w_in[l])
        q = y[..., :QK_WIDTH].reshape(B, S, DIFF_HEADS, 2, DIFF_HEAD_DIM)
        k = y[..., QK_WIDTH:2 * QK_WIDTH].reshape(B, S, DIFF_HEADS, 2, DIFF_HEAD_DIM)
        v = y[..., 2 * QK_WIDTH:2 * QK_WIDTH + DIFF_WIDTH].reshape(B, S, DIFF_HEADS, DIFF_V_DIM)
        u = y[..., 2 * QK_WIDTH + DIFF_WIDTH:]
        lam = (jnp.exp(jnp.sum(lambda_q1[l] * lambda_k1[l]).astype(jnp.float32))
               - jnp.exp(jnp.sum(lambda_q2[l] * lambda_k2[l]).astype(jnp.float32))
               + lam_init)
        o_attn = diff_attention(q, k, v, rel_bias, lam, subln_g[l], lam_init)
        o_conv = conformer_conv(u, conv_dw_w[l], conv_dw_b[l], conv_ln_g[l], conv_ln_b[l])
        mix = jnp.concatenate([o_attn, o_conv.astype(o_attn.dtype)], axis=-1)
        mix = jnp.einsum('bse,ed->bsd', mix, w_out[l])
        x = x + rmsnorm(mix, norm_post_mix[l])
        h = rmsnorm(x, norm_pre_ffn[l])
        g = jnp.einsum('bsd,df->bsf', h, w_gate[l])
        up = jnp.einsum('bsd,df->bsf', h, w_up[l])
        f = jnp.einsum('bsf,fd->bsd', jax.nn.silu(g) * up, w_down[l])
        x = x + rmsnorm(f, norm_post_ffn[l])
    return x
```

```python
import math
import os
from contextlib import ExitStack

import numpy as np
import concourse.bass as bass
import concourse.mybir as mybir
from concourse.bass_utils import run_bass_kernel_spmd

F32 = mybir.dt.float32
BF16 = mybir.dt.bfloat16
AF = mybir.ActivationFunctionType
ALU = mybir.AluOpType

D = 2048
DC = 16
S_LEN = 4096
NH = 8
FF = 5632
FC = 44
IN_W = 5120
EPS = 1e-6
LAM_INIT = 0.2
NTAP = 31
MASK = -1e30
N_CORES = 8


class Sched:
    ENGS = ("pe", "act", "dve", "pool", "sp")

    def __init__(self):
        self.ops = []
        self.last_writer = {}
        self.readers = {}
        self.last_eng = {}
        self.open_dmas = []
        self.bg_dmas = []

    def op(self, eng, fn, reads=(), writes=(), dma=None, extra_deps=(), bg=False):
        deps = set(extra_deps)
        for k in reads:
            w = self.last_writer.get(k)
            if w is not None:
                deps.add(w)
        for k in writes:
            w = self.last_writer.get(k)
            if w is not None:
                deps.add(w)
            for r in self.readers.get(k, ()):
                deps.add(r)
        idx = len(self.ops)
        self.ops.append(dict(eng=eng, fn=fn, deps=deps, dma=dma, cons=False, sig=None))
        for k in reads:
            self.readers.setdefault(k, []).append(idx)
        for k in writes:
            self.last_writer[k] = idx
            self.readers[k] = []
        if fn is not None:
            if dma is None:
                self.last_eng[eng] = idx
            elif bg:
                self.bg_dmas.append(idx)
            else:
                self.open_dmas.append(idx)
        return idx

    def barrier(self, final=False):
        deps = set(self.last_eng.values()) | set(self.open_dmas)
        if final:
            deps |= set(self.bg_dmas)
            self.bg_dmas = []
        self.open_dmas = []
        for e in self.ENGS:
            self.op(e, None, extra_deps=deps)
        self.last_writer = {}
        self.readers = {}

    @staticmethod
    def _needs_sync(P, C):
        if P["dma"] is None and C["dma"] is None and P["eng"] == C["eng"] == "pe":
            return False
        return True

    def emit(self, nc, stack):
        ops = self.ops
        for o in ops:
            for d in o["deps"]:
                if self._needs_sync(ops[d], o):
                    ops[d]["cons"] = True
        counts = {}
        for o in ops:
            if not o["cons"]:
                continue
            name = ("e_" + o["eng"]) if o["dma"] is None else ("d_" + o["dma"])
            inc = 1 if o["dma"] is None else 16
            counts[name] = counts.get(name, 0) + inc
            o["sig"] = (name, counts[name], inc)
        per_eng = {e: [] for e in self.ENGS}
        waited = {e: {} for e in self.ENGS}
        for o in ops:
            need = {}
            for d in o["deps"]:
                P = ops[d]
                if not self._needs_sync(P, o):
                    continue
                name, val, _ = P["sig"]
                if need.get(name, 0) < val:
                    need[name] = val
            waits = []
            for name, val in need.items():
                if waited[o["eng"]].get(name, 0) < val:
                    waited[o["eng"]][name] = val
                    waits.append((name, val))
            per_eng[o["eng"]].append((waits, o["fn"], o["sig"]))
        sems = {}
        for name in counts:
            sems[name] = stack.enter_context(nc.semaphore("s_" + name))
        block = stack.enter_context(nc.Block())

        def make(engname):
            def body(eng):
                for waits, fn, sig in per_eng[engname]:
                    for name, val in waits:
                        eng.wait_ge(sems[name], val)
                    if fn is None:
                        continue
                    ins = fn(eng)
                    if sig is not None:
                        ins.then_inc(sems[sig[0]], sig[2])
            return body

        block.tensor(make("pe"))
        block.scalar(make("act"))
        block.vector(make("dve"))
        block.gpsimd(make("pool"))
        block.sync(make("sp"))
        self.counts = counts


class Arena:
    def __init__(self, tensor, nbytes):
        self.t = tensor
        self.n = nbytes
        self.off = 0

    def alloc(self, free_shape, dt):
        n = 1
        for s in free_shape:
            n *= s
        esz = 4 if dt == F32 else 2
        nb = n * esz
        a = self.off
        self.off = (a + nb + 63) // 64 * 64
        assert self.off <= self.n, ("SBUF arena overflow", self.off, self.n)
        ap = self.t[:, a // 4:(a + nb) // 4]
        if dt != F32:
            ap = ap.bitcast(dt)
        if len(free_shape) == 2:
            ap = ap.rearrange("p (a b) -> p a b", b=free_shape[1])
        elif len(free_shape) == 3:
            ap = ap.rearrange("p (a b c) -> p a b c", b=free_shape[1], c=free_shape[2])
        return ap

    def at(self, off, free_shape, dt):
        save = self.off
        self.off = off
        ap = self.alloc(free_shape, dt)
        self.off = save
        return ap

    def mark(self):
        return self.off

    def reset(self, m):
        self.off = m


def build_nc(debug=False, stop_after=None, phases="ABCDE", nheads=NH):
    nc = bass.Bass("TRN2", target_bir_lowering=False)
    skind = "ExternalOutput" if debug else "Internal"

    def din(name, shape, dt=F32):
        return nc.dram_tensor(name, shape, dt, kind="ExternalInput").ap()

    x_in = din("x", [33 * 128, D])
    w_in = din("w_in", [D, IN_W])
    w_out = din("w_out", [D, D])
    w_gate = din("w_gate", [D, FF])
    w_up = din("w_up", [D, FF])
    w_down = din("w_down", [FF, D])
    p_small = din("p_small", [128, 16 + 16 + 248 + 8 + 8 + 8 + 1 + 8 + 256])
    g_post = din("g_post", [2, 128, D])
    bias_t = din("bias_t", [NH, 128, 9 * 512])
    out = nc.dram_tensor("out", [2048, D], F32, kind="ExternalOutput").ap()

    KT = nc.dram_tensor("KT", [NH, 128, S_LEN], BF16, kind=skind).ap()
    VS = nc.dram_tensor("VS", [NH, 128, 32, 128], BF16, kind=skind).ap()
    QT = nc.dram_tensor("QT", [NH, 2, 128, 2048], BF16, kind=skind).ap()
    CS = nc.dram_tensor("CS", [8, 128, 2048], F32, kind=skind).ap()
    MT = nc.dram_tensor("MT", [16, 128, 2048], BF16, kind=skind).ap()
    H2 = nc.dram_tensor("H2", [128, DC, 2048], BF16, kind=skind).ap()
    WDB = nc.dram_tensor("WDB", [FC, 128, D], BF16, kind="Internal").ap()

    w_in_v = w_in.rearrange("(dc p) c -> p dc c", p=128)
    w_gate_v = w_gate.rearrange("(dc p) c -> p dc c", p=128)
    w_up_v = w_up.rearrange("(dc p) c -> p dc c", p=128)

    st = ExitStack()
    with st:
        ARENA_BYTES = 206 * 1024
        arena_t = st.enter_context(nc.sbuf_tensor("arena", [128, ARENA_BYTES // 4], F32))
        A = Arena(arena_t, ARENA_BYTES)
        psum_t = st.enter_context(nc.psum_tensor("psum", [128, 8 * 512], F32))

        def bank(b, n=512, off=0):
            return psum_t[:, 512 * b + off:512 * b + off + n]

        S = Sched()

        identf = A.alloc([128], F32)
        ident = A.alloc([128], BF16)
        ones_bf = A.alloc([128], BF16)
        ones_f = A.alloc([128], F32)
        small = A.alloc([16 + 16 + 248 + 8 + 8 + 8 + 1 + 8 + 256], F32)
        gBm = A.alloc([DC, 128], F32)
        gBf = A.alloc([DC, 128], F32)
        misc = A.alloc([16], F32)
        o0 = 0
        gpm = small[:, 0:16]
        gpf = small[:, 16:32]
        cw = small[:, 32:280].rearrange("p (c j) -> p c j", j=NTAP)
        cb = small[:, 280:288]
        lng = small[:, 288:296]
        lnb = small[:, 296:304]
        subg = small[:, 304:305]
        relc = small[:, 305:313]
        lamp = small[:, 313:569].rearrange("p (k d) -> p k d", d=64)
        neglam = misc[:, 0:1]
        negc = misc[:, 1:9]
        subg08 = misc[:, 9:10]
        lam_t = misc[:, 10:14]

        S.op("sp", lambda e: e.dma_start(out=small, in_=p_small), writes=["small"], dma="small")
        S.op("pool", lambda e: e.memset(identf, 0.0), writes=["identf"])
        S.op("pool", lambda e: e.affine_select(out=identf, in_=identf, pattern=[[-1, 128]],
                                               compare_op=ALU.not_equal, fill=1.0, base=0,
                                               channel_multiplier=1),
             reads=["identf"], writes=["identf"])
        S.op("dve", lambda e: e.tensor_copy(out=ident, in_=identf), reads=["identf"], writes=["ident"])
        S.op("dve", lambda e: e.memset(ones_bf, 1.0), writes=["ones_bf"])
        S.op("dve", lambda e: e.memset(ones_f, 1.0), writes=["ones_f"])
        for dc in range(DC):
            S.op("dve", (lambda dc: lambda e: e.tensor_scalar(out=gBm[:, dc, :], in0=ones_f, scalar1=gpm[:, dc:dc + 1],
                                                               scalar2=None, op0=ALU.mult))(dc),
                 reads=["small", "ones_f"], writes=[("gBm", dc)])
            S.op("dve", (lambda dc: lambda e: e.tensor_scalar(out=gBf[:, dc, :], in0=ones_f, scalar1=gpf[:, dc:dc + 1],
                                                               scalar2=None, op0=ALU.mult))(dc),
                 reads=["small", "ones_f"], writes=[("gBf", dc)])
        gBm_keys = [("gBm", dc) for dc in range(DC)]
        gBf_keys = [("gBf", dc) for dc in range(DC)]
        lj = A.alloc([2, 64], F32)
        S.op("dve", lambda e: e.tensor_tensor(out=lj[:, 0, :], in0=lamp[:, 0, :], in1=lamp[:, 1, :], op=ALU.mult),
             reads=["small"], writes=["lj0"])
        S.op("dve", lambda e: e.tensor_tensor(out=lj[:, 1, :], in0=lamp[:, 2, :], in1=lamp[:, 3, :], op=ALU.mult),
             reads=["small"], writes=["lj1"])
        S.op("dve", lambda e: e.reduce_sum(out=lam_t[:, 0:1], in_=lj[:, 0, :], axis=mybir.AxisListType.X),
             reads=["lj0"], writes=["lam0"])
        S.op("dve", lambda e: e.reduce_sum(out=lam_t[:, 1:2], in_=lj[:, 1, :], axis=mybir.AxisListType.X),
             reads=["lj1"], writes=["lam1"])
        S.op("act", lambda e: e.activation(out=lam_t[:, 2:4], in_=lam_t[:, 0:2], func=AF.Exp),
             reads=["lam0", "lam1"], writes=["lam2"])
        S.op("dve", lambda e: e.tensor_tensor(out=lam_t[:, 0:1], in0=lam_t[:, 3:4], in1=lam_t[:, 2:3], op=ALU.subtract),
             reads=["lam2"], writes=["lam3"])
        S.op("dve", lambda e: e.tensor_scalar(out=neglam, in0=lam_t[:, 0:1], scalar1=-LAM_INIT, scalar2=None, op0=ALU.add),
             reads=["lam3"], writes=["neglam"])
        S.op("dve", lambda e: e.tensor_scalar(out=negc, in0=relc, scalar1=-1.0, scalar2=None, op0=ALU.mult),
             reads=["small"], writes=["negc"])
        S.op("dve", lambda e: e.tensor_scalar(out=subg08, in0=subg, scalar1=1.0 - LAM_INIT, scalar2=None, op0=ALU.mult),
             reads=["small"], writes=["subg08"])
        base_mark = A.mark()

        def rstd_ops(tag, ss_ap, n, rs_ap, tmp_ap, ss_key, rs_key):
            S.op("dve", lambda e: e.tensor_scalar(out=tmp_ap, in0=ss_ap, scalar1=1.0 / n, scalar2=EPS,
                                                  op0=ALU.mult, op1=ALU.add),
                 reads=[ss_key], writes=[(tag, "ms")])
            S.op("act", lambda e: e.activation(out=tmp_ap, in_=tmp_ap, func=AF.Ln),
                 reads=[(tag, "ms")], writes=[(tag, "sq")])
            S.op("act", lambda e: e.activation(out=rs_ap, in_=tmp_ap, func=AF.Exp, scale=-0.5), reads=[(tag, "sq")], writes=[rs_key])

        def build_hT(tiles, hT, gB, gB_keys, xt, xn, sqj, stat, pt_banks, hook=None):
            for j, tt in enumerate(tiles):
                if hook is not None:
                    hook(j)
                s = j % len(xt)
                S.op("sp", (lambda s, tt: lambda e: e.dma_start(out=xt[s], in_=x_in[128 * tt:128 * tt + 128, :]))(s, tt),
                     writes=[("xt", s)], dma="xt%d" % s)
                S.op("act", (lambda s: lambda e: e.activation(out=sqj, in_=xt[s], func=AF.Square,
                                                             accum_out=stat[:, 4 * s:4 * s + 1]))(s),
                     reads=[("xt", s)], writes=["sqj", ("ss", s)])
                rstd_ops(("h", s), stat[:, 4 * s:4 * s + 1], D, stat[:, 4 * s + 2:4 * s + 3], stat[:, 4 * s + 1:4 * s + 2],
                         ("ss", s), ("rs", s))
                S.op("dve", (lambda s: lambda e: e.tensor_scalar(out=xn[s], in0=xt[s], scalar1=stat[:, 4 * s + 2:4 * s + 3],
                                                                scalar2=None, op0=ALU.mult))(s),
                     reads=[("xt", s), ("rs", s)], writes=[("xn", s)])
                pb = pt_banks[s]
                pT = psum_t[:, 512 * pb:512 * pb + 1024].bitcast(BF16).rearrange("p (a b) -> p a b", b=128)

                def tr(e, s=s, pT=pT):
                    for dc in range(DC):
                        ins = e.transpose(pT[:, dc, :], xn[s][:, 128 * dc:128 * dc + 128], ident)
                    return ins
                S.op("pe", tr, reads=[("xn", s), "ident"], writes=[("ps", pb), ("ps", pb + 1)])
                S.op("dve", (lambda j, pT: lambda e: e.tensor_tensor(out=hT[:, :, 128 * j:128 * j + 128], in0=pT, in1=gB,
                                                                     op=ALU.mult))(j, pT),
                     reads=[("ps", pb), ("ps", pb + 1)] + gB_keys, writes=[("hT", j)])

        def load_wchunk(dst, src_view, c0, ncol, key, slot):
            S.op("pool", lambda e: e.dma_start(out=dst, in_=src_view[:, :, c0:c0 + ncol]),
                 writes=[(key, slot)], dma="%s%d" % (key, slot))

        def proj_fm(wt, wkey, hT, hkeys, col0, ps_b):
            def fn(e):
                for dc in range(DC):
                    ins = e.matmul(bank(ps_b), lhsT=wt[:, dc, :], rhs=hT[:, dc, col0:col0 + 512],
                                   start=(dc == 0), stop=(dc == DC - 1))
                return ins
            S.op("pe", fn, reads=[wkey] + hkeys, writes=[("ps", ps_b)])

        wdb_ops = []

        def precast_wd(n):
            for _ in range(n):
                fc = len(wdb_ops)
                if fc >= FC:
                    return
                wdb_ops.append(S.op("pool", (lambda fc: lambda e: e.dma_start(out=WDB[fc], in_=w_down[128 * fc:128 * fc + 128, :]))(fc),
                                    dma="wdb%d" % (fc % 4), bg=True))

        def build_bufs():
            xt = [A.alloc([D], F32) for _ in range(4)]
            xn = [A.alloc([D], BF16) for _ in range(4)]
            sqj = A.alloc([D], BF16)
            stat = A.alloc([16], F32)
            return xt, xn, sqj, stat

        def phase_proj(own, prebuilt=False):
            A.reset(base_mark)
            ntile = 17 if own else 16
            tiles = list(range(0, 17)) if own else list(range(17, 33))
            hT = A.alloc([DC, ntile * 128], BF16)
            m1 = A.mark()
            if not prebuilt:
                xt, xn, sqj, stat = build_bufs()
                build_hT(tiles, hT, gBm, gBm_keys, xt, xn, sqj, stat, [0, 2, 4, 6])
                S.barrier()
            A.reset(m1)
            wc = [A.alloc([DC, 128], BF16) for _ in range(4)]
            wv = [A.alloc([DC, 256], BF16) for _ in range(2)]
            stg = [A.alloc([2048], BF16) for _ in range(2)]
            qz = [A.alloc([2, 2048], BF16) for _ in range(2)] if own else None
            if own:
                for q_ in range(2):
                    S.op("pool", (lambda q_: lambda e: e.memset(qz[q_], 0.0))(q_), writes=[("qz", q_, b_) for b_ in range(4)])
            vstg = [A.alloc([2, 16, 128], BF16) for _ in range(2)]
            hk = lambda blk: [("hT", 4 * blk + t) for t in range(4)]
            pair_half = 0 if own else 1
            chunks = []
            if own:
                chunks += [("q", h) for h in range(NH)]
            chunks += [("k", h) for h in range(NH)]
            pb_rot = 0
            for ci, (kind, h) in enumerate(chunks):
                ws = ci % 4
                col = (0 if kind == "q" else 1024) + 128 * h
                load_wchunk(wc[ws], w_in_v, col, 128, "wc", ws)
                ss_ = ci % 2
                for blk in range(4):
                    pb = 4 + (pb_rot % 4)
                    pb_rot += 1
                    proj_fm(wc[ws], ("wc", ws), hT, hk(blk), 512 * blk, pb)
                    if kind == "q":
                        S.op("act", (lambda ss_, blk, pb: lambda e: e.activation(
                            out=qz[ss_][0:64, 0, 512 * blk:512 * blk + 512], in_=bank(pb)[0:64, :], func=AF.Copy, scale=0.125))(ss_, blk, pb),
                            reads=[("ps", pb), ("qz", ss_, blk)], writes=[("qza", ss_, blk)])
                        S.op("dve", (lambda ss_, blk, pb: lambda e: e.tensor_scalar(
                            out=qz[ss_][64:128, 1, 512 * blk:512 * blk + 512], in0=bank(pb)[64:128, :], scalar1=0.125, scalar2=None,
                            op0=ALU.mult))(ss_, blk, pb),
                            reads=[("ps", pb), ("qz", ss_, blk), ("qza", ss_, blk)], writes=[("qzb", ss_, blk)])
                    else:
                        eng = "act" if blk % 2 == 0 else "dve"
                        if eng == "act":
                            S.op("act", (lambda ss_, blk, pb: lambda e: e.activation(
                                out=stg[ss_][:, 512 * blk:512 * blk + 512], in_=bank(pb), func=AF.Copy))(ss_, blk, pb),
                                reads=[("ps", pb)], writes=[("stg", ss_, blk)])
                        else:
                            S.op("dve", (lambda ss_, blk, pb: lambda e: e.tensor_copy(
                                out=stg[ss_][:, 512 * blk:512 * blk + 512], in_=bank(pb)))(ss_, blk, pb),
                                reads=[("ps", pb)], writes=[("stg", ss_, blk)])
                if kind == "q":
                    S.op("sp", (lambda h, ss_: lambda e: e.dma_start(out=QT[h].rearrange("c p t -> p c t"), in_=qz[ss_]))(h, ss_),
                         reads=[(k_, ss_, b_) for b_ in range(4) for k_ in ("qza", "qzb")],
                         writes=[("scr", kind, h)] + [(k_, ss_, b_) for b_ in range(4) for k_ in ("qza", "qzb")], dma="qzo%d" % ss_)
                    continue
                else:
                    dst = KT[h].rearrange("p (i two c) -> p i two c", two=2, c=512)[:, :, pair_half, :]
                    src = stg[ss_].rearrange("p (i c) -> p i c", c=512)
                S.op("sp", (lambda dst, src: lambda e: e.dma_start(out=dst, in_=src))(dst, src),
                     reads=[("stg", ss_, b_) for b_ in range(4)], writes=[("scr", kind, h)], dma="stgo%d" % ss_)
            for vg in range(4):
                ws = vg % 2
                load_wchunk(wv[ws], w_in_v, 2048 + 256 * vg, 256, "wv", ws)
                for t in range(16):
                    pb = 4 + (pb_rot % 4)
                    pb_rot += 1

                    def fn(e, t=t, pb=pb, ws=ws):
                        for dc in range(DC):
                            ins = e.matmul(bank(pb, 256), lhsT=hT[:, dc, 128 * t:128 * t + 128], rhs=wv[ws][:, dc, :],
                                           start=(dc == 0), stop=(dc == DC - 1))
                        return ins
                    S.op("pe", fn, reads=[("wv", ws), ("hT", t)], writes=[("ps", pb)])
                    src = bank(pb, 256).rearrange("p (a b) -> p a b", b=128)
                    if t % 2 == 0:
                        S.op("act", (lambda ws, t, src: lambda e: e.activation(out=vstg[ws][:, :, t, :], in_=src, func=AF.Copy))(ws, t, src),
                             reads=[("ps", pb)], writes=[("vstg", ws, t)])
                    else:
                        S.op("dve", (lambda ws, t, src: lambda e: e.tensor_copy(out=vstg[ws][:, :, t, :], in_=src))(ws, t, src),
                             reads=[("ps", pb)], writes=[("vstg", ws, t)])
                for hh in range(2):
                    h = 2 * vg + hh
                    dst = VS[h].rearrange("p (i j) e -> p i j e", j=8)[:, :, 4 * pair_half:4 * pair_half + 4, :]
                    src = vstg[ws][:, hh, :, :].rearrange("p (i t) e -> p i t e", t=4)
                    S.op("sp", (lambda dst, src: lambda e: e.dma_start(out=dst, in_=src))(dst, src),
                         reads=[("vstg", ws, t) for t in range(16)], writes=[("scr", "v", h)], dma="vstgo%d_%d" % (ws, hh))
            if not own:
                S.barrier()
                return
            S.barrier()
            A.reset(m1)
            NDVE = 22
            NPE = NTAP - NDVE
            wc = [A.alloc([DC, 128], BF16) for _ in range(4)]
            hg = [A.alloc([4, 544], F32) for _ in range(2)]
            sga = A.alloc([4, 544], F32)
            accA = A.alloc([4, 512], F32)
            off_hgh = A.mark()
            hgh = [A.alloc([4, 544], BF16) for _ in range(2)]
            hgl = [A.alloc([4, 544], BF16) for _ in range(2)]
            off_dw = A.mark()
            dwh = [A.alloc([NPE, 128], BF16) for _ in range(2)]
            dwl = [A.alloc([NPE, 128], BF16) for _ in range(2)]
            cres = [A.alloc([2048], F32) for _ in range(2)]
            sqt = A.alloc([2048], F32)
            sumacc = A.alloc([2048], F32)
            sqacc = A.alloc([2048], F32)

            def fnh(e, wt, pb):
                for dc in range(DC):
                    ins = e.matmul(bank(pb, 128), lhsT=wt[:, dc, :], rhs=hT[:, dc, 2048:2176],
                                   start=(dc == 0), stop=(dc == DC - 1))
                return ins

            def stage_u(cc):
                wa, wg_ = wc[(2 * cc) % 4], wc[(2 * cc + 1) % 4]
                ka, kg = ("wc", (2 * cc) % 4), ("wc", (2 * cc + 1) % 4)
                load_wchunk(wa, w_in_v, 3072 + 128 * cc, 128, "wc", (2 * cc) % 4)
                load_wchunk(wg_, w_in_v, 4096 + 128 * cc, 128, "wc", (2 * cc + 1) % 4)
                precast_wd(6)
                hs = cc % 2
                for blk in range(4):
                    pa, pg = 4 + 2 * (blk % 2), 5 + 2 * (blk % 2)
                    proj_fm(wa, ka, hT, hk(blk), 512 * blk, pa)
                    proj_fm(wg_, kg, hT, hk(blk), 512 * blk, pg)
                    S.op("act", (lambda blk, pg: lambda e: e.activation(out=sga[:, blk, 32:544], in_=bank(pg), func=AF.Sigmoid))(blk, pg),
                         reads=[("ps", pg)], writes=[("sga", blk)])
                    S.op("act", (lambda blk, pa, hs: lambda e: e.activation(out=hg[hs][:, blk, 32:544], in_=bank(pa), func=AF.Copy))(blk, pa, hs),
                         reads=[("ps", pa)], writes=[("hga", hs, blk)])
                S.op("pe", (lambda wa: lambda e: fnh(e, wa, 4))(wa), reads=[ka, ("hT", 16)], writes=[("ps", 4)])
                S.op("pe", (lambda wg_: lambda e: fnh(e, wg_, 5))(wg_), reads=[kg, ("hT", 16)], writes=[("ps", 5)])
                S.op("act", lambda e: e.activation(out=sga[:, :, 0:32], in_=bank(5, 128).rearrange("p (a b) -> p a b", b=32), func=AF.Sigmoid),
                     reads=[("ps", 5)], writes=[("sga", "halo")])
                S.op("act", (lambda hs: lambda e: e.activation(out=hg[hs][:, :, 0:32], in_=bank(4, 128).rearrange("p (a b) -> p a b", b=32),
                                                               func=AF.Copy))(hs),
                     reads=[("ps", 4)], writes=[("hga", hs, "halo")])

            def stage_g(cc):
                hs = cc % 2
                hak = [("hga", hs, blk) for blk in range(4)] + [("hga", hs, "halo")]
                sgk = [("sga", blk) for blk in range(4)] + [("sga", "halo")]
                S.op("dve", (lambda hs: lambda e: e.tensor_tensor(out=hg[hs], in0=hg[hs], in1=sga, op=ALU.mult))(hs),
                     reads=hak + sgk, writes=[("hg", hs)] + hak)
                S.op("act", (lambda hs: lambda e: e.activation(out=hgh[hs], in_=hg[hs], func=AF.Copy))(hs), reads=[("hg", hs)], writes=[("hgh", hs)])
                S.op("dve", (lambda hs: lambda e: e.tensor_tensor(out=hgl[hs], in0=hg[hs], in1=hgh[hs], op=ALU.subtract))(hs),
                     reads=[("hg", hs), ("hgh", hs)], writes=[("hgl", hs)])
                for jj in range(NPE):
                    j = NDVE + jj
                    wj = cw[:, cc, j:j + 1]
                    S.op("act", (lambda hs, jj, wj: lambda e: e.activation(out=dwh[hs][:, jj, :], in_=identf, func=AF.Copy, scale=wj))(hs, jj, wj),
                         reads=["identf", "small"], writes=[("dwh", hs, jj)])
                    S.op("dve", (lambda hs, jj, wj: lambda e: e.scalar_tensor_tensor(out=dwl[hs][:, jj, :], in0=identf, scalar=wj, in1=dwh[hs][:, jj, :],
                                                                                      op0=ALU.mult, op1=ALU.subtract))(hs, jj, wj),
                         reads=["identf", "small", ("dwh", hs, jj)], writes=[("dwl", hs, jj)])

            def stage_t(cc):
                hs = cc % 2
                cs = cc % 2
                for blk in range(4):
                    def pconv(e, hs=hs, blk=blk):
                        n = 0
                        for jj in range(NPE):
                            j = NDVE + jj
                            for (wt, ht) in ((dwh, hgh), (dwh, hgl), (dwl, hgh)):
                                ins = e.matmul(bank(blk), lhsT=wt[hs][:, jj, :], rhs=ht[hs][:, blk, 2 + j:2 + j + 512],
                                               start=(n == 0), stop=(n == 3 * NPE - 1))
                                n += 1
                        return ins
                    S.op("pe", pconv, reads=[("hgh", hs), ("hgl", hs)] + [(k_, hs, jj) for jj in range(NPE) for k_ in ("dwh", "dwl")],
                         writes=[("ps", blk)])
                for j in range(NDVE):
                    src = hg[hs][:, :, 2 + j:2 + j + 512]
                    wj = cw[:, cc, j:j + 1]
                    if j == 0:
                        S.op("dve", (lambda src, wj, cc: lambda e: e.tensor_scalar(out=accA, in0=src, scalar1=wj, scalar2=cb[:, cc:cc + 1],
                                                                                   op0=ALU.mult, op1=ALU.add))(src, wj, cc),
                             reads=[("hg", hs), "small"], writes=["accA"])
                    else:
                        S.op("dve", (lambda src, wj: lambda e: e.scalar_tensor_tensor(out=accA, in0=src, scalar=wj, in1=accA,
                                                                                      op0=ALU.mult, op1=ALU.add))(src, wj),
                             reads=[("hg", hs), "small", "accA"], writes=["accA"])
                S.op("dve", (lambda cs: lambda e: e.tensor_tensor(out=cres[cs], in0=accA.rearrange("p a b -> p (a b)"), in1=psum_t[:, 0:2048], op=ALU.add))(cs),
                     reads=["accA"] + [("ps", b_) for b_ in range(4)], writes=[("cres", cs)])
                S.op("sp", (lambda cs, cc: lambda e: e.dma_start(out=CS[cc], in_=cres[cs]))(cs, cc),
                     reads=[("cres", cs)], writes=[("scr", "cs", cc)], dma="cres%d" % cs)
                S.op("act", (lambda cs: lambda e: e.activation(out=sqt, in_=cres[cs], func=AF.Square))(cs),
                     reads=[("cres", cs)], writes=["sqt"])

            def stage_s(cc):
                cs = cc % 2
                for blk in range(4):
                    for which, srcb, acc, key in ((0, cres[cs], sumacc, ("cres", cs)), (1, sqt, sqacc, "sqt")):
                        pb = 6 + which
                        S.op("pe", (lambda srcb, blk, pb: lambda e: e.matmul(bank(pb), lhsT=ones_f, rhs=srcb[:, 512 * blk:512 * blk + 512],
                                                                           start=True, stop=True))(srcb, blk, pb),
                             reads=[key, "ones_f"], writes=[("ps", pb)])
                        if cc == 0:
                            S.op("dve", (lambda acc, blk, pb: lambda e: e.tensor_copy(out=acc[:, 512 * blk:512 * blk + 512], in_=bank(pb)))(acc, blk, pb),
                                 reads=[("ps", pb)], writes=[("acc", which, blk)])
                        else:
                            S.op("dve", (lambda acc, blk, pb: lambda e: e.tensor_tensor(out=acc[:, 512 * blk:512 * blk + 512],
                                                                                       in0=acc[:, 512 * blk:512 * blk + 512], in1=bank(pb), op=ALU.add))(acc, blk, pb),
                                 reads=[("ps", pb), ("acc", which, blk)], writes=[("acc", which, blk)])

            stage_u(0)
            stage_g(0)
            for cc in range(8):
                if cc + 1 < 8:
                    stage_u(cc + 1)
                stage_t(cc)
                if cc + 1 < 8:
                    stage_g(cc + 1)
                stage_s(cc)
            S.barrier()
            acck = [("acc", w_, b_) for w_ in range(2) for b_ in range(4)]
            mean = sumacc
            rstd = sqacc
            m2 = sqt
            S.op("dve", lambda e: e.tensor_scalar(out=mean, in0=sumacc, scalar1=1.0 / 1024, scalar2=None, op0=ALU.mult),
                 reads=acck, writes=["mean"])
            S.op("dve", lambda e: e.tensor_tensor(out=m2, in0=mean, in1=mean, op=ALU.mult), reads=["mean"], writes=["m2", "sqt"])
            S.op("dve", lambda e: e.scalar_tensor_tensor(out=rstd, in0=sqacc, scalar=1.0 / 1024, in1=m2, op0=ALU.mult, op1=ALU.subtract),
                 reads=acck + ["m2"], writes=["var"])
            S.op("dve", lambda e: e.tensor_scalar(out=rstd, in0=rstd, scalar1=EPS, scalar2=None, op0=ALU.add), reads=["var"], writes=["var2"])
            S.op("act", lambda e: e.activation(out=rstd, in_=rstd, func=AF.Sqrt), reads=["var2"], writes=["sd"])
            S.op("dve", lambda e: e.reciprocal(out=rstd, in_=rstd), reads=["sd"], writes=["rstd"])
            cl = cres
            tmp = [A.at(off_hgh, [2048], F32), A.at(off_hgh + 8192, [2048], F32)]
            mst = [A.at(off_dw, [2048], BF16), A.at(off_dw + 4096, [2048], BF16)]
            A.reset(base_mark)
            hT_b = A.alloc([DC, 16 * 128], BF16)
            xt_b, xn_b, sqj_b, stat_b = build_bufs()
            assert A.mark() <= off_hgh, (A.mark(), off_hgh)

            def ln_chunk(cc):
                s = cc % 2
                S.op("pool", (lambda s, cc: lambda e: e.dma_start(out=cl[s], in_=CS[cc]))(s, cc),
                     reads=[("scr", "cs", cc)], writes=[("cres", s)], dma="cl%d" % s)
                S.op("dve", (lambda s: lambda e: e.tensor_tensor(out=tmp[s], in0=cl[s], in1=mean, op=ALU.subtract))(s),
                     reads=[("cres", s), "mean"], writes=[("tmp", s)])
                S.op("dve", (lambda s: lambda e: e.tensor_tensor(out=tmp[s], in0=tmp[s], in1=rstd, op=ALU.mult))(s),
                     reads=[("tmp", s), "rstd"], writes=[("tmp", s)])
                S.op("act", (lambda s, cc: lambda e: e.activation(out=mst[s], in_=tmp[s], func=AF.Silu, bias=lnb[:, cc:cc + 1],
                                                                scale=lng[:, cc:cc + 1]))(s, cc),
                     reads=[("tmp", s), "small"], writes=[("mst", s)])
                S.op("act", (lambda s, cc: lambda e: e.dma_start(out=MT[8 + cc], in_=mst[s]))(s, cc),
                     reads=[("mst", s)], writes=[("scr", "mt", 8 + cc)], dma="mst%d" % s)

            build_hT(list(range(17, 33)), hT_b, gBm, gBm_keys, xt_b, xn_b, sqj_b, stat_b, [0, 2, 4, 6],
                     hook=lambda j: ln_chunk(j - 1) if 1 <= j <= 8 else None)
            S.barrier()

        if "A" in phases:
            phase_proj(True)
        if stop_after != "A" and "B" in phases:
            phase_proj(False, prebuilt=("A" in phases))

        def phase_attn():
            A.reset(base_mark)
            wo = A.alloc([DC, D], BF16)
            kt = [A.alloc([S_LEN], BF16) for _ in range(2)]
            vv = [A.alloc([32, 128], BF16) for _ in range(2)]
            qt = [A.alloc([2, 2048], BF16) for _ in range(2)]
            ee = [A.alloc([9 * 512], BF16) for _ in range(2)]
            bb = [A.alloc([9 * 512], F32)] * 2
            pbuf = [A.alloc([2, 512], BF16) for _ in range(3)]
            r0 = A.alloc([512], F32)
            r1 = A.alloc([512], F32)
            t0 = A.alloc([512], F32)
            t1 = A.alloc([512], F32)
            osq = A.alloc([512], F32)
            cO = [A.alloc([512], F32) for _ in range(2)]
            cL = [A.alloc([512], F32) for _ in range(2)]
            ohi = A.alloc([512], BF16)
            olo = A.alloc([512], BF16)
            ostg = [A.alloc([2048], BF16) for _ in range(2)]
            OB = [4, 5]
            LB = [6, 7]

            def head_loads(h):
                s = h % 2
                S.op("sp", (lambda s, h: lambda e: e.dma_start(out=kt[s], in_=KT[h]))(s, h), writes=[("kt", s)], dma="kt%d" % s)
                S.op("sp", (lambda s, h: lambda e: e.dma_start(out=vv[s], in_=VS[h]))(s, h), writes=[("vv", s)], dma="vv%d" % s)
                S.op("sp", (lambda s, h: lambda e: e.dma_start(out=qt[s], in_=QT[h].rearrange("c p t -> p c t")))(s, h), writes=[("qt", s)], dma="qt%d" % s)
                S.op("sp", (lambda s, h: lambda e: e.dma_start(out=bb[s], in_=bias_t[h]))(s, h), writes=[("bb", 0)], dma="bb0")

            def head_ee(h):
                s = h % 2
                S.op("act", (lambda s, h: lambda e: e.activation(out=ee[s], in_=bb[s], func=AF.Exp, bias=negc[:, h:h + 1]))(s, h),
                     reads=[("bb", 0), "negc"], writes=[("ee", s)])

            def group_units(i):
                units = []
                for pr in range(i):
                    for t in range(8):
                        if pr == i - 1 and t == 7:
                            continue
                        units.append((8 * pr + t, None))
                if i >= 1:
                    units.append((8 * (i - 1) + 7, 0))
                for t in range(8):
                    units.append((8 * i + t, 1 + t))
                return units

            st_ = dict(slot=0, pu=0)

            def s_op(h, i, kp):
                s = h % 2
                sl = st_["slot"] % 2
                st_["slot"] += 1

                def fn(e):
                    e.matmul(bank(2 * sl), lhsT=kt[s][:, 128 * kp:128 * kp + 128], rhs=qt[s][:, 0, 512 * i:512 * i + 512], start=True, stop=True)
                    return e.matmul(bank(2 * sl + 1), lhsT=kt[s][:, 128 * kp:128 * kp + 128], rhs=qt[s][:, 1, 512 * i:512 * i + 512],
                                    start=True, stop=True)
                S.op("pe", fn, reads=[("kt", s), ("qt", s)], writes=[("ps", 2 * sl), ("ps", 2 * sl + 1)])
                return sl

            groups = [(h, i) for h in range(nheads) for i in range(4)]
            for c in range(DC):
                S.op("pool", (lambda c: lambda e: e.dma_start(out=wo[:, c, :], in_=w_out[128 * c:128 * c + 128, :]))(c),
                     writes=[("wo", c)], dma="wo%d" % (c % 4))
            head_loads(0)
            head_ee(0)
            pre_slot = None
            pending = [None]
            pending15 = [None]
            for gi, (h, i) in enumerate(groups):
                s = h % 2
                if i == 0 and h + 1 < nheads:
                    head_loads(h + 1)
                units = group_units(i)
                nu = len(units)
                cur = pre_slot if pre_slot is not None else s_op(h, i, units[0][0])
                for u in range(nu):
                    kp, nj = units[u]
                    nxt = s_op(h, i, units[u + 1][0]) if u + 1 < nu else None
                    pk = st_["pu"] % 3
                    st_["pu"] += 1
                    pb_ = pbuf[pk]
                    if nj is None:
                        src = psum_t[:, 1024 * cur:1024 * cur + 1024].rearrange("p (c q) -> p c q", q=512)
                        S.op("act", (lambda pb_, src: lambda e: e.activation(out=pb_, in_=src, func=AF.Exp))(pb_, src),
                             reads=[("ps", 2 * cur), ("ps", 2 * cur + 1)], writes=[("pb", pk, 0), ("pb", pk, 1)])
                    else:
                        for c in range(2):
                            S.op("act", (lambda pb_, c, cur: lambda e: e.activation(out=pb_[:, c, :], in_=bank(2 * cur + c), func=AF.Exp))(pb_, c, cur),
                                 reads=[("ps", 2 * cur + c)], writes=[("pb", pk, c)])
                            S.op("dve", (lambda pb_, nj, s, c: lambda e: e.tensor_tensor(out=pb_[:, c, :], in0=pb_[:, c, :],
                                                                                      in1=ee[s][:, 512 * nj:512 * nj + 512], op=ALU.mult))(pb_, nj, s, c),
                                 reads=[("pb", pk, c), ("ee", s)], writes=[("pb", pk, c)])
                    for c in range(2):
                        def pv(e, pb_=pb_, kp=kp, u=u, s=s, nu=nu, c=c):
                            e.matmul(bank(OB[c]), lhsT=vv[s][:, kp, :], rhs=pb_[:, c, :], start=(u == 0), stop=(u == nu - 1))
                            return e.matmul(bank(LB[c]), lhsT=ones_bf, rhs=pb_[:, c, :], start=(u == 0), stop=(u == nu - 1))
                        S.op("pe", pv, reads=[("vv", s), ("pb", pk, c), "ones_bf"], writes=[("ps", OB[c]), ("ps", LB[c])])
                    cur = nxt
                    if u == 2 and pending15[0] is not None:
                        pending15[0]()
                        pending15[0] = None
                    if u == min(nu - 2, 10) and pending[0] is not None:
                        pending[0]()
                        pending[0] = None
                if i == 0 and h + 1 < nheads:
                    head_ee(h + 1)
                if gi + 1 < len(groups):
                    nh_, ni_ = groups[gi + 1]
                    pre_slot = s_op(nh_, ni_, group_units(ni_)[0][0])
                else:
                    pre_slot = None
                S.op("act", lambda e: e.activation(out=cO[0], in_=bank(OB[0]), func=AF.Copy), reads=[("ps", OB[0])], writes=["cO0"])
                S.op("dve", lambda e: e.tensor_copy(out=cL[0], in_=bank(LB[0])), reads=[("ps", LB[0])], writes=["cL0"])
                S.op("act", lambda e: e.activation(out=cO[1], in_=bank(OB[1]), func=AF.Copy), reads=[("ps", OB[1])], writes=["cO1"])
                S.op("dve", lambda e: e.tensor_copy(out=cL[1], in_=bank(LB[1])), reads=[("ps", LB[1])], writes=["cL1"])
                def part15():
                  S.op("act", lambda e: e.activation(out=r0, in_=cL[0], func=AF.Ln), reads=["cL0"], writes=["r0"])
                  S.op("act", lambda e: e.activation(out=r0, in_=r0, func=AF.Exp, scale=-1.0), reads=["r0"], writes=["r0"])
                  S.op("act", lambda e: e.activation(out=r1, in_=cL[1], func=AF.Ln), reads=["cL1"], writes=["r1"])
                  S.op("act", lambda e: e.activation(out=r1, in_=r1, func=AF.Exp, scale=-1.0), reads=["r1"], writes=["r1"])
                  S.op("dve", lambda e: e.tensor_tensor(out=t0, in0=cO[0], in1=r0, op=ALU.mult), reads=["cO0", "r0"], writes=["t0"])
                  S.op("dve", lambda e: e.tensor_tensor(out=t1, in0=cO[1], in1=r1, op=ALU.mult), reads=["cO1", "r1"], writes=["t1"])
                  S.op("dve", lambda e: e.scalar_tensor_tensor(out=t0, in0=t1, scalar=neglam, in1=t0, op0=ALU.mult, op1=ALU.add),
                       reads=["t0", "t1", "neglam"], writes=["t0"])
                  S.op("dve", lambda e: e.tensor_tensor(out=osq, in0=t0, in1=t0, op=ALU.mult), reads=["t0"], writes=["osq"])
                  S.op("dve", lambda e: e.tensor_copy(out=ohi, in_=osq), reads=["osq"], writes=["ohi"])
                  S.op("dve", lambda e: e.tensor_tensor(out=olo, in0=osq, in1=ohi, op=ALU.subtract), reads=["osq", "ohi"], writes=["olo"])


                def part2(s=s, i=i, h=h):
                    NBk = 2 * (st_["slot"] % 2)

                    def nmm(e, NBk=NBk):
                        e.matmul(bank(NBk), lhsT=ones_bf, rhs=ohi, start=True, stop=False)
                        return e.matmul(bank(NBk), lhsT=ones_bf, rhs=olo, start=False, stop=True)
                    S.op("pe", nmm, reads=["ohi", "olo", "ones_bf"], writes=[("ps", NBk)])
                    S.op("dve", (lambda NBk: lambda e: e.tensor_scalar(out=r0, in0=bank(NBk), scalar1=1.0 / 128, scalar2=EPS, op0=ALU.mult, op1=ALU.add))(NBk),
                         reads=[("ps", NBk)], writes=["r0"])
                    S.op("act", lambda e: e.activation(out=r0, in_=r0, func=AF.Ln), reads=["r0"], writes=["r0"])
                    S.op("act", lambda e: e.activation(out=r0, in_=r0, func=AF.Exp, scale=-0.5), reads=["r0"], writes=["r0"])
                    S.op("dve", (lambda s, i: lambda e: e.scalar_tensor_tensor(out=ostg[s][:, 512 * i:512 * i + 512], in0=t0, scalar=subg08, in1=r0,
                                                                              op0=ALU.mult, op1=ALU.mult))(s, i),
                         reads=["t0", "r0", "subg08"], writes=[("ostg", s, i)])
                    if i == 3:
                        S.op("sp", (lambda s, h: lambda e: e.dma_start(out=MT[h], in_=ostg[s]))(s, h),
                             reads=[("ostg", s, i_) for i_ in range(4)], writes=[("scr", "mt", h)], dma="ostg%d" % s)
                if gi + 1 < len(groups):
                    pending15[0] = part15
                    pending[0] = part2
                else:
                    part15()
                    part2()
            S.barrier()

        if stop_after not in ("A", "B") and "C" in phases:
            phase_attn()

        def phase_outproj():
            A.reset(base_mark)
            wo = A.alloc([DC, D], BF16)
            gpost = A.alloc([D], F32)
            mt = [A.alloc([DC, 512], BF16) for _ in range(2)]
            xt = [A.alloc([D], F32) for _ in range(2)]
            x1 = [A.alloc([D], F32) for _ in range(2)]
            xn = [A.alloc([D], BF16) for _ in range(2)]
            sqj = A.alloc([D], BF16)
            h2s = [A.alloc([DC, 512], BF16) for _ in range(2)]
            stat = A.alloc([16], F32)
            S.op("sp", lambda e: e.dma_start(out=gpost, in_=g_post[0]), writes=["gpost"], dma="gpost")
            if "C" not in phases:
                for c in range(DC):
                    S.op("pool", (lambda c: lambda e: e.dma_start(out=wo[:, c, :], in_=w_out[128 * c:128 * c + 128, :]))(c),
                         writes=[("wo", c)], dma="wo%d" % (c % 4))
            wok = [("wo", c) for c in range(DC)]
            MTv = MT.rearrange("c p t -> p c t")
            def d_front(tt):
                blk, tl = tt // 4, tt % 4
                bs = blk % 2
                s = tt % 2
                pb0 = 4 * s
                if tl == 0:
                    S.op("sp", (lambda bs, blk: lambda e: e.dma_start(out=mt[bs], in_=MTv[:, :, 512 * blk:512 * blk + 512]))(bs, blk),
                         writes=[("mt", bs)], dma="mt%d" % bs)
                S.op("sp", (lambda s, tt: lambda e: e.dma_start(out=xt[s], in_=x_in[128 * tt:128 * tt + 128, :]))(s, tt),
                     writes=[("xt", s)], dma="xt%d" % s)

                def fn(e, bs=bs, tl=tl, pb0=pb0):
                    for j in range(4):
                        for c in range(DC):
                            ins = e.matmul(bank(pb0 + j), lhsT=mt[bs][:, c, 128 * tl:128 * tl + 128], rhs=wo[:, c, 512 * j:512 * j + 512],
                                           start=(c == 0), stop=(c == DC - 1))
                    return ins
                S.op("pe", fn, reads=[("mt", bs)] + wok, writes=[("ps", pb0 + j) for j in range(4)])

            def d_back(tt):
                blk, tl = tt // 4, tt % 4
                bs = blk % 2
                s = tt % 2
                pb0 = 4 * s
                pk = [("ps", pb0 + j) for j in range(4)]
                pv_ = psum_t[:, 512 * pb0:512 * pb0 + 2048]
                S.op("act", (lambda s, pv_: lambda e: e.activation(out=sqj, in_=pv_, func=AF.Square,
                                                                  accum_out=stat[:, 8 * s:8 * s + 1]))(s, pv_),
                     reads=pk, writes=["sqj", ("ss", s)])
                rstd_ops(("o", s), stat[:, 8 * s:8 * s + 1], D, stat[:, 8 * s + 2:8 * s + 3], stat[:, 8 * s + 1:8 * s + 2], ("ss", s), ("rs", s))
                S.op("dve", (lambda s, pv_: lambda e: e.scalar_tensor_tensor(out=x1[s], in0=pv_, scalar=stat[:, 8 * s + 2:8 * s + 3],
                                                                            in1=gpost, op0=ALU.mult, op1=ALU.mult))(s, pv_),
                     reads=pk + [("rs", s), "gpost"], writes=[("x1", s)])
                S.op("dve", (lambda s: lambda e: e.tensor_tensor(out=x1[s], in0=x1[s], in1=xt[s], op=ALU.add))(s),
                     reads=[("x1", s), ("xt", s)], writes=[("x1", s)])
                S.op("sp", (lambda s, tt: lambda e: e.dma_start(out=out[128 * tt:128 * tt + 128, :], in_=x1[s]))(s, tt),
                     reads=[("x1", s)], writes=[("out", tt)], dma="x1o%d" % s)
                S.op("act", (lambda s: lambda e: e.activation(out=sqj, in_=x1[s], func=AF.Square,
                                                             accum_out=stat[:, 8 * s + 4:8 * s + 5]))(s),
                     reads=[("x1", s)], writes=["sqj", ("ss2", s)])
                rstd_ops(("o2", s), stat[:, 8 * s + 4:8 * s + 5], D, stat[:, 8 * s + 6:8 * s + 7], stat[:, 8 * s + 5:8 * s + 6], ("ss2", s), ("rs2", s))
                S.op("dve", (lambda s: lambda e: e.tensor_scalar(out=xn[s], in0=x1[s], scalar1=stat[:, 8 * s + 6:8 * s + 7],
                                                                 scalar2=None, op0=ALU.mult))(s),
                     reads=[("x1", s), ("rs2", s)], writes=[("xn", s)])
                pT = psum_t[:, 512 * pb0:512 * pb0 + 1024].bitcast(BF16).rearrange("p (a b) -> p a b", b=128)

                def tr(e, s=s, pT=pT):
                    for dc in range(DC):
                        ins = e.transpose(pT[:, dc, :], xn[s][:, 128 * dc:128 * dc + 128], ident)
                    return ins
                S.op("pe", tr, reads=[("xn", s), "ident"], writes=[("ps", pb0), ("ps", pb0 + 1)])
                S.op("dve", (lambda bs, tl, pT: lambda e: e.tensor_tensor(out=h2s[bs][:, :, 128 * tl:128 * tl + 128], in0=pT, in1=gBf,
                                                                          op=ALU.mult))(bs, tl, pT),
                     reads=[("ps", pb0), ("ps", pb0 + 1)] + gBf_keys, writes=[("h2s", bs, tl)])
                if tl == 3:
                    S.op("sp", (lambda bs, blk: lambda e: e.dma_start(out=H2[:, :, 512 * blk:512 * blk + 512], in_=h2s[bs]))(bs, blk),
                         reads=[("h2s", bs, t_) for t_ in range(4)], writes=[("scr", "h2", blk)], dma="h2s%d" % bs)

            for tt in range(17):
                if tt < 16:
                    d_front(tt)
                if tt > 0:
                    d_back(tt - 1)
            S.barrier()

        if stop_after not in ("A", "B", "C") and "D" in phases:
            phase_outproj()

        def phase_ffn():
            A.reset(base_mark)
            gpost = A.alloc([D], F32)
            h2 = [A.alloc([DC, 512], BF16)] * 2
            actT = A.alloc([FC, 512], BF16)
            wg = [A.alloc([DC, 512], BF16) for _ in range(2)]
            wu = [A.alloc([DC, 512], BF16) for _ in range(2)]
            wd = [A.alloc([D], BF16) for _ in range(3)]
            sgt = [A.alloc([512], F32) for _ in range(2)]
            x1 = [A.alloc([D], F32) for _ in range(2)]
            y = [A.alloc([D], F32) for _ in range(2)]
            sqj = A.alloc([D], BF16)
            stat = A.alloc([8], F32)
            S.op("sp", lambda e: e.dma_start(out=gpost, in_=g_post[1]), writes=["gpost"], dma="gpost2")
            wdi = 0
            for blk in range(4):
                bs = 0
                S.op("sp", (lambda bs, blk: lambda e: e.dma_start(out=h2[bs], in_=H2[:, :, 512 * blk:512 * blk + 512]))(bs, blk),
                     writes=[("h2", bs)], dma="h2%d" % bs)
                for fc in range(FC):
                    ws = (fc // 4) % 2
                    if fc % 4 == 0:
                        S.op("pool", (lambda ws, fc: lambda e: e.dma_start(out=wg[ws], in_=w_gate_v[:, :, 128 * fc:128 * fc + 512]))(ws, fc),
                             writes=[("wg", ws)], dma="wg%d" % ws)
                        S.op("pool", (lambda ws, fc: lambda e: e.dma_start(out=wu[ws], in_=w_up_v[:, :, 128 * fc:128 * fc + 512]))(ws, fc),
                             writes=[("wu", ws)], dma="wu%d" % ws)
                    pg, pu = 2 * (fc % 2), 2 * (fc % 2) + 1
                    fo = 128 * (fc % 4)

                    def fn(e, wt, pb, bs=bs, fo=fo):
                        for dc in range(DC):
                            ins = e.matmul(bank(pb), lhsT=wt[:, dc, fo:fo + 128], rhs=h2[bs][:, dc, :], start=(dc == 0), stop=(dc == DC - 1))
                        return ins
                    S.op("pe", (lambda fn, ws, pg: lambda e: fn(e, wg[ws], pg))(fn, ws, pg), reads=[("wg", ws), ("h2", bs)], writes=[("ps", pg)])
                    S.op("pe", (lambda fn, ws, pu: lambda e: fn(e, wu[ws], pu))(fn, ws, pu), reads=[("wu", ws), ("h2", bs)], writes=[("ps", pu)])
                    S.op("act", (lambda fc, pg: lambda e: e.activation(out=sgt[fc % 2], in_=bank(pg), func=AF.Silu))(fc, pg),
                         reads=[("ps", pg)], writes=[("sgt", fc % 2)])
                    S.op("dve", (lambda fc, pu: lambda e: e.tensor_tensor(out=actT[:, fc, :], in0=sgt[fc % 2], in1=bank(pu), op=ALU.mult))(fc, pu),
                         reads=[("sgt", fc % 2), ("ps", pu)], writes=[("actT", fc)])
                for half in range(2):
                    for tl in range(2):
                        tt = 4 * blk + 2 * half + tl
                        S.op("act", (lambda tl, tt: lambda e: e.dma_start(out=x1[tl], in_=out[128 * tt:128 * tt + 128, :]))(tl, tt),
                             reads=[("out", tt)], writes=[("x1", tl)], dma="x1i%d" % tl)
                    for fc in range(FC):
                        ws = wdi % 3
                        wdi += 1
                        S.op("sp", (lambda ws, fc: lambda e: e.dma_start(out=wd[ws], in_=WDB[fc]))(ws, fc),
                             writes=[("wd", ws)], dma="wd%d" % ws, extra_deps=wdb_ops)

                        def fn(e, ws=ws, fc=fc, half=half):
                            for tl in range(2):
                                t = 2 * half + tl
                                for j in range(4):
                                    ins = e.matmul(bank(4 * tl + j), lhsT=actT[:, fc, 128 * t:128 * t + 128], rhs=wd[ws][:, 512 * j:512 * j + 512],
                                                   start=(fc == 0), stop=(fc == FC - 1))
                            return ins
                        S.op("pe", fn, reads=[("wd", ws), ("actT", fc)], writes=[("ps", b_) for b_ in range(8)])
                    for tl in range(2):
                        tt = 4 * blk + 2 * half + tl
                        s = tl
                        pk = [("ps", 4 * tl + j) for j in range(4)]
                        pv_ = psum_t[:, 2048 * tl:2048 * tl + 2048]
                        S.op("act", (lambda s, pv_: lambda e: e.activation(out=sqj, in_=pv_, func=AF.Square, accum_out=stat[:, 4 * s:4 * s + 1]))(s, pv_),
                             reads=pk, writes=["sqj", ("ss", s)])
                        rstd_ops(("f", s), stat[:, 4 * s:4 * s + 1], D, stat[:, 4 * s + 2:4 * s + 3], stat[:, 4 * s + 1:4 * s + 2], ("ss", s), ("rs", s))
                        S.op("dve", (lambda s, pv_: lambda e: e.scalar_tensor_tensor(out=y[s], in0=pv_, scalar=stat[:, 4 * s + 2:4 * s + 3],
                                                                                    in1=gpost, op0=ALU.mult, op1=ALU.mult))(s, pv_),
                             reads=pk + [("rs", s), "gpost"], writes=[("y", s)])
                        S.op("dve", (lambda s: lambda e: e.tensor_tensor(out=y[s], in0=y[s], in1=x1[s], op=ALU.add))(s),
                             reads=[("y", s), ("x1", s)], writes=[("y", s)])
                        S.op("act", (lambda s, tt: lambda e: e.dma_start(out=out[128 * tt:128 * tt + 128, :], in_=y[s]))(s, tt),
                             reads=[("y", s)], writes=[("out", tt)], dma="yo%d" % s)
            S.barrier()

        if stop_after not in ("A", "B", "C", "D") and "E" in phases:
            phase_ffn()
        S.barrier(final=True)
        S.emit(nc, st)
    return nc


def _t5_bucket_np(n):
    n = np.maximum(n, 0)
    max_exact = 16
    nf = np.maximum(n, 1).astype(np.float32)
    large = max_exact + (np.log(nf / np.float32(max_exact)) / np.float32(math.log(128 / max_exact))
                         * np.float32(32 - max_exact)).astype(np.int32)
    large = np.minimum(large, 31)
    return np.where(n < max_exact, n, large)


def _bias_tiles(rel_bias, p):
    ki = np.arange(128)[:, None, None]
    a = (np.arange(512) // 128)[None, None, :]
    qi = (np.arange(512) % 128)[None, None, :]
    j = np.arange(9)[None, :, None]
    t = j - 1
    dt = np.where(j == 0, 8 * p + a + 1, np.where(t < 4, a - t, 8 * p + a - t))
    n = 128 * dt + qi - ki
    bucket = _t5_bucket_np(n)
    vals = rel_bias[bucket]
    vals = np.where((n >= 0)[..., None], vals, np.float32(MASK))
    return np.ascontiguousarray(np.transpose(vals, (3, 0, 1, 2)).reshape(8, 128, 9 * 512).astype(np.float32))


def _core_inputs(c, inp):
    b, p = c // 2, c % 2
    x = inp["x"][b]
    own = [2 * i + p for i in range(4)]
    oth = [2 * i + 1 - p for i in range(4)]
    halo = np.zeros((128, D), np.float32)
    for i, gb in enumerate(own):
        if gb > 0:
            halo[32 * i:32 * i + 32] = x[512 * gb - 32:512 * gb]
    xc = np.concatenate([x[512 * g:512 * g + 512] for g in own] + [halo] + [x[512 * g:512 * g + 512] for g in oth], axis=0)
    return xc


def _small_params(inp):
    f = lambda a: np.asarray(a, np.float32)
    cols = [
        f(inp["norm_pre_mix"][0]).reshape(16, 128).T,
        f(inp["norm_pre_ffn"][0]).reshape(16, 128).T,
        f(inp["conv_dw_w"][0]).T.reshape(8, 128, NTAP).transpose(1, 0, 2).reshape(128, 8 * NTAP),
        f(inp["conv_dw_b"][0]).reshape(8, 128).T,
        f(inp["conv_ln_g"][0]).reshape(8, 128).T,
        f(inp["conv_ln_b"][0]).reshape(8, 128).T,
        f(inp["subln_g"][0]).reshape(128, 1),
        np.broadcast_to(f(inp["rel_bias"])[31][None, :], (128, 8)),
        np.broadcast_to(np.concatenate([f(inp["lambda_q1"][0]), f(inp["lambda_k1"][0]),
                                        f(inp["lambda_q2"][0]), f(inp["lambda_k2"][0])])[None, :], (128, 256)),
    ]
    return np.ascontiguousarray(np.concatenate(cols, axis=1).astype(np.float32))


_NC_CACHE = {}


def kernel(**inputs):
    inp = {k: np.asarray(v) for k, v in inputs.items()}
    if "nc" not in _NC_CACHE:
        _NC_CACHE["nc"] = build_nc()
    nc = _NC_CACHE["nc"]
    small = _small_params(inp)
    gpost = np.ascontiguousarray(np.stack([np.broadcast_to(inp["norm_post_mix"][0][None, :], (128, D)),
                                           np.broadcast_to(inp["norm_post_ffn"][0][None, :], (128, D))]).astype(np.float32))
    rel = np.asarray(inp["rel_bias"], np.float32)
    bt = [_bias_tiles(rel, 0), _bias_tiles(rel, 1)]
    shared = dict(w_in=np.ascontiguousarray(inp["w_in"][0]), w_out=np.ascontiguousarray(inp["w_out"][0]),
                  w_gate=np.ascontiguousarray(inp["w_gate"][0]), w_up=np.ascontiguousarray(inp["w_up"][0]),
                  w_down=np.ascontiguousarray(inp["w_down"][0]), p_small=small, g_post=gpost)
    in_maps = []
    for c in range(N_CORES):
        m = dict(shared)
        m["x"] = _core_inputs(c, inp)
        m["bias_t"] = bt[c % 2]
        in_maps.append(m)
    res = run_bass_kernel_spmd(nc, in_maps, core_ids=list(range(N_CORES)))
    outp = np.empty((4, S_LEN, D), np.float32)
    for c in range(N_CORES):
        b, p = c // 2, c % 2
        o = res.results[c]["out"]
        for i in range(4):
            gb = 2 * i + p
            outp[b, 512 * gb:512 * gb + 512] = o[512 * i:512 * i + 512]
    return outp
```

```python
import math
import os
from contextlib import ExitStack

import numpy as np
import concourse.bass as bass
import concourse.mybir as mybir
from concourse.bass_utils import run_bass_kernel_spmd

F32 = mybir.dt.float32
BF16 = mybir.dt.bfloat16
AF = mybir.ActivationFunctionType
ALU = mybir.AluOpType

D = 2048
DC = 16
S_LEN = 4096
NH = 8
FF = 5632
FC = 44
IN_W = 5120
EPS = 1e-6
LAM_INIT = 0.2
NTAP = 31
MASK = -1e30
N_CORES = 8


class Sched:
    ENGS = ("pe", "act", "dve", "pool", "sp")

    def __init__(self):
        self.ops = []
        self.last_writer = {}
        self.readers = {}
        self.last_eng = {}
        self.open_dmas = []
        self.bg_dmas = []

    def op(self, eng, fn, reads=(), writes=(), dma=None, extra_deps=(), bg=False):
        deps = set(extra_deps)
        for k in reads:
            w = self.last_writer.get(k)
            if w is not None:
                deps.add(w)
        for k in writes:
            w = self.last_writer.get(k)
            if w is not None:
                deps.add(w)
            for r in self.readers.get(k, ()):
                deps.add(r)
        idx = len(self.ops)
        self.ops.append(dict(eng=eng, fn=fn, deps=deps, dma=dma, cons=False, sig=None))
        for k in reads:
            self.readers.setdefault(k, []).append(idx)
        for k in writes:
            self.last_writer[k] = idx
            self.readers[k] = []
        if fn is not None:
            if dma is None:
                self.last_eng[eng] = idx
            elif bg:
                self.bg_dmas.append(idx)
            else:
                self.open_dmas.append(idx)
        return idx

    def barrier(self, final=False):
        deps = set(self.last_eng.values()) | set(self.open_dmas)
        if final:
            deps |= set(self.bg_dmas)
            self.bg_dmas = []
        self.open_dmas = []
        for e in self.ENGS:
            self.op(e, None, extra_deps=deps)
        self.last_writer = {}
        self.readers = {}

    @staticmethod
    def _needs_sync(P, C):
        if P["dma"] is None and C["dma"] is None and P["eng"] == C["eng"] == "pe":
            return False
        return True

    def emit(self, nc, stack):
        ops = self.ops
        for o in ops:
            for d in o["deps"]:
                if self._needs_sync(ops[d], o):
                    ops[d]["cons"] = True
        counts = {}
        for o in ops:
            if not o["cons"]:
                continue
            name = ("e_" + o["eng"]) if o["dma"] is None else ("d_" + o["dma"])
            inc = 1 if o["dma"] is None else 16
            counts[name] = counts.get(name, 0) + inc
            o["sig"] = (name, counts[name], inc)
        per_eng = {e: [] for e in self.ENGS}
        waited = {e: {} for e in self.ENGS}
        for o in ops:
            need = {}
            for d in o["deps"]:
                P = ops[d]
                if not self._needs_sync(P, o):
                    continue
                name, val, _ = P["sig"]
                if need.get(name, 0) < val:
                    need[name] = val
            waits = []
            for name, val in need.items():
                if waited[o["eng"]].get(name, 0) < val:
                    waited[o["eng"]][name] = val
                    waits.append((name, val))
            per_eng[o["eng"]].append((waits, o["fn"], o["sig"]))
        sems = {}
        for name in counts:
            sems[name] = stack.enter_context(nc.semaphore("s_" + name))
        block = stack.enter_context(nc.Block())

        def make(engname):
            def body(eng):
                for waits, fn, sig in per_eng[engname]:
                    for name, val in waits:
                        eng.wait_ge(sems[name], val)
                    if fn is None:
                        continue
                    ins = fn(eng)
                    if sig is not None:
                        ins.then_inc(sems[sig[0]], sig[2])
            return body

        block.tensor(make("pe"))
        block.scalar(make("act"))
        block.vector(make("dve"))
        block.gpsimd(make("pool"))
        block.sync(make("sp"))
        self.counts = counts


class Arena:
    def __init__(self, tensor, nbytes):
        self.t = tensor
        self.n = nbytes
        self.off = 0

    def alloc(self, free_shape, dt):
        n = 1
        for s in free_shape:
            n *= s
        esz = 4 if dt == F32 else 2
        nb = n * esz
        a = self.off
        self.off = (a + nb + 63) // 64 * 64
        assert self.off <= self.n, ("SBUF arena overflow", self.off, self.n)
        ap = self.t[:, a // 4:(a + nb) // 4]
        if dt != F32:
            ap = ap.bitcast(dt)
        if len(free_shape) == 2:
            ap = ap.rearrange("p (a b) -> p a b", b=free_shape[1])
        elif len(free_shape) == 3:
            ap = ap.rearrange("p (a b c) -> p a b c", b=free_shape[1], c=free_shape[2])
        return ap

    def at(self, off, free_shape, dt):
        save = self.off
        self.off = off
        ap = self.alloc(free_shape, dt)
        self.off = save
        return ap

    def mark(self):
        return self.off

    def reset(self, m):
        self.off = m


def build_nc(debug=False, stop_after=None, phases="ABCDE", nheads=NH):
    nc = bass.Bass("TRN2", target_bir_lowering=False)
    skind = "ExternalOutput" if debug else "Internal"

    def din(name, shape, dt=F32):
        return nc.dram_tensor(name, shape, dt, kind="ExternalInput").ap()

    x_in = din("x", [33 * 128, D])
    w_in = din("w_in", [D, IN_W])
    w_out = din("w_out", [D, D])
    w_gate = din("w_gate", [D, FF])
    w_up = din("w_up", [D, FF])
    w_down = din("w_down", [FF, D])
    p_small = din("p_small", [128, 16 + 16 + 248 + 8 + 8 + 8 + 1 + 8 + 256])
    g_post = din("g_post", [2, 128, D])
    bias_t = din("bias_t", [NH, 128, 9 * 512])
    out = nc.dram_tensor("out", [2048, D], F32, kind="ExternalOutput").ap()

    KT = nc.dram_tensor("KT", [NH, 128, S_LEN], BF16, kind=skind).ap()
    VS = nc.dram_tensor("VS", [NH, 128, 32, 128], BF16, kind=skind).ap()
    QT = nc.dram_tensor("QT", [NH, 2, 128, 2048], BF16, kind=skind).ap()
    CS = nc.dram_tensor("CS", [8, 128, 2048], F32, kind=skind).ap()
    MT = nc.dram_tensor("MT", [16, 128, 2048], BF16, kind=skind).ap()
    H2 = nc.dram_tensor("H2", [128, DC, 2048], BF16, kind=skind).ap()
    WDB = nc.dram_tensor("WDB", [FC, 128, D], BF16, kind="Internal").ap()

    w_in_v = w_in.rearrange("(dc p) c -> p dc c", p=128)
    w_gate_v = w_gate.rearrange("(dc p) c -> p dc c", p=128)
    w_up_v = w_up.rearrange("(dc p) c -> p dc c", p=128)

    st = ExitStack()
    with st:
        ARENA_BYTES = 206 * 1024
        arena_t = st.enter_context(nc.sbuf_tensor("arena", [128, ARENA_BYTES // 4], F32))
        A = Arena(arena_t, ARENA_BYTES)
        psum_t = st.enter_context(nc.psum_tensor("psum", [128, 8 * 512], F32))

        def bank(b, n=512, off=0):
            return psum_t[:, 512 * b + off:512 * b + off + n]

        S = Sched()

        identf = A.alloc([128], F32)
        ident = A.alloc([128], BF16)
        ones_bf = A.alloc([128], BF16)
        ones_f = A.alloc([128], F32)
        small = A.alloc([16 + 16 + 248 + 8 + 8 + 8 + 1 + 8 + 256], F32)
        gBm = A.alloc([DC, 128], F32)
        gBf = A.alloc([DC, 128], F32)
        misc = A.alloc([16], F32)
        o0 = 0
        gpm = small[:, 0:16]
        gpf = small[:, 16:32]
        cw = small[:, 32:280].rearrange("p (c j) -> p c j", j=NTAP)
        cb = small[:, 280:288]
        lng = small[:, 288:296]
        lnb = small[:, 296:304]
        subg = small[:, 304:305]
        relc = small[:, 305:313]
        lamp = small[:, 313:569].rearrange("p (k d) -> p k d", d=64)
        neglam = misc[:, 0:1]
        negc = misc[:, 1:9]
        subg08 = misc[:, 9:10]
        lam_t = misc[:, 10:14]

        S.op("sp", lambda e: e.dma_start(out=small, in_=p_small), writes=["small"], dma="small")
        S.op("pool", lambda e: e.memset(identf, 0.0), writes=["identf"])
        S.op("pool", lambda e: e.affine_select(out=identf, in_=identf, pattern=[[-1, 128]],
                                               compare_op=ALU.not_equal, fill=1.0, base=0,
                                               channel_multiplier=1),
             reads=["identf"], writes=["identf"])
        S.op("dve", lambda e: e.tensor_copy(out=ident, in_=identf), reads=["identf"], writes=["ident"])
        S.op("dve", lambda e: e.memset(ones_bf, 1.0), writes=["ones_bf"])
        S.op("dve", lambda e: e.memset(ones_f, 1.0), writes=["ones_f"])
        for dc in range(DC):
            S.op("dve", (lambda dc: lambda e: e.tensor_scalar(out=gBm[:, dc, :], in0=ones_f, scalar1=gpm[:, dc:dc + 1],
                                                               scalar2=None, op0=ALU.mult))(dc),
                 reads=["small", "ones_f"], writes=[("gBm", dc)])
            S.op("dve", (lambda dc: lambda e: e.tensor_scalar(out=gBf[:, dc, :], in0=ones_f, scalar1=gpf[:, dc:dc + 1],
                                                               scalar2=None, op0=ALU.mult))(dc),
                 reads=["small", "ones_f"], writes=[("gBf", dc)])
        gBm_keys = [("gBm", dc) for dc in range(DC)]
        gBf_keys = [("gBf", dc) for dc in range(DC)]
        lj = A.alloc([2, 64], F32)
        S.op("dve", lambda e: e.tensor_tensor(out=lj[:, 0, :], in0=lamp[:, 0, :], in1=lamp[:, 1, :], op=ALU.mult),
             reads=["small"], writes=["lj0"])
        S.op("dve", lambda e: e.tensor_tensor(out=lj[:, 1, :], in0=lamp[:, 2, :], in1=lamp[:, 3, :], op=ALU.mult),
             reads=["small"], writes=["lj1"])
        S.op("dve", lambda e: e.reduce_sum(out=lam_t[:, 0:1], in_=lj[:, 0, :], axis=mybir.AxisListType.X),
             reads=["lj0"], writes=["lam0"])
        S.op("dve", lambda e: e.reduce_sum(out=lam_t[:, 1:2], in_=lj[:, 1, :], axis=mybir.AxisListType.X),
             reads=["lj1"], writes=["lam1"])
        S.op("act", lambda e: e.activation(out=lam_t[:, 2:4], in_=lam_t[:, 0:2], func=AF.Exp),
             reads=["lam0", "lam1"], writes=["lam2"])
        S.op("dve", lambda e: e.tensor_tensor(out=lam_t[:, 0:1], in0=lam_t[:, 3:4], in1=lam_t[:, 2:3], op=ALU.subtract),
             reads=["lam2"], writes=["lam3"])
        S.op("dve", lambda e: e.tensor_scalar(out=neglam, in0=lam_t[:, 0:1], scalar1=-LAM_INIT, scalar2=None, op0=ALU.add),
             reads=["lam3"], writes=["neglam"])
        S.op("dve", lambda e: e.tensor_scalar(out=negc, in0=relc, scalar1=-1.0, scalar2=None, op0=ALU.mult),
             reads=["small"], writes=["negc"])
        S.op("dve", lambda e: e.tensor_scalar(out=subg08, in0=subg, scalar1=1.0 - LAM_INIT, scalar2=None, op0=ALU.mult),
             reads=["small"], writes=["subg08"])
        base_mark = A.mark()

        def rstd_ops(tag, ss_ap, n, rs_ap, tmp_ap, ss_key, rs_key):
            S.op("dve", lambda e: e.tensor_scalar(out=tmp_ap, in0=ss_ap, scalar1=1.0 / n, scalar2=EPS,
                                                  op0=ALU.mult, op1=ALU.add),
                 reads=[ss_key], writes=[(tag, "ms")])
            S.op("act", lambda e: e.activation(out=tmp_ap, in_=tmp_ap, func=AF.Ln),
                 reads=[(tag, "ms")], writes=[(tag, "sq")])
            S.op("act", lambda e: e.activation(out=rs_ap, in_=tmp_ap, func=AF.Exp, scale=-0.5), reads=[(tag, "sq")], writes=[rs_key])

        def build_hT(tiles, hT, gB, gB_keys, xt, xn, sqj, stat, pt_banks, hook=None):
            ns = len(xt)
            nt = len(tiles)

            def st1(j):
                tt = tiles[j]
                s = j % ns
                S.op("sp", (lambda s, tt: lambda e: e.dma_start(out=xt[s], in_=x_in[128 * tt:128 * tt + 128, :]))(s, tt),
                     writes=[("xt", s)], dma="xt%d" % s)
                S.op("act", (lambda s: lambda e: e.activation(out=sqj, in_=xt[s], func=AF.Square,
                                                             accum_out=stat[:, 4 * s:4 * s + 1]))(s),
                     reads=[("xt", s)], writes=["sqj", ("ss", s)])
                rstd_ops(("h", s), stat[:, 4 * s:4 * s + 1], D, stat[:, 4 * s + 2:4 * s + 3], stat[:, 4 * s + 1:4 * s + 2],
                         ("ss", s), ("rs", s))

            def st2(j):
                s = j % ns
                S.op("dve", (lambda s: lambda e: e.tensor_scalar(out=xn[s], in0=xt[s], scalar1=stat[:, 4 * s + 2:4 * s + 3],
                                                                scalar2=None, op0=ALU.mult))(s),
                     reads=[("xt", s), ("rs", s)], writes=[("xn", s)])
                pb = pt_banks[s]
                pT = psum_t[:, 512 * pb:512 * pb + 1024].bitcast(BF16).rearrange("p (a b) -> p a b", b=128)

                def tr(e, s=s, pT=pT):
                    for dc in range(DC):
                        ins = e.transpose(pT[:, dc, :], xn[s][:, 128 * dc:128 * dc + 128], ident)
                    return ins
                S.op("pe", tr, reads=[("xn", s), "ident"], writes=[("ps", pb), ("ps", pb + 1)])

            def st3(j):
                s = j % ns
                pb = pt_banks[s]
                pT = psum_t[:, 512 * pb:512 * pb + 1024].bitcast(BF16).rearrange("p (a b) -> p a b", b=128)
                S.op("dve", (lambda j, pT: lambda e: e.tensor_tensor(out=hT[:, :, 128 * j:128 * j + 128], in0=pT, in1=gB,
                                                                     op=ALU.mult))(j, pT),
                     reads=[("ps", pb), ("ps", pb + 1)] + gB_keys, writes=[("hT", j)])

            for k in range(nt + 3):
                if k < nt:
                    if hook is not None:
                        hook(k)
                    st1(k)
                if 0 <= k - 1 < nt:
                    st2(k - 1)
                if 0 <= k - 2 < nt:
                    st3(k - 2)

        def load_wchunk(dst, src_view, c0, ncol, key, slot):
            S.op("pool", lambda e: e.dma_start(out=dst, in_=src_view[:, :, c0:c0 + ncol]),
                 writes=[(key, slot)], dma="%s%d" % (key, slot))

        def proj_fm(wt, wkey, hT, hkeys, col0, ps_b):
            def fn(e):
                for dc in range(DC):
                    ins = e.matmul(bank(ps_b), lhsT=wt[:, dc, :], rhs=hT[:, dc, col0:col0 + 512],
                                   start=(dc == 0), stop=(dc == DC - 1))
                return ins
            S.op("pe", fn, reads=[wkey] + hkeys, writes=[("ps", ps_b)])

        wdb_ops = []

        def precast_wd(n):
            for _ in range(n):
                fc = len(wdb_ops)
                if fc >= FC:
                    return
                wdb_ops.append(S.op("pool", (lambda fc: lambda e: e.dma_start(out=WDB[fc], in_=w_down[128 * fc:128 * fc + 128, :]))(fc),
                                    dma="wdb%d" % (fc % 4), bg=True))

        def build_bufs():
            xt = [A.alloc([D], F32) for _ in range(4)]
            xn = [A.alloc([D], BF16) for _ in range(4)]
            sqj = A.alloc([D], BF16)
            stat = A.alloc([16], F32)
            return xt, xn, sqj, stat

        def phase_proj(own, prebuilt=False):
            A.reset(base_mark)
            ntile = 17 if own else 16
            tiles = list(range(0, 17)) if own else list(range(17, 33))
            hT = A.alloc([DC, ntile * 128], BF16)
            m1 = A.mark()
            if not prebuilt:
                xt, xn, sqj, stat = build_bufs()
                build_hT(tiles, hT, gBm, gBm_keys, xt, xn, sqj, stat, [0, 2, 4, 6])
                S.barrier()
            A.reset(m1)
            wc = [A.alloc([DC, 128], BF16) for _ in range(4)]
            wv = [A.alloc([DC, 256], BF16) for _ in range(2)]
            stg = [A.alloc([2048], BF16) for _ in range(2)]
            qz = [A.alloc([2, 2048], BF16) for _ in range(2)] if own else None
            if own:
                for q_ in range(2):
                    S.op("pool", (lambda q_: lambda e: e.memset(qz[q_], 0.0))(q_), writes=[("qz", q_, b_) for b_ in range(4)])
            vstg = [A.alloc([2, 16, 128], BF16) for _ in range(2)]
            hk = lambda blk: [("hT", 4 * blk + t) for t in range(4)]
            pair_half = 0 if own else 1
            chunks = []
            if own:
                chunks += [("q", h) for h in range(NH)]
            chunks += [("k", h) for h in range(NH)]
            pb_rot = 0
            for ci, (kind, h) in enumerate(chunks):
                ws = ci % 4
                col = (0 if kind == "q" else 1024) + 128 * h
                load_wchunk(wc[ws], w_in_v, col, 128, "wc", ws)
                ss_ = ci % 2
                for blk in range(4):
                    pb = 4 + (pb_rot % 4)
                    pb_rot += 1
                    proj_fm(wc[ws], ("wc", ws), hT, hk(blk), 512 * blk, pb)
                    if kind == "q":
                        S.op("act", (lambda ss_, blk, pb: lambda e: e.activation(
                            out=qz[ss_][0:64, 0, 512 * blk:512 * blk + 512], in_=bank(pb)[0:64, :], func=AF.Copy, scale=0.125))(ss_, blk, pb),
                            reads=[("ps", pb), ("qz", ss_, blk)], writes=[("qza", ss_, blk)])
                        S.op("dve", (lambda ss_, blk, pb: lambda e: e.tensor_scalar(
                            out=qz[ss_][64:128, 1, 512 * blk:512 * blk + 512], in0=bank(pb)[64:128, :], scalar1=0.125, scalar2=None,
                            op0=ALU.mult))(ss_, blk, pb),
                            reads=[("ps", pb), ("qz", ss_, blk), ("qza", ss_, blk)], writes=[("qzb", ss_, blk)])
                    else:
                        eng = "act" if blk % 2 == 0 else "dve"
                        if eng == "act":
                            S.op("act", (lambda ss_, blk, pb: lambda e: e.activation(
                                out=stg[ss_][:, 512 * blk:512 * blk + 512], in_=bank(pb), func=AF.Copy))(ss_, blk, pb),
                                reads=[("ps", pb)], writes=[("stg", ss_, blk)])
                        else:
                            S.op("dve", (lambda ss_, blk, pb: lambda e: e.tensor_copy(
                                out=stg[ss_][:, 512 * blk:512 * blk + 512], in_=bank(pb)))(ss_, blk, pb),
                                reads=[("ps", pb)], writes=[("stg", ss_, blk)])
                if kind == "q":
                    S.op("sp", (lambda h, ss_: lambda e: e.dma_start(out=QT[h].rearrange("c p t -> p c t"), in_=qz[ss_]))(h, ss_),
                         reads=[(k_, ss_, b_) for b_ in range(4) for k_ in ("qza", "qzb")],
                         writes=[("scr", kind, h)] + [(k_, ss_, b_) for b_ in range(4) for k_ in ("qza", "qzb")], dma="qzo%d" % ss_)
                    continue
                else:
                    dst = KT[h].rearrange("p (i two c) -> p i two c", two=2, c=512)[:, :, pair_half, :]
                    src = stg[ss_].rearrange("p (i c) -> p i c", c=512)
                S.op("sp", (lambda dst, src: lambda e: e.dma_start(out=dst, in_=src))(dst, src),
                     reads=[("stg", ss_, b_) for b_ in range(4)], writes=[("scr", kind, h)], dma="stgo%d" % ss_)
            for vg in range(4):
                ws = vg % 2
                load_wchunk(wv[ws], w_in_v, 2048 + 256 * vg, 256, "wv", ws)
                for t in range(16):
                    pb = 4 + (pb_rot % 4)
                    pb_rot += 1

                    def fn(e, t=t, pb=pb, ws=ws):
                        for dc in range(DC):
                            ins = e.matmul(bank(pb, 256), lhsT=hT[:, dc, 128 * t:128 * t + 128], rhs=wv[ws][:, dc, :],
                                           start=(dc == 0), stop=(dc == DC - 1))
                        return ins
                    S.op("pe", fn, reads=[("wv", ws), ("hT", t)], writes=[("ps", pb)])
                    src = bank(pb, 256).rearrange("p (a b) -> p a b", b=128)
                    if t % 2 == 0:
                        S.op("act", (lambda ws, t, src: lambda e: e.activation(out=vstg[ws][:, :, t, :], in_=src, func=AF.Copy))(ws, t, src),
                             reads=[("ps", pb)], writes=[("vstg", ws, t)])
                    else:
                        S.op("dve", (lambda ws, t, src: lambda e: e.tensor_copy(out=vstg[ws][:, :, t, :], in_=src))(ws, t, src),
                             reads=[("ps", pb)], writes=[("vstg", ws, t)])
                for hh in range(2):
                    h = 2 * vg + hh
                    dst = VS[h].rearrange("p (i j) e -> p i j e", j=8)[:, :, 4 * pair_half:4 * pair_half + 4, :]
                    src = vstg[ws][:, hh, :, :].rearrange("p (i t) e -> p i t e", t=4)
                    S.op("sp", (lambda dst, src: lambda e: e.dma_start(out=dst, in_=src))(dst, src),
                         reads=[("vstg", ws, t) for t in range(16)], writes=[("scr", "v", h)], dma="vstgo%d_%d" % (ws, hh))
            if not own:
                S.barrier()
                return
            S.barrier()
            A.reset(m1)
            NDVE = 22
            NPE = NTAP - NDVE
            wc = [A.alloc([DC, 128], BF16) for _ in range(4)]
            hg = [A.alloc([4, 544], F32) for _ in range(2)]
            sga = A.alloc([4, 544], F32)
            accA = A.alloc([4, 512], F32)
            off_hgh = A.mark()
            hgh = [A.alloc([4, 544], BF16) for _ in range(2)]
            hgl = [A.alloc([4, 544], BF16) for _ in range(2)]
            off_dw = A.mark()
            dwh = [A.alloc([NPE, 128], BF16) for _ in range(2)]
            dwl = [A.alloc([NPE, 128], BF16) for _ in range(2)]
            cres = [A.alloc([2048], F32) for _ in range(2)]
            sqt = A.alloc([2048], F32)
            sumacc = A.alloc([2048], F32)
            sqacc = A.alloc([2048], F32)

            def fnh(e, wt, pb):
                for dc in range(DC):
                    ins = e.matmul(bank(pb, 128), lhsT=wt[:, dc, :], rhs=hT[:, dc, 2048:2176],
                                   start=(dc == 0), stop=(dc == DC - 1))
                return ins

            def stage_u(cc):
                wa, wg_ = wc[(2 * cc) % 4], wc[(2 * cc + 1) % 4]
                ka, kg = ("wc", (2 * cc) % 4), ("wc", (2 * cc + 1) % 4)
                load_wchunk(wa, w_in_v, 3072 + 128 * cc, 128, "wc", (2 * cc) % 4)
                load_wchunk(wg_, w_in_v, 4096 + 128 * cc, 128, "wc", (2 * cc + 1) % 4)
                precast_wd(6)
                hs = cc % 2
                for blk in range(4):
                    pa, pg = 4 + 2 * (blk % 2), 5 + 2 * (blk % 2)
                    proj_fm(wa, ka, hT, hk(blk), 512 * blk, pa)
                    proj_fm(wg_, kg, hT, hk(blk), 512 * blk, pg)
                    S.op("act", (lambda blk, pg: lambda e: e.activation(out=sga[:, blk, 32:544], in_=bank(pg), func=AF.Sigmoid))(blk, pg),
                         reads=[("ps", pg)], writes=[("sga", blk)])
                    S.op("act", (lambda blk, pa, hs: lambda e: e.activation(out=hg[hs][:, blk, 32:544], in_=bank(pa), func=AF.Copy))(blk, pa, hs),
                         reads=[("ps", pa)], writes=[("hga", hs, blk)])
                S.op("pe", (lambda wa: lambda e: fnh(e, wa, 4))(wa), reads=[ka, ("hT", 16)], writes=[("ps", 4)])
                S.op("pe", (lambda wg_: lambda e: fnh(e, wg_, 5))(wg_), reads=[kg, ("hT", 16)], writes=[("ps", 5)])
                S.op("act", lambda e: e.activation(out=sga[:, :, 0:32], in_=bank(5, 128).rearrange("p (a b) -> p a b", b=32), func=AF.Sigmoid),
                     reads=[("ps", 5)], writes=[("sga", "halo")])
                S.op("act", (lambda hs: lambda e: e.activation(out=hg[hs][:, :, 0:32], in_=bank(4, 128).rearrange("p (a b) -> p a b", b=32),
                                                               func=AF.Copy))(hs),
                     reads=[("ps", 4)], writes=[("hga", hs, "halo")])

            def stage_g(cc):
                hs = cc % 2
                hak = [("hga", hs, blk) for blk in range(4)] + [("hga", hs, "halo")]
                sgk = [("sga", blk) for blk in range(4)] + [("sga", "halo")]
                S.op("dve", (lambda hs: lambda e: e.tensor_tensor(out=hg[hs], in0=hg[hs], in1=sga, op=ALU.mult))(hs),
                     reads=hak + sgk, writes=[("hg", hs)] + hak)
                S.op("act", (lambda hs: lambda e: e.activation(out=hgh[hs], in_=hg[hs], func=AF.Copy))(hs), reads=[("hg", hs)], writes=[("hgh", hs)])
                S.op("dve", (lambda hs: lambda e: e.tensor_tensor(out=hgl[hs], in0=hg[hs], in1=hgh[hs], op=ALU.subtract))(hs),
                     reads=[("hg", hs), ("hgh", hs)], writes=[("hgl", hs)])
                for jj in range(NPE):
                    j = NDVE + jj
                    wj = cw[:, cc, j:j + 1]
                    S.op("act", (lambda hs, jj, wj: lambda e: e.activation(out=dwh[hs][:, jj, :], in_=identf, func=AF.Copy, scale=wj))(hs, jj, wj),
                         reads=["identf", "small"], writes=[("dwh", hs, jj)])
                    S.op("dve", (lambda hs, jj, wj: lambda e: e.scalar_tensor_tensor(out=dwl[hs][:, jj, :], in0=identf, scalar=wj, in1=dwh[hs][:, jj, :],
                                                                                      op0=ALU.mult, op1=ALU.subtract))(hs, jj, wj),
                         reads=["identf", "small", ("dwh", hs, jj)], writes=[("dwl", hs, jj)])

            def stage_t(cc):
                hs = cc % 2
                cs = cc % 2
                for blk in range(4):
                    def pconv(e, hs=hs, blk=blk):
                        n = 0
                        for jj in range(NPE):
                            j = NDVE + jj
                            for (wt, ht) in ((dwh, hgh), (dwh, hgl), (dwl, hgh)):
                                ins = e.matmul(bank(blk), lhsT=wt[hs][:, jj, :], rhs=ht[hs][:, blk, 2 + j:2 + j + 512],
                                               start=(n == 0), stop=(n == 3 * NPE - 1))
                                n += 1
                        return ins
                    S.op("pe", pconv, reads=[("hgh", hs), ("hgl", hs)] + [(k_, hs, jj) for jj in range(NPE) for k_ in ("dwh", "dwl")],
                         writes=[("ps", blk)])
                for j in range(NDVE):
                    src = hg[hs][:, :, 2 + j:2 + j + 512]
                    wj = cw[:, cc, j:j + 1]
                    if j == 0:
                        S.op("dve", (lambda src, wj, cc: lambda e: e.tensor_scalar(out=accA, in0=src, scalar1=wj, scalar2=cb[:, cc:cc + 1],
                                                                                   op0=ALU.mult, op1=ALU.add))(src, wj, cc),
                             reads=[("hg", hs), "small"], writes=["accA"])
                    else:
                        S.op("dve", (lambda src, wj: lambda e: e.scalar_tensor_tensor(out=accA, in0=src, scalar=wj, in1=accA,
                                                                                      op0=ALU.mult, op1=ALU.add))(src, wj),
                             reads=[("hg", hs), "small", "accA"], writes=["accA"])
                S.op("dve", (lambda cs: lambda e: e.tensor_tensor(out=cres[cs], in0=accA.rearrange("p a b -> p (a b)"), in1=psum_t[:, 0:2048], op=ALU.add))(cs),
                     reads=["accA"] + [("ps", b_) for b_ in range(4)], writes=[("cres", cs)])
                S.op("sp", (lambda cs, cc: lambda e: e.dma_start(out=CS[cc], in_=cres[cs]))(cs, cc),
                     reads=[("cres", cs)], writes=[("scr", "cs", cc)], dma="cres%d" % cs)
                S.op("act", (lambda cs: lambda e: e.activation(out=sqt, in_=cres[cs], func=AF.Square))(cs),
                     reads=[("cres", cs)], writes=["sqt"])

            def stage_s(cc):
                cs = cc % 2
                for blk in range(4):
                    for which, srcb, acc, key in ((0, cres[cs], sumacc, ("cres", cs)), (1, sqt, sqacc, "sqt")):
                        pb = 6 + which
                        S.op("pe", (lambda srcb, blk, pb: lambda e: e.matmul(bank(pb), lhsT=ones_f, rhs=srcb[:, 512 * blk:512 * blk + 512],
                                                                           start=True, stop=True))(srcb, blk, pb),
                             reads=[key, "ones_f"], writes=[("ps", pb)])
                        if cc == 0:
                            S.op("dve", (lambda acc, blk, pb: lambda e: e.tensor_copy(out=acc[:, 512 * blk:512 * blk + 512], in_=bank(pb)))(acc, blk, pb),
                                 reads=[("ps", pb)], writes=[("acc", which, blk)])
                        else:
                            S.op("dve", (lambda acc, blk, pb: lambda e: e.tensor_tensor(out=acc[:, 512 * blk:512 * blk + 512],
                                                                                       in0=acc[:, 512 * blk:512 * blk + 512], in1=bank(pb), op=ALU.add))(acc, blk, pb),
                                 reads=[("ps", pb), ("acc", which, blk)], writes=[("acc", which, blk)])

            stage_u(0)
            stage_g(0)
            for cc in range(8):
                if cc + 1 < 8:
                    stage_u(cc + 1)
                stage_t(cc)
                if cc + 1 < 8:
                    stage_g(cc + 1)
                stage_s(cc)
            S.barrier()
            acck = [("acc", w_, b_) for w_ in range(2) for b_ in range(4)]
            mean = sumacc
            rstd = sqacc
            m2 = sqt
            S.op("dve", lambda e: e.tensor_scalar(out=mean, in0=sumacc, scalar1=1.0 / 1024, scalar2=None, op0=ALU.mult),
                 reads=acck, writes=["mean"])
            S.op("dve", lambda e: e.tensor_tensor(out=m2, in0=mean, in1=mean, op=ALU.mult), reads=["mean"], writes=["m2", "sqt"])
            S.op("dve", lambda e: e.scalar_tensor_tensor(out=rstd, in0=sqacc, scalar=1.0 / 1024, in1=m2, op0=ALU.mult, op1=ALU.subtract),
                 reads=acck + ["m2"], writes=["var"])
            S.op("dve", lambda e: e.tensor_scalar(out=rstd, in0=rstd, scalar1=EPS, scalar2=None, op0=ALU.add), reads=["var"], writes=["var2"])
            S.op("act", lambda e: e.activation(out=rstd, in_=rstd, func=AF.Sqrt), reads=["var2"], writes=["sd"])
            S.op("dve", lambda e: e.reciprocal(out=rstd, in_=rstd), reads=["sd"], writes=["rstd"])
            cl = cres
            tmp = [A.at(off_hgh, [2048], F32), A.at(off_hgh + 8192, [2048], F32)]
            mst = [A.at(off_dw, [2048], BF16), A.at(off_dw + 4096, [2048], BF16)]
            A.reset(base_mark)
            hT_b = A.alloc([DC, 16 * 128], BF16)
            xt_b, xn_b, sqj_b, stat_b = build_bufs()
            assert A.mark() <= off_hgh, (A.mark(), off_hgh)

            def ln_chunk(cc):
                s = cc % 2
                S.op("pool", (lambda s, cc: lambda e: e.dma_start(out=cl[s], in_=CS[cc]))(s, cc),
                     reads=[("scr", "cs", cc)], writes=[("cres", s)], dma="cl%d" % s)
                S.op("dve", (lambda s: lambda e: e.tensor_tensor(out=tmp[s], in0=cl[s], in1=mean, op=ALU.subtract))(s),
                     reads=[("cres", s), "mean"], writes=[("tmp", s)])
                S.op("dve", (lambda s: lambda e: e.tensor_tensor(out=tmp[s], in0=tmp[s], in1=rstd, op=ALU.mult))(s),
                     reads=[("tmp", s), "rstd"], writes=[("tmp", s)])
                S.op("act", (lambda s, cc: lambda e: e.activation(out=mst[s], in_=tmp[s], func=AF.Silu, bias=lnb[:, cc:cc + 1],
                                                                scale=lng[:, cc:cc + 1]))(s, cc),
                     reads=[("tmp", s), "small"], writes=[("mst", s)])
                S.op("act", (lambda s, cc: lambda e: e.dma_start(out=MT[8 + cc], in_=mst[s]))(s, cc),
                     reads=[("mst", s)], writes=[("scr", "mt", 8 + cc)], dma="mst%d" % s)

            build_hT(list(range(17, 33)), hT_b, gBm, gBm_keys, xt_b, xn_b, sqj_b, stat_b, [0, 2, 4, 6],
                     hook=lambda j: ln_chunk(j - 1) if 1 <= j <= 8 else None)
            S.barrier()

        if "A" in phases:
            phase_proj(True)
        if stop_after != "A" and "B" in phases:
            phase_proj(False, prebuilt=("A" in phases))

        def phase_attn():
            A.reset(base_mark)
            wo = A.alloc([DC, D], BF16)
            kt = [A.alloc([S_LEN], BF16) for _ in range(2)]
            vv = [A.alloc([32, 128], BF16) for _ in range(2)]
            qt = [A.alloc([2, 2048], BF16) for _ in range(2)]
            ee = [A.alloc([9 * 512], BF16) for _ in range(2)]
            bb = [A.alloc([9 * 512], F32)] * 2
            pbuf = [A.alloc([2, 512], BF16) for _ in range(3)]
            r0 = A.alloc([512], F32)
            r1 = A.alloc([512], F32)
            t0 = A.alloc([512], F32)
            t1 = A.alloc([512], F32)
            osq = A.alloc([512], F32)
            cO = [A.alloc([512], F32) for _ in range(2)]
            cL = [A.alloc([512], F32) for _ in range(2)]
            ohi = A.alloc([512], BF16)
            olo = A.alloc([512], BF16)
            ostg = [A.alloc([2048], BF16) for _ in range(2)]
            OB = [4, 5]
            LB = [6, 7]

            def head_loads(h):
                s = h % 2
                S.op("sp", (lambda s, h: lambda e: e.dma_start(out=kt[s], in_=KT[h]))(s, h), writes=[("kt", s)], dma="kt%d" % s)
                S.op("sp", (lambda s, h: lambda e: e.dma_start(out=vv[s], in_=VS[h]))(s, h), writes=[("vv", s)], dma="vv%d" % s)
                S.op("sp", (lambda s, h: lambda e: e.dma_start(out=qt[s], in_=QT[h].rearrange("c p t -> p c t")))(s, h), writes=[("qt", s)], dma="qt%d" % s)
                S.op("sp", (lambda s, h: lambda e: e.dma_start(out=bb[s], in_=bias_t[h]))(s, h), writes=[("bb", 0)], dma="bb0")

            def head_ee(h):
                s = h % 2
                S.op("act", (lambda s, h: lambda e: e.activation(out=ee[s], in_=bb[s], func=AF.Exp, bias=negc[:, h:h + 1]))(s, h),
                     reads=[("bb", 0), "negc"], writes=[("ee", s)])

            def group_units(i):
                units = []
                for pr in range(i):
                    for t in range(8):
                        if pr == i - 1 and t == 7:
                            continue
                        units.append((8 * pr + t, None))
                if i >= 1:
                    units.append((8 * (i - 1) + 7, 0))
                for t in range(8):
                    units.append((8 * i + t, 1 + t))
                return units

            st_ = dict(slot=0, pu=0)

            def s_op(h, i, kp):
                s = h % 2
                sl = st_["slot"] % 2
                st_["slot"] += 1

                def fn(e):
                    e.matmul(bank(2 * sl), lhsT=kt[s][:, 128 * kp:128 * kp + 128], rhs=qt[s][:, 0, 512 * i:512 * i + 512], start=True, stop=True)
                    return e.matmul(bank(2 * sl + 1), lhsT=kt[s][:, 128 * kp:128 * kp + 128], rhs=qt[s][:, 1, 512 * i:512 * i + 512],
                                    start=True, stop=True)
                S.op("pe", fn, reads=[("kt", s), ("qt", s)], writes=[("ps", 2 * sl), ("ps", 2 * sl + 1)])
                return sl

            groups = [(h, i) for h in range(nheads) for i in range(4)]
            for c in range(DC):
                S.op("pool", (lambda c: lambda e: e.dma_start(out=wo[:, c, :], in_=w_out[128 * c:128 * c + 128, :]))(c),
                     writes=[("wo", c)], dma="wo%d" % (c % 4))
            head_loads(0)
            head_ee(0)
            pre_slot = None
            pending = [None]
            pending15 = [None]
            for gi, (h, i) in enumerate(groups):
                s = h % 2
                if i == 0 and h + 1 < nheads:
                    head_loads(h + 1)
                units = group_units(i)
                nu = len(units)
                cur = pre_slot if pre_slot is not None else s_op(h, i, units[0][0])
                for u in range(nu):
                    kp, nj = units[u]
                    nxt = s_op(h, i, units[u + 1][0]) if u + 1 < nu else None
                    pk = st_["pu"] % 3
                    st_["pu"] += 1
                    pb_ = pbuf[pk]
                    if nj is None:
                        src = psum_t[:, 1024 * cur:1024 * cur + 1024].rearrange("p (c q) -> p c q", q=512)
                        S.op("act", (lambda pb_, src: lambda e: e.activation(out=pb_, in_=src, func=AF.Exp))(pb_, src),
                             reads=[("ps", 2 * cur), ("ps", 2 * cur + 1)], writes=[("pb", pk, 0), ("pb", pk, 1)])
                    else:
                        for c in range(2):
                            S.op("act", (lambda pb_, c, cur: lambda e: e.activation(out=pb_[:, c, :], in_=bank(2 * cur + c), func=AF.Exp))(pb_, c, cur),
                                 reads=[("ps", 2 * cur + c)], writes=[("pb", pk, c)])
                            S.op("dve", (lambda pb_, nj, s, c: lambda e: e.tensor_tensor(out=pb_[:, c, :], in0=pb_[:, c, :],
                                                                                      in1=ee[s][:, 512 * nj:512 * nj + 512], op=ALU.mult))(pb_, nj, s, c),
                                 reads=[("pb", pk, c), ("ee", s)], writes=[("pb", pk, c)])
                    for c in range(2):
                        def pv(e, pb_=pb_, kp=kp, u=u, s=s, nu=nu, c=c):
                            e.matmul(bank(OB[c]), lhsT=vv[s][:, kp, :], rhs=pb_[:, c, :], start=(u == 0), stop=(u == nu - 1))
                            return e.matmul(bank(LB[c]), lhsT=ones_bf, rhs=pb_[:, c, :], start=(u == 0), stop=(u == nu - 1))
                        S.op("pe", pv, reads=[("vv", s), ("pb", pk, c), "ones_bf"], writes=[("ps", OB[c]), ("ps", LB[c])])
                    cur = nxt
                    if u == 2 and pending15[0] is not None:
                        pending15[0]()
                        pending15[0] = None
                    if u == min(nu - 2, 10) and pending[0] is not None:
                        pending[0]()
                        pending[0] = None
                if i == 0 and h + 1 < nheads:
                    head_ee(h + 1)
                if gi + 1 < len(groups):
                    nh_, ni_ = groups[gi + 1]
                    pre_slot = s_op(nh_, ni_, group_units(ni_)[0][0])
                else:
                    pre_slot = None
                S.op("act", lambda e: e.activation(out=cO[0], in_=bank(OB[0]), func=AF.Copy), reads=[("ps", OB[0])], writes=["cO0"])
                S.op("dve", lambda e: e.tensor_copy(out=cL[0], in_=bank(LB[0])), reads=[("ps", LB[0])], writes=["cL0"])
                S.op("act", lambda e: e.activation(out=cO[1], in_=bank(OB[1]), func=AF.Copy), reads=[("ps", OB[1])], writes=["cO1"])
                S.op("dve", lambda e: e.tensor_copy(out=cL[1], in_=bank(LB[1])), reads=[("ps", LB[1])], writes=["cL1"])
                def part15():
                  S.op("act", lambda e: e.activation(out=r0, in_=cL[0], func=AF.Ln), reads=["cL0"], writes=["r0"])
                  S.op("act", lambda e: e.activation(out=r0, in_=r0, func=AF.Exp, scale=-1.0), reads=["r0"], writes=["r0"])
                  S.op("act", lambda e: e.activation(out=r1, in_=cL[1], func=AF.Ln), reads=["cL1"], writes=["r1"])
                  S.op("act", lambda e: e.activation(out=r1, in_=r1, func=AF.Exp, scale=-1.0), reads=["r1"], writes=["r1"])
                  S.op("dve", lambda e: e.tensor_tensor(out=t0, in0=cO[0], in1=r0, op=ALU.mult), reads=["cO0", "r0"], writes=["t0"])
                  S.op("dve", lambda e: e.tensor_tensor(out=t1, in0=cO[1], in1=r1, op=ALU.mult), reads=["cO1", "r1"], writes=["t1"])
                  S.op("dve", lambda e: e.scalar_tensor_tensor(out=t0, in0=t1, scalar=neglam, in1=t0, op0=ALU.mult, op1=ALU.add),
                       reads=["t0", "t1", "neglam"], writes=["t0"])
                  S.op("dve", lambda e: e.tensor_tensor(out=osq, in0=t0, in1=t0, op=ALU.mult), reads=["t0"], writes=["osq"])
                  S.op("dve", lambda e: e.tensor_copy(out=ohi, in_=osq), reads=["osq"], writes=["ohi"])
                  S.op("dve", lambda e: e.tensor_tensor(out=olo, in0=osq, in1=ohi, op=ALU.subtract), reads=["osq", "ohi"], writes=["olo"])


                def part2(s=s, i=i, h=h):
                    NBk = 2 * (st_["slot"] % 2)

                    def nmm(e, NBk=NBk):
                        e.matmul(bank(NBk), lhsT=ones_bf, rhs=ohi, start=True, stop=False)
                        return e.matmul(bank(NBk), lhsT=ones_bf, rhs=olo, start=False, stop=True)
                    S.op("pe", nmm, reads=["ohi", "olo", "ones_bf"], writes=[("ps", NBk)])
                    S.op("dve", (lambda NBk: lambda e: e.tensor_scalar(out=r0, in0=bank(NBk), scalar1=1.0 / 128, scalar2=EPS, op0=ALU.mult, op1=ALU.add))(NBk),
                         reads=[("ps", NBk)], writes=["r0"])
                    S.op("act", lambda e: e.activation(out=r0, in_=r0, func=AF.Ln), reads=["r0"], writes=["r0"])
                    S.op("act", lambda e: e.activation(out=r0, in_=r0, func=AF.Exp, scale=-0.5), reads=["r0"], writes=["r0"])
                    S.op("dve", (lambda s, i: lambda e: e.scalar_tensor_tensor(out=ostg[s][:, 512 * i:512 * i + 512], in0=t0, scalar=subg08, in1=r0,
                                                                              op0=ALU.mult, op1=ALU.mult))(s, i),
                         reads=["t0", "r0", "subg08"], writes=[("ostg", s, i)])
                    if i == 3:
                        S.op("sp", (lambda s, h: lambda e: e.dma_start(out=MT[h], in_=ostg[s]))(s, h),
                             reads=[("ostg", s, i_) for i_ in range(4)], writes=[("scr", "mt", h)], dma="ostg%d" % s)
                if gi + 1 < len(groups):
                    pending15[0] = part15
                    pending[0] = part2
                else:
                    part15()
                    part2()
            S.barrier()

        if stop_after not in ("A", "B") and "C" in phases:
            phase_attn()

        def phase_outproj():
            A.reset(base_mark)
            wo = A.alloc([DC, D], BF16)
            gpost = A.alloc([D], F32)
            mt = [A.alloc([DC, 512], BF16) for _ in range(2)]
            xt = [A.alloc([D], F32) for _ in range(2)]
            x1 = [A.alloc([D], F32) for _ in range(2)]
            xn = [A.alloc([D], BF16) for _ in range(2)]
            sqj = A.alloc([D], BF16)
            h2s = [A.alloc([DC, 512], BF16) for _ in range(2)]
            stat = A.alloc([16], F32)
            S.op("sp", lambda e: e.dma_start(out=gpost, in_=g_post[0]), writes=["gpost"], dma="gpost")
            if "C" not in phases:
                for c in range(DC):
                    S.op("pool", (lambda c: lambda e: e.dma_start(out=wo[:, c, :], in_=w_out[128 * c:128 * c + 128, :]))(c),
                         writes=[("wo", c)], dma="wo%d" % (c % 4))
            wok = [("wo", c) for c in range(DC)]
            MTv = MT.rearrange("c p t -> p c t")
            def d_front(tt):
                blk, tl = tt // 4, tt % 4
                bs = blk % 2
                s = tt % 2
                pb0 = 4 * s
                if tl == 0:
                    S.op("sp", (lambda bs, blk: lambda e: e.dma_start(out=mt[bs], in_=MTv[:, :, 512 * blk:512 * blk + 512]))(bs, blk),
                         writes=[("mt", bs)], dma="mt%d" % bs)
                S.op("sp", (lambda s, tt: lambda e: e.dma_start(out=xt[s], in_=x_in[128 * tt:128 * tt + 128, :]))(s, tt),
                     writes=[("xt", s)], dma="xt%d" % s)

                def fn(e, bs=bs, tl=tl, pb0=pb0):
                    for j in range(4):
                        for c in range(DC):
                            ins = e.matmul(bank(pb0 + j), lhsT=mt[bs][:, c, 128 * tl:128 * tl + 128], rhs=wo[:, c, 512 * j:512 * j + 512],
                                           start=(c == 0), stop=(c == DC - 1))
                    return ins
                S.op("pe", fn, reads=[("mt", bs)] + wok, writes=[("ps", pb0 + j) for j in range(4)])

            def d_back(tt):
                blk, tl = tt // 4, tt % 4
                bs = blk % 2
                s = tt % 2
                pb0 = 4 * s
                pk = [("ps", pb0 + j) for j in range(4)]
                pv_ = psum_t[:, 512 * pb0:512 * pb0 + 2048]
                S.op("act", (lambda s, pv_: lambda e: e.activation(out=sqj, in_=pv_, func=AF.Square,
                                                                  accum_out=stat[:, 8 * s:8 * s + 1]))(s, pv_),
                     reads=pk, writes=["sqj", ("ss", s)])
                rstd_ops(("o", s), stat[:, 8 * s:8 * s + 1], D, stat[:, 8 * s + 2:8 * s + 3], stat[:, 8 * s + 1:8 * s + 2], ("ss", s), ("rs", s))
                S.op("dve", (lambda s, pv_: lambda e: e.scalar_tensor_tensor(out=x1[s], in0=pv_, scalar=stat[:, 8 * s + 2:8 * s + 3],
                                                                            in1=gpost, op0=ALU.mult, op1=ALU.mult))(s, pv_),
                     reads=pk + [("rs", s), "gpost"], writes=[("x1", s)])
                S.op("dve", (lambda s: lambda e: e.tensor_tensor(out=x1[s], in0=x1[s], in1=xt[s], op=ALU.add))(s),
                     reads=[("x1", s), ("xt", s)], writes=[("x1", s)])
                S.op("sp", (lambda s, tt: lambda e: e.dma_start(out=out[128 * tt:128 * tt + 128, :], in_=x1[s]))(s, tt),
                     reads=[("x1", s)], writes=[("out", tt)], dma="x1o%d" % s)
                S.op("act", (lambda s: lambda e: e.activation(out=sqj, in_=x1[s], func=AF.Square,
                                                             accum_out=stat[:, 8 * s + 4:8 * s + 5]))(s),
                     reads=[("x1", s)], writes=["sqj", ("ss2", s)])
                rstd_ops(("o2", s), stat[:, 8 * s + 4:8 * s + 5], D, stat[:, 8 * s + 6:8 * s + 7], stat[:, 8 * s + 5:8 * s + 6], ("ss2", s), ("rs2", s))
                S.op("dve", (lambda s: lambda e: e.tensor_scalar(out=xn[s], in0=x1[s], scalar1=stat[:, 8 * s + 6:8 * s + 7],
                                                                 scalar2=None, op0=ALU.mult))(s),
                     reads=[("x1", s), ("rs2", s)], writes=[("xn", s)])
                pT = psum_t[:, 512 * pb0:512 * pb0 + 1024].bitcast(BF16).rearrange("p (a b) -> p a b", b=128)

                def tr(e, s=s, pT=pT):
                    for dc in range(DC):
                        ins = e.transpose(pT[:, dc, :], xn[s][:, 128 * dc:128 * dc + 128], ident)
                    return ins
                S.op("pe", tr, reads=[("xn", s), "ident"], writes=[("ps", pb0), ("ps", pb0 + 1)])
                S.op("dve", (lambda bs, tl, pT: lambda e: e.tensor_tensor(out=h2s[bs][:, :, 128 * tl:128 * tl + 128], in0=pT, in1=gBf,
                                                                          op=ALU.mult))(bs, tl, pT),
                     reads=[("ps", pb0), ("ps", pb0 + 1)] + gBf_keys, writes=[("h2s", bs, tl)])
                if tl == 3:
                    S.op("sp", (lambda bs, blk: lambda e: e.dma_start(out=H2[:, :, 512 * blk:512 * blk + 512], in_=h2s[bs]))(bs, blk),
                         reads=[("h2s", bs, t_) for t_ in range(4)], writes=[("scr", "h2", blk)], dma="h2s%d" % bs)

            for tt in range(17):
                if tt < 16:
                    d_front(tt)
                if tt > 0:
                    d_back(tt - 1)
            S.barrier()

        if stop_after not in ("A", "B", "C") and "D" in phases:
            phase_outproj()

        def phase_ffn():
            A.reset(base_mark)
            gpost = A.alloc([D], F32)
            h2 = [A.alloc([DC, 512], BF16)] * 2
            actT = A.alloc([FC, 512], BF16)
            wg = [A.alloc([DC, 512], BF16) for _ in range(2)]
            wu = [A.alloc([DC, 512], BF16) for _ in range(2)]
            wd = [A.alloc([D], BF16) for _ in range(3)]
            sgt = [A.alloc([512], F32) for _ in range(2)]
            x1 = [A.alloc([D], F32) for _ in range(2)]
            y = [A.alloc([D], F32) for _ in range(2)]
            sqj = A.alloc([D], BF16)
            stat = A.alloc([8], F32)
            S.op("sp", lambda e: e.dma_start(out=gpost, in_=g_post[1]), writes=["gpost"], dma="gpost2")
            wdi = 0
            for blk in range(4):
                bs = 0
                S.op("sp", (lambda bs, blk: lambda e: e.dma_start(out=h2[bs], in_=H2[:, :, 512 * blk:512 * blk + 512]))(bs, blk),
                     writes=[("h2", bs)], dma="h2%d" % bs)
                for fc in range(FC):
                    ws = (fc // 4) % 2
                    if fc % 4 == 0:
                        S.op("pool", (lambda ws, fc: lambda e: e.dma_start(out=wg[ws], in_=w_gate_v[:, :, 128 * fc:128 * fc + 512]))(ws, fc),
                             writes=[("wg", ws)], dma="wg%d" % ws)
                        S.op("pool", (lambda ws, fc: lambda e: e.dma_start(out=wu[ws], in_=w_up_v[:, :, 128 * fc:128 * fc + 512]))(ws, fc),
                             writes=[("wu", ws)], dma="wu%d" % ws)
                    pg, pu = 2 * (fc % 2), 2 * (fc % 2) + 1
                    fo = 128 * (fc % 4)

                    def fn(e, wt, pb, bs=bs, fo=fo):
                        for dc in range(DC):
                            ins = e.matmul(bank(pb), lhsT=wt[:, dc, fo:fo + 128], rhs=h2[bs][:, dc, :], start=(dc == 0), stop=(dc == DC - 1))
                        return ins
                    S.op("pe", (lambda fn, ws, pg: lambda e: fn(e, wg[ws], pg))(fn, ws, pg), reads=[("wg", ws), ("h2", bs)], writes=[("ps", pg)])
                    S.op("pe", (lambda fn, ws, pu: lambda e: fn(e, wu[ws], pu))(fn, ws, pu), reads=[("wu", ws), ("h2", bs)], writes=[("ps", pu)])
                    S.op("act", (lambda fc, pg: lambda e: e.activation(out=sgt[fc % 2], in_=bank(pg), func=AF.Silu))(fc, pg),
                         reads=[("ps", pg)], writes=[("sgt", fc % 2)])
                    S.op("dve", (lambda fc, pu: lambda e: e.tensor_tensor(out=actT[:, fc, :], in0=sgt[fc % 2], in1=bank(pu), op=ALU.mult))(fc, pu),
                         reads=[("sgt", fc % 2), ("ps", pu)], writes=[("actT", fc)])
                for half in range(2):
                    for tl in range(2):
                        tt = 4 * blk + 2 * half + tl
                        S.op("act", (lambda tl, tt: lambda e: e.dma_start(out=x1[tl], in_=out[128 * tt:128 * tt + 128, :]))(tl, tt),
                             reads=[("out", tt)], writes=[("x1", tl)], dma="x1i%d" % tl)
                    for fc in range(FC):
                        ws = wdi % 3
                        wdi += 1
                        S.op("sp", (lambda ws, fc: lambda e: e.dma_start(out=wd[ws], in_=WDB[fc]))(ws, fc),
                             writes=[("wd", ws)], dma="wd%d" % ws, extra_deps=wdb_ops)

                        def fn(e, ws=ws, fc=fc, half=half):
                            for tl in range(2):
                                t = 2 * half + tl
                                for j in range(4):
                                    ins = e.matmul(bank(4 * tl + j), lhsT=actT[:, fc, 128 * t:128 * t + 128], rhs=wd[ws][:, 512 * j:512 * j + 512],
                                                   start=(fc == 0), stop=(fc == FC - 1))
                            return ins
                        S.op("pe", fn, reads=[("wd", ws), ("actT", fc)], writes=[("ps", b_) for b_ in range(8)])
                    for tl in range(2):
                        tt = 4 * blk + 2 * half + tl
                        s = tl
                        pk = [("ps", 4 * tl + j) for j in range(4)]
                        pv_ = psum_t[:, 2048 * tl:2048 * tl + 2048]
                        S.op("act", (lambda s, pv_: lambda e: e.activation(out=sqj, in_=pv_, func=AF.Square, accum_out=stat[:, 4 * s:4 * s + 1]))(s, pv_),
                             reads=pk, writes=["sqj", ("ss", s)])
                        rstd_ops(("f", s), stat[:, 4 * s:4 * s + 1], D, stat[:, 4 * s + 2:4 * s + 3], stat[:, 4 * s + 1:4 * s + 2], ("ss", s), ("rs", s))
                        S.op("dve", (lambda s, pv_: lambda e: e.scalar_tensor_tensor(out=y[s], in0=pv_, scalar=stat[:, 4 * s + 2:4 * s + 3],
                                                                                    in1=gpost, op0=ALU.mult, op1=ALU.mult))(s, pv_),
                             reads=pk + [("rs", s), "gpost"], writes=[("y", s)])
                        S.op("dve", (lambda s: lambda e: e.tensor_tensor(out=y[s], in0=y[s], in1=x1[s], op=ALU.add))(s),
                             reads=[("y", s), ("x1", s)], writes=[("y", s)])
                        S.op("act", (lambda s, tt: lambda e: e.dma_start(out=out[128 * tt:128 * tt + 128, :], in_=y[s]))(s, tt),
                             reads=[("y", s)], writes=[("out", tt)], dma="yo%d" % s)
            S.barrier()

        if stop_after not in ("A", "B", "C", "D") and "E" in phases:
            phase_ffn()
        S.barrier(final=True)
        S.emit(nc, st)
    return nc


def _t5_bucket_np(n):
    n = np.maximum(n, 0)
    max_exact = 16
    nf = np.maximum(n, 1).astype(np.float32)
    large = max_exact + (np.log(nf / np.float32(max_exact)) / np.float32(math.log(128 / max_exact))
                         * np.float32(32 - max_exact)).astype(np.int32)
    large = np.minimum(large, 31)
    return np.where(n < max_exact, n, large)


def _bias_tiles(rel_bias, p):
    ki = np.arange(128)[:, None, None]
    a = (np.arange(512) // 128)[None, None, :]
    qi = (np.arange(512) % 128)[None, None, :]
    j = np.arange(9)[None, :, None]
    t = j - 1
    dt = np.where(j == 0, 8 * p + a + 1, np.where(t < 4, a - t, 8 * p + a - t))
    n = 128 * dt + qi - ki
    bucket = _t5_bucket_np(n)
    vals = rel_bias[bucket]
    vals = np.where((n >= 0)[..., None], vals, np.float32(MASK))
    return np.ascontiguousarray(np.transpose(vals, (3, 0, 1, 2)).reshape(8, 128, 9 * 512).astype(np.float32))


def _core_inputs(c, inp):
    b, p = c // 2, c % 2
    x = inp["x"][b]
    own = [2 * i + p for i in range(4)]
    oth = [2 * i + 1 - p for i in range(4)]
    halo = np.zeros((128, D), np.float32)
    for i, gb in enumerate(own):
        if gb > 0:
            halo[32 * i:32 * i + 32] = x[512 * gb - 32:512 * gb]
    xc = np.concatenate([x[512 * g:512 * g + 512] for g in own] + [halo] + [x[512 * g:512 * g + 512] for g in oth], axis=0)
    return xc


def _small_params(inp):
    f = lambda a: np.asarray(a, np.float32)
    cols = [
        f(inp["norm_pre_mix"][0]).reshape(16, 128).T,
        f(inp["norm_pre_ffn"][0]).reshape(16, 128).T,
        f(inp["conv_dw_w"][0]).T.reshape(8, 128, NTAP).transpose(1, 0, 2).reshape(128, 8 * NTAP),
        f(inp["conv_dw_b"][0]).reshape(8, 128).T,
        f(inp["conv_ln_g"][0]).reshape(8, 128).T,
        f(inp["conv_ln_b"][0]).reshape(8, 128).T,
        f(inp["subln_g"][0]).reshape(128, 1),
        np.broadcast_to(f(inp["rel_bias"])[31][None, :], (128, 8)),
        np.broadcast_to(np.concatenate([f(inp["lambda_q1"][0]), f(inp["lambda_k1"][0]),
                                        f(inp["lambda_q2"][0]), f(inp["lambda_k2"][0])])[None, :], (128, 256)),
    ]
    return np.ascontiguousarray(np.concatenate(cols, axis=1).astype(np.float32))


_NC_CACHE = {}


def kernel(**inputs):
    inp = {k: np.asarray(v) for k, v in inputs.items()}
    if "nc" not in _NC_CACHE:
        _NC_CACHE["nc"] = build_nc()
    nc = _NC_CACHE["nc"]
    small = _small_params(inp)
    gpost = np.ascontiguousarray(np.stack([np.broadcast_to(inp["norm_post_mix"][0][None, :], (128, D)),
                                           np.broadcast_to(inp["norm_post_ffn"][0][None, :], (128, D))]).astype(np.float32))
    rel = np.asarray(inp["rel_bias"], np.float32)
    bt = [_bias_tiles(rel, 0), _bias_tiles(rel, 1)]
    shared = dict(w_in=np.ascontiguousarray(inp["w_in"][0]), w_out=np.ascontiguousarray(inp["w_out"][0]),
                  w_gate=np.ascontiguousarray(inp["w_gate"][0]), w_up=np.ascontiguousarray(inp["w_up"][0]),
                  w_down=np.ascontiguousarray(inp["w_down"][0]), p_small=small, g_post=gpost)
    in_maps = []
    for c in range(N_CORES):
        m = dict(shared)
        m["x"] = _core_inputs(c, inp)
        m["bias_t"] = bt[c % 2]
        in_maps.append(m)
    res = run_bass_kernel_spmd(nc, in_maps, core_ids=list(range(N_CORES)))
    outp = np.empty((4, S_LEN, D), np.float32)
    for c in range(N_CORES):
        b, p = c // 2, c % 2
        o = res.results[c]["out"]
        for i in range(4):
            gb = 2 * i + p
            outp[b, 512 * gb:512 * gb + 512] = o[512 * i:512 * i + 512]
    return outp
```

```python
import math
import os
from contextlib import ExitStack

import numpy as np
import concourse.bass as bass
import concourse.mybir as mybir
from concourse.bass_utils import run_bass_kernel_spmd

F32 = mybir.dt.float32
BF16 = mybir.dt.bfloat16
AF = mybir.ActivationFunctionType
ALU = mybir.AluOpType

D = 2048
DC = 16
S_LEN = 4096
NH = 8
FF = 5632
FC = 44
IN_W = 5120
EPS = 1e-6
LAM_INIT = 0.2
NTAP = 31
MASK = -1e30
N_CORES = 8


class Sched:
    ENGS = ("pe", "act", "dve", "pool", "sp")

    def __init__(self):
        self.ops = []
        self.last_writer = {}
        self.readers = {}
        self.last_eng = {}
        self.open_dmas = []
        self.bg_dmas = []

    def op(self, eng, fn, reads=(), writes=(), dma=None, extra_deps=(), bg=False):
        deps = set(extra_deps)
        for k in reads:
            w = self.last_writer.get(k)
            if w is not None:
                deps.add(w)
        for k in writes:
            w = self.last_writer.get(k)
            if w is not None:
                deps.add(w)
            for r in self.readers.get(k, ()):
                deps.add(r)
        idx = len(self.ops)
        self.ops.append(dict(eng=eng, fn=fn, deps=deps, dma=dma, cons=False, sig=None))
        for k in reads:
            self.readers.setdefault(k, []).append(idx)
        for k in writes:
            self.last_writer[k] = idx
            self.readers[k] = []
        if fn is not None:
            if dma is None:
                self.last_eng[eng] = idx
            elif bg:
                self.bg_dmas.append(idx)
            else:
                self.open_dmas.append(idx)
        return idx

    def barrier(self, final=False):
        deps = set(self.last_eng.values()) | set(self.open_dmas)
        if final:
            deps |= set(self.bg_dmas)
            self.bg_dmas = []
        self.open_dmas = []
        for e in self.ENGS:
            self.op(e, None, extra_deps=deps)
        self.last_writer = {}
        self.readers = {}

    @staticmethod
    def _needs_sync(P, C):
        if P["dma"] is None and C["dma"] is None and P["eng"] == C["eng"] == "pe":
            return False
        return True

    def emit(self, nc, stack):
        ops = self.ops
        for o in ops:
            for d in o["deps"]:
                if self._needs_sync(ops[d], o):
                    ops[d]["cons"] = True
        counts = {}
        for o in ops:
            if not o["cons"]:
                continue
            name = ("e_" + o["eng"]) if o["dma"] is None else ("d_" + o["dma"])
            inc = 1 if o["dma"] is None else 16
            counts[name] = counts.get(name, 0) + inc
            o["sig"] = (name, counts[name], inc)
        per_eng = {e: [] for e in self.ENGS}
        waited = {e: {} for e in self.ENGS}
        for o in ops:
            need = {}
            for d in o["deps"]:
                P = ops[d]
                if not self._needs_sync(P, o):
                    continue
                name, val, _ = P["sig"]
                if need.get(name, 0) < val:
                    need[name] = val
            waits = []
            for name, val in need.items():
                if waited[o["eng"]].get(name, 0) < val:
                    waited[o["eng"]][name] = val
                    waits.append((name, val))
            per_eng[o["eng"]].append((waits, o["fn"], o["sig"]))
        sems = {}
        for name in counts:
            sems[name] = stack.enter_context(nc.semaphore("s_" + name))
        block = stack.enter_context(nc.Block())

        def make(engname):
            def body(eng):
                for waits, fn, sig in per_eng[engname]:
                    for name, val in waits:
                        eng.wait_ge(sems[name], val)
                    if fn is None:
                        continue
                    ins = fn(eng)
                    if sig is not None:
                        ins.then_inc(sems[sig[0]], sig[2])
            return body

        block.tensor(make("pe"))
        block.scalar(make("act"))
        block.vector(make("dve"))
        block.gpsimd(make("pool"))
        block.sync(make("sp"))
        self.counts = counts


class Arena:
    def __init__(self, tensor, nbytes):
        self.t = tensor
        self.n = nbytes
        self.off = 0

    def alloc(self, free_shape, dt):
        n = 1
        for s in free_shape:
            n *= s
        esz = 4 if dt == F32 else 2
        nb = n * esz
        a = self.off
        self.off = (a + nb + 63) // 64 * 64
        assert self.off <= self.n, ("SBUF arena overflow", self.off, self.n)
        ap = self.t[:, a // 4:(a + nb) // 4]
        if dt != F32:
            ap = ap.bitcast(dt)
        if len(free_shape) == 2:
            ap = ap.rearrange("p (a b) -> p a b", b=free_shape[1])
        elif len(free_shape) == 3:
            ap = ap.rearrange("p (a b c) -> p a b c", b=free_shape[1], c=free_shape[2])
        return ap

    def at(self, off, free_shape, dt):
        save = self.off
        self.off = off
        ap = self.alloc(free_shape, dt)
        self.off = save
        return ap

    def mark(self):
        return self.off

    def reset(self, m):
        self.off = m


def build_nc(debug=False, stop_after=None, phases="ABCDE", nheads=NH):
    nc = bass.Bass("TRN2", target_bir_lowering=False)
    skind = "ExternalOutput" if debug else "Internal"

    def din(name, shape, dt=F32):
        return nc.dram_tensor(name, shape, dt, kind="ExternalInput").ap()

    x_in = din("x", [33 * 128, D])
    w_in = din("w_in", [D, IN_W])
    w_out = din("w_out", [D, D])
    w_gate = din("w_gate", [D, FF])
    w_up = din("w_up", [D, FF])
    w_down = din("w_down", [FF, D])
    p_small = din("p_small", [128, 16 + 16 + 248 + 8 + 8 + 8 + 1 + 8 + 256])
    g_post = din("g_post", [2, 128, D])
    bias_t = din("bias_t", [NH, 128, 9 * 512])
    out = nc.dram_tensor("out", [2048, D], F32, kind="ExternalOutput").ap()

    KT = nc.dram_tensor("KT", [NH, 128, S_LEN], BF16, kind=skind).ap()
    VS = nc.dram_tensor("VS", [NH, 128, 32, 128], BF16, kind=skind).ap()
    QT = nc.dram_tensor("QT", [NH, 2, 128, 2048], BF16, kind=skind).ap()
    CS = nc.dram_tensor("CS", [8, 128, 2048], F32, kind=skind).ap()
    MT = nc.dram_tensor("MT", [16, 128, 2048], BF16, kind=skind).ap()
    H2 = nc.dram_tensor("H2", [128, DC, 2048], BF16, kind=skind).ap()
    WDB = nc.dram_tensor("WDB", [FC, 128, D], BF16, kind="Internal").ap()

    w_in_v = w_in.rearrange("(dc p) c -> p dc c", p=128)
    w_gate_v = w_gate.rearrange("(dc p) c -> p dc c", p=128)
    w_up_v = w_up.rearrange("(dc p) c -> p dc c", p=128)

    st = ExitStack()
    with st:
        ARENA_BYTES = 206 * 1024
        arena_t = st.enter_context(nc.sbuf_tensor("arena", [128, ARENA_BYTES // 4], F32))
        A = Arena(arena_t, ARENA_BYTES)
        psum_t = st.enter_context(nc.psum_tensor("psum", [128, 8 * 512], F32))

        def bank(b, n=512, off=0):
            return psum_t[:, 512 * b + off:512 * b + off + n]

        S = Sched()

        identf = A.alloc([128], F32)
        ident = A.alloc([128], BF16)
        ones_bf = A.alloc([128], BF16)
        ones_f = A.alloc([128], F32)
        small = A.alloc([16 + 16 + 248 + 8 + 8 + 8 + 1 + 8 + 256], F32)
        gBm = A.alloc([DC, 128], F32)
        gBf = A.alloc([DC, 128], F32)
        misc = A.alloc([16], F32)
        o0 = 0
        gpm = small[:, 0:16]
        gpf = small[:, 16:32]
        cw = small[:, 32:280].rearrange("p (c j) -> p c j", j=NTAP)
        cb = small[:, 280:288]
        lng = small[:, 288:296]
        lnb = small[:, 296:304]
        subg = small[:, 304:305]
        relc = small[:, 305:313]
        lamp = small[:, 313:569].rearrange("p (k d) -> p k d", d=64)
        neglam = misc[:, 0:1]
        negc = misc[:, 1:9]
        subg08 = misc[:, 9:10]
        lam_t = misc[:, 10:14]

        S.op("sp", lambda e: e.dma_start(out=small, in_=p_small), writes=["small"], dma="small")
        S.op("pool", lambda e: e.memset(identf, 0.0), writes=["identf"])
        S.op("pool", lambda e: e.affine_select(out=identf, in_=identf, pattern=[[-1, 128]],
                                               compare_op=ALU.not_equal, fill=1.0, base=0,
                                               channel_multiplier=1),
             reads=["identf"], writes=["identf"])
        S.op("dve", lambda e: e.tensor_copy(out=ident, in_=identf), reads=["identf"], writes=["ident"])
        S.op("dve", lambda e: e.memset(ones_bf, 1.0), writes=["ones_bf"])
        S.op("dve", lambda e: e.memset(ones_f, 1.0), writes=["ones_f"])
        for dc in range(DC):
            S.op("dve", (lambda dc: lambda e: e.tensor_scalar(out=gBm[:, dc, :], in0=ones_f, scalar1=gpm[:, dc:dc + 1],
                                                               scalar2=None, op0=ALU.mult))(dc),
                 reads=["small", "ones_f"], writes=[("gBm", dc)])
            S.op("dve", (lambda dc: lambda e: e.tensor_scalar(out=gBf[:, dc, :], in0=ones_f, scalar1=gpf[:, dc:dc + 1],
                                                               scalar2=None, op0=ALU.mult))(dc),
                 reads=["small", "ones_f"], writes=[("gBf", dc)])
        gBm_keys = [("gBm", dc) for dc in range(DC)]
        gBf_keys = [("gBf", dc) for dc in range(DC)]
        lj = A.alloc([2, 64], F32)
        S.op("dve", lambda e: e.tensor_tensor(out=lj[:, 0, :], in0=lamp[:, 0, :], in1=lamp[:, 1, :], op=ALU.mult),
             reads=["small"], writes=["lj0"])
        S.op("dve", lambda e: e.tensor_tensor(out=lj[:, 1, :], in0=lamp[:, 2, :], in1=lamp[:, 3, :], op=ALU.mult),
             reads=["small"], writes=["lj1"])
        S.op("dve", lambda e: e.reduce_sum(out=lam_t[:, 0:1], in_=lj[:, 0, :], axis=mybir.AxisListType.X),
             reads=["lj0"], writes=["lam0"])
        S.op("dve", lambda e: e.reduce_sum(out=lam_t[:, 1:2], in_=lj[:, 1, :], axis=mybir.AxisListType.X),
             reads=["lj1"], writes=["lam1"])
        S.op("act", lambda e: e.activation(out=lam_t[:, 2:4], in_=lam_t[:, 0:2], func=AF.Exp),
             reads=["lam0", "lam1"], writes=["lam2"])
        S.op("dve", lambda e: e.tensor_tensor(out=lam_t[:, 0:1], in0=lam_t[:, 3:4], in1=lam_t[:, 2:3], op=ALU.subtract),
             reads=["lam2"], writes=["lam3"])
        S.op("dve", lambda e: e.tensor_scalar(out=neglam, in0=lam_t[:, 0:1], scalar1=-LAM_INIT, scalar2=None, op0=ALU.add),
             reads=["lam3"], writes=["neglam"])
        S.op("dve", lambda e: e.tensor_scalar(out=negc, in0=relc, scalar1=-1.0, scalar2=None, op0=ALU.mult),
             reads=["small"], writes=["negc"])
        S.op("dve", lambda e: e.tensor_scalar(out=subg08, in0=subg, scalar1=1.0 - LAM_INIT, scalar2=None, op0=ALU.mult),
             reads=["small"], writes=["subg08"])
        base_mark = A.mark()

        def rstd_ops(tag, ss_ap, n, rs_ap, tmp_ap, ss_key, rs_key):
            S.op("dve", lambda e: e.tensor_scalar(out=tmp_ap, in0=ss_ap, scalar1=1.0 / n, scalar2=EPS,
                                                  op0=ALU.mult, op1=ALU.add),
                 reads=[ss_key], writes=[(tag, "ms")])
            S.op("act", lambda e: e.activation(out=tmp_ap, in_=tmp_ap, func=AF.Ln),
                 reads=[(tag, "ms")], writes=[(tag, "sq")])
            S.op("act", lambda e: e.activation(out=rs_ap, in_=tmp_ap, func=AF.Exp, scale=-0.5), reads=[(tag, "sq")], writes=[rs_key])

        def build_hT(tiles, hT, gB, gB_keys, xt, xn, sqj, stat, pt_banks, hook=None):
            ns = len(xt)
            nt = len(tiles)

            def st1(j):
                tt = tiles[j]
                s = j % ns
                S.op("sp", (lambda s, tt: lambda e: e.dma_start(out=xt[s], in_=x_in[128 * tt:128 * tt + 128, :]))(s, tt),
                     writes=[("xt", s)], dma="xt%d" % s)
                S.op("act", (lambda s: lambda e: e.activation(out=sqj, in_=xt[s], func=AF.Square,
                                                             accum_out=stat[:, 4 * s:4 * s + 1]))(s),
                     reads=[("xt", s)], writes=["sqj", ("ss", s)])
                rstd_ops(("h", s), stat[:, 4 * s:4 * s + 1], D, stat[:, 4 * s + 2:4 * s + 3], stat[:, 4 * s + 1:4 * s + 2],
                         ("ss", s), ("rs", s))

            def st2(j):
                s = j % ns
                S.op("dve", (lambda s: lambda e: e.tensor_scalar(out=xn[s], in0=xt[s], scalar1=stat[:, 4 * s + 2:4 * s + 3],
                                                                scalar2=None, op0=ALU.mult))(s),
                     reads=[("xt", s), ("rs", s)], writes=[("xn", s)])
                pb = pt_banks[s]
                pT = psum_t[:, 512 * pb:512 * pb + 1024].bitcast(BF16).rearrange("p (a b) -> p a b", b=128)

                def tr(e, s=s, pT=pT):
                    for dc in range(DC):
                        ins = e.transpose(pT[:, dc, :], xn[s][:, 128 * dc:128 * dc + 128], ident)
                    return ins
                S.op("pe", tr, reads=[("xn", s), "ident"], writes=[("ps", pb), ("ps", pb + 1)])

            def st3(j):
                s = j % ns
                pb = pt_banks[s]
                pT = psum_t[:, 512 * pb:512 * pb + 1024].bitcast(BF16).rearrange("p (a b) -> p a b", b=128)
                S.op("dve", (lambda j, pT: lambda e: e.tensor_tensor(out=hT[:, :, 128 * j:128 * j + 128], in0=pT, in1=gB,
                                                                     op=ALU.mult))(j, pT),
                     reads=[("ps", pb), ("ps", pb + 1)] + gB_keys, writes=[("hT", j)])

            for k in range(nt + 3):
                if k < nt:
                    if hook is not None:
                        hook(k)
                    st1(k)
                if 0 <= k - 1 < nt:
                    st2(k - 1)
                if 0 <= k - 2 < nt:
                    st3(k - 2)

        def load_wchunk(dst, src_view, c0, ncol, key, slot):
            S.op("pool", lambda e: e.dma_start(out=dst, in_=src_view[:, :, c0:c0 + ncol]),
                 writes=[(key, slot)], dma="%s%d" % (key, slot))

        def proj_fm(wt, wkey, hT, hkeys, col0, ps_b):
            def fn(e):
                for dc in range(DC):
                    ins = e.matmul(bank(ps_b), lhsT=wt[:, dc, :], rhs=hT[:, dc, col0:col0 + 512],
                                   start=(dc == 0), stop=(dc == DC - 1))
                return ins
            S.op("pe", fn, reads=[wkey] + hkeys, writes=[("ps", ps_b)])

        wdb_ops = []

        def precast_wd(n):
            for _ in range(n):
                fc = len(wdb_ops)
                if fc >= FC:
                    return
                wdb_ops.append(S.op("pool", (lambda fc: lambda e: e.dma_start(out=WDB[fc], in_=w_down[128 * fc:128 * fc + 128, :]))(fc),
                                    dma="wdb%d" % (fc % 4), bg=True))

        def build_bufs():
            xt = [A.alloc([D], F32) for _ in range(4)]
            xn = [A.alloc([D], BF16) for _ in range(4)]
            sqj = A.alloc([D], BF16)
            stat = A.alloc([16], F32)
            return xt, xn, sqj, stat

        def phase_proj(own, prebuilt=False):
            A.reset(base_mark)
            ntile = 17 if own else 16
            tiles = list(range(0, 17)) if own else list(range(17, 33))
            hT = A.alloc([DC, ntile * 128], BF16)
            m1 = A.mark()
            if not prebuilt:
                xt, xn, sqj, stat = build_bufs()
                build_hT(tiles, hT, gBm, gBm_keys, xt, xn, sqj, stat, [0, 2, 4, 6])
                S.barrier()
            A.reset(m1)
            wc = [A.alloc([DC, 128], BF16) for _ in range(4)]
            wv = [A.alloc([DC, 256], BF16) for _ in range(2)]
            stg = [A.alloc([2048], BF16) for _ in range(2)]
            qz = [A.alloc([2, 2048], BF16) for _ in range(2)] if own else None
            if own:
                for q_ in range(2):
                    S.op("pool", (lambda q_: lambda e: e.memset(qz[q_], 0.0))(q_), writes=[("qz", q_, b_) for b_ in range(4)])
            vstg = [A.alloc([2, 16, 128], BF16) for _ in range(2)]
            hk = lambda blk: [("hT", 4 * blk + t) for t in range(4)]
            pair_half = 0 if own else 1
            chunks = []
            if own:
                chunks += [("q", h) for h in range(NH)]
            chunks += [("k", h) for h in range(NH)]
            pb_rot = 0
            for ci, (kind, h) in enumerate(chunks):
                ws = ci % 4
                col = (0 if kind == "q" else 1024) + 128 * h
                load_wchunk(wc[ws], w_in_v, col, 128, "wc", ws)
                ss_ = ci % 2
                for blk in range(4):
                    pb = 4 + (pb_rot % 4)
                    pb_rot += 1
                    proj_fm(wc[ws], ("wc", ws), hT, hk(blk), 512 * blk, pb)
                    if kind == "q":
                        S.op("act", (lambda ss_, blk, pb: lambda e: e.activation(
                            out=qz[ss_][0:64, 0, 512 * blk:512 * blk + 512], in_=bank(pb)[0:64, :], func=AF.Copy, scale=0.125))(ss_, blk, pb),
                            reads=[("ps", pb), ("qz", ss_, blk)], writes=[("qza", ss_, blk)])
                        S.op("dve", (lambda ss_, blk, pb: lambda e: e.tensor_scalar(
                            out=qz[ss_][64:128, 1, 512 * blk:512 * blk + 512], in0=bank(pb)[64:128, :], scalar1=0.125, scalar2=None,
                            op0=ALU.mult))(ss_, blk, pb),
                            reads=[("ps", pb), ("qz", ss_, blk), ("qza", ss_, blk)], writes=[("qzb", ss_, blk)])
                    else:
                        eng = "act" if blk % 2 == 0 else "dve"
                        if eng == "act":
                            S.op("act", (lambda ss_, blk, pb: lambda e: e.activation(
                                out=stg[ss_][:, 512 * blk:512 * blk + 512], in_=bank(pb), func=AF.Copy))(ss_, blk, pb),
                                reads=[("ps", pb)], writes=[("stg", ss_, blk)])
                        else:
                            S.op("dve", (lambda ss_, blk, pb: lambda e: e.tensor_copy(
                                out=stg[ss_][:, 512 * blk:512 * blk + 512], in_=bank(pb)))(ss_, blk, pb),
                                reads=[("ps", pb)], writes=[("stg", ss_, blk)])
                if kind == "q":
                    S.op("sp", (lambda h, ss_: lambda e: e.dma_start(out=QT[h].rearrange("c p t -> p c t"), in_=qz[ss_]))(h, ss_),
                         reads=[(k_, ss_, b_) for b_ in range(4) for k_ in ("qza", "qzb")],
                         writes=[("scr", kind, h)] + [(k_, ss_, b_) for b_ in range(4) for k_ in ("qza", "qzb")], dma="qzo%d" % ss_)
                    continue
                else:
                    dst = KT[h].rearrange("p (i two c) -> p i two c", two=2, c=512)[:, :, pair_half, :]
                    src = stg[ss_].rearrange("p (i c) -> p i c", c=512)
                S.op("sp", (lambda dst, src: lambda e: e.dma_start(out=dst, in_=src))(dst, src),
                     reads=[("stg", ss_, b_) for b_ in range(4)], writes=[("scr", kind, h)], dma="stgo%d" % ss_)
            for vg in range(4):
                ws = vg % 2
                load_wchunk(wv[ws], w_in_v, 2048 + 256 * vg, 256, "wv", ws)
                for t in range(16):
                    pb = 4 + (pb_rot % 4)
                    pb_rot += 1

                    def fn(e, t=t, pb=pb, ws=ws):
                        for dc in range(DC):
                            ins = e.matmul(bank(pb, 256), lhsT=hT[:, dc, 128 * t:128 * t + 128], rhs=wv[ws][:, dc, :],
                                           start=(dc == 0), stop=(dc == DC - 1))
                        return ins
                    S.op("pe", fn, reads=[("wv", ws), ("hT", t)], writes=[("ps", pb)])
                    src = bank(pb, 256).rearrange("p (a b) -> p a b", b=128)
                    if t % 2 == 0:
                        S.op("act", (lambda ws, t, src: lambda e: e.activation(out=vstg[ws][:, :, t, :], in_=src, func=AF.Copy))(ws, t, src),
                             reads=[("ps", pb)], writes=[("vstg", ws, t)])
                    else:
                        S.op("dve", (lambda ws, t, src: lambda e: e.tensor_copy(out=vstg[ws][:, :, t, :], in_=src))(ws, t, src),
                             reads=[("ps", pb)], writes=[("vstg", ws, t)])
                for hh in range(2):
                    h = 2 * vg + hh
                    dst = VS[h].rearrange("p (i j) e -> p i j e", j=8)[:, :, 4 * pair_half:4 * pair_half + 4, :]
                    src = vstg[ws][:, hh, :, :].rearrange("p (i t) e -> p i t e", t=4)
                    S.op("sp", (lambda dst, src: lambda e: e.dma_start(out=dst, in_=src))(dst, src),
                         reads=[("vstg", ws, t) for t in range(16)], writes=[("scr", "v", h)], dma="vstgo%d_%d" % (ws, hh))
            if not own:
                S.barrier()
                return
            S.barrier()
            A.reset(m1)
            NDVE = 22
            NPE = NTAP - NDVE
            wc = [A.alloc([DC, 128], BF16) for _ in range(4)]
            hg = [A.alloc([4, 544], F32) for _ in range(2)]
            sga = A.alloc([4, 544], F32)
            accA = A.alloc([4, 512], F32)
            off_hgh = A.mark()
            hgh = [A.alloc([4, 544], BF16) for _ in range(2)]
            hgl = [A.alloc([4, 544], BF16) for _ in range(2)]
            off_dw = A.mark()
            dwh = [A.alloc([NPE, 128], BF16) for _ in range(2)]
            dwl = [A.alloc([NPE, 128], BF16) for _ in range(2)]
            cres = [A.alloc([2048], F32) for _ in range(2)]
            sqt = A.alloc([2048], F32)
            sumacc = A.alloc([2048], F32)
            sqacc = A.alloc([2048], F32)

            def fnh(e, wt, pb):
                for dc in range(DC):
                    ins = e.matmul(bank(pb, 128), lhsT=wt[:, dc, :], rhs=hT[:, dc, 2048:2176],
                                   start=(dc == 0), stop=(dc == DC - 1))
                return ins

            def stage_u(cc):
                wa, wg_ = wc[(2 * cc) % 4], wc[(2 * cc + 1) % 4]
                ka, kg = ("wc", (2 * cc) % 4), ("wc", (2 * cc + 1) % 4)
                load_wchunk(wa, w_in_v, 3072 + 128 * cc, 128, "wc", (2 * cc) % 4)
                load_wchunk(wg_, w_in_v, 4096 + 128 * cc, 128, "wc", (2 * cc + 1) % 4)
                precast_wd(6)
                hs = cc % 2
                for blk in range(4):
                    pa, pg = 4 + 2 * (blk % 2), 5 + 2 * (blk % 2)
                    proj_fm(wa, ka, hT, hk(blk), 512 * blk, pa)
                    proj_fm(wg_, kg, hT, hk(blk), 512 * blk, pg)
                    S.op("act", (lambda blk, pg: lambda e: e.activation(out=sga[:, blk, 32:544], in_=bank(pg), func=AF.Sigmoid))(blk, pg),
                         reads=[("ps", pg)], writes=[("sga", blk)])
                    S.op("act", (lambda blk, pa, hs: lambda e: e.activation(out=hg[hs][:, blk, 32:544], in_=bank(pa), func=AF.Copy))(blk, pa, hs),
                         reads=[("ps", pa)], writes=[("hga", hs, blk)])
                S.op("pe", (lambda wa: lambda e: fnh(e, wa, 4))(wa), reads=[ka, ("hT", 16)], writes=[("ps", 4)])
                S.op("pe", (lambda wg_: lambda e: fnh(e, wg_, 5))(wg_), reads=[kg, ("hT", 16)], writes=[("ps", 5)])
                S.op("act", lambda e: e.activation(out=sga[:, :, 0:32], in_=bank(5, 128).rearrange("p (a b) -> p a b", b=32), func=AF.Sigmoid),
                     reads=[("ps", 5)], writes=[("sga", "halo")])
                S.op("act", (lambda hs: lambda e: e.activation(out=hg[hs][:, :, 0:32], in_=bank(4, 128).rearrange("p (a b) -> p a b", b=32),
                                                               func=AF.Copy))(hs),
                     reads=[("ps", 4)], writes=[("hga", hs, "halo")])

            def stage_g(cc):
                hs = cc % 2
                hak = [("hga", hs, blk) for blk in range(4)] + [("hga", hs, "halo")]
                sgk = [("sga", blk) for blk in range(4)] + [("sga", "halo")]
                S.op("dve", (lambda hs: lambda e: e.tensor_tensor(out=hg[hs], in0=hg[hs], in1=sga, op=ALU.mult))(hs),
                     reads=hak + sgk, writes=[("hg", hs)] + hak)
                S.op("act", (lambda hs: lambda e: e.activation(out=hgh[hs], in_=hg[hs], func=AF.Copy))(hs), reads=[("hg", hs)], writes=[("hgh", hs)])
                S.op("dve", (lambda hs: lambda e: e.tensor_tensor(out=hgl[hs], in0=hg[hs], in1=hgh[hs], op=ALU.subtract))(hs),
                     reads=[("hg", hs), ("hgh", hs)], writes=[("hgl", hs)])
                for jj in range(NPE):
                    j = NDVE + jj
                    wj = cw[:, cc, j:j + 1]
                    S.op("act", (lambda hs, jj, wj: lambda e: e.activation(out=dwh[hs][:, jj, :], in_=identf, func=AF.Copy, scale=wj))(hs, jj, wj),
                         reads=["identf", "small"], writes=[("dwh", hs, jj)])
                    S.op("dve", (lambda hs, jj, wj: lambda e: e.scalar_tensor_tensor(out=dwl[hs][:, jj, :], in0=identf, scalar=wj, in1=dwh[hs][:, jj, :],
                                                                                      op0=ALU.mult, op1=ALU.subtract))(hs, jj, wj),
                         reads=["identf", "small", ("dwh", hs, jj)], writes=[("dwl", hs, jj)])

            def stage_t(cc):
                hs = cc % 2
                cs = cc % 2
                for blk in range(4):
                    def pconv(e, hs=hs, blk=blk):
                        n = 0
                        for jj in range(NPE):
                            j = NDVE + jj
                            for (wt, ht) in ((dwh, hgh), (dwh, hgl), (dwl, hgh)):
                                ins = e.matmul(bank(blk), lhsT=wt[hs][:, jj, :], rhs=ht[hs][:, blk, 2 + j:2 + j + 512],
                                               start=(n == 0), stop=(n == 3 * NPE - 1))
                                n += 1
                        return ins
                    S.op("pe", pconv, reads=[("hgh", hs), ("hgl", hs)] + [(k_, hs, jj) for jj in range(NPE) for k_ in ("dwh", "dwl")],
                         writes=[("ps", blk)])
                for j in range(NDVE):
                    src = hg[hs][:, :, 2 + j:2 + j + 512]
                    wj = cw[:, cc, j:j + 1]
                    if j == 0:
                        S.op("dve", (lambda src, wj, cc: lambda e: e.tensor_scalar(out=accA, in0=src, scalar1=wj, scalar2=cb[:, cc:cc + 1],
                                                                                   op0=ALU.mult, op1=ALU.add))(src, wj, cc),
                             reads=[("hg", hs), "small"], writes=["accA"])
                    else:
                        S.op("dve", (lambda src, wj: lambda e: e.scalar_tensor_tensor(out=accA, in0=src, scalar=wj, in1=accA,
                                                                                      op0=ALU.mult, op1=ALU.add))(src, wj),
                             reads=[("hg", hs), "small", "accA"], writes=["accA"])
                S.op("dve", (lambda cs: lambda e: e.tensor_tensor(out=cres[cs], in0=accA.rearrange("p a b -> p (a b)"), in1=psum_t[:, 0:2048], op=ALU.add))(cs),
                     reads=["accA"] + [("ps", b_) for b_ in range(4)], writes=[("cres", cs)])
                S.op("sp", (lambda cs, cc: lambda e: e.dma_start(out=CS[cc], in_=cres[cs]))(cs, cc),
                     reads=[("cres", cs)], writes=[("scr", "cs", cc)], dma="cres%d" % cs)
                S.op("act", (lambda cs: lambda e: e.activation(out=sqt, in_=cres[cs], func=AF.Square))(cs),
                     reads=[("cres", cs)], writes=["sqt"])

            def stage_s(cc):
                cs = cc % 2
                for blk in range(4):
                    for which, srcb, acc, key in ((0, cres[cs], sumacc, ("cres", cs)), (1, sqt, sqacc, "sqt")):
                        pb = 6 + which
                        S.op("pe", (lambda srcb, blk, pb: lambda e: e.matmul(bank(pb), lhsT=ones_f, rhs=srcb[:, 512 * blk:512 * blk + 512],
                                                                           start=True, stop=True))(srcb, blk, pb),
                             reads=[key, "ones_f"], writes=[("ps", pb)])
                        if cc == 0:
                            S.op("dve", (lambda acc, blk, pb: lambda e: e.tensor_copy(out=acc[:, 512 * blk:512 * blk + 512], in_=bank(pb)))(acc, blk, pb),
                                 reads=[("ps", pb)], writes=[("acc", which, blk)])
                        else:
                            S.op("dve", (lambda acc, blk, pb: lambda e: e.tensor_tensor(out=acc[:, 512 * blk:512 * blk + 512],
                                                                                       in0=acc[:, 512 * blk:512 * blk + 512], in1=bank(pb), op=ALU.add))(acc, blk, pb),
                                 reads=[("ps", pb), ("acc", which, blk)], writes=[("acc", which, blk)])

            stage_u(0)
            stage_g(0)
            for cc in range(8):
                if cc + 1 < 8:
                    stage_u(cc + 1)
                stage_t(cc)
                if cc + 1 < 8:
                    stage_g(cc + 1)
                stage_s(cc)
            S.barrier()
            acck = [("acc", w_, b_) for w_ in range(2) for b_ in range(4)]
            mean = sumacc
            rstd = sqacc
            m2 = sqt
            S.op("dve", lambda e: e.tensor_scalar(out=mean, in0=sumacc, scalar1=1.0 / 1024, scalar2=None, op0=ALU.mult),
                 reads=acck, writes=["mean"])
            S.op("dve", lambda e: e.tensor_tensor(out=m2, in0=mean, in1=mean, op=ALU.mult), reads=["mean"], writes=["m2", "sqt"])
            S.op("dve", lambda e: e.scalar_tensor_tensor(out=rstd, in0=sqacc, scalar=1.0 / 1024, in1=m2, op0=ALU.mult, op1=ALU.subtract),
                 reads=acck + ["m2"], writes=["var"])
            S.op("dve", lambda e: e.tensor_scalar(out=rstd, in0=rstd, scalar1=EPS, scalar2=None, op0=ALU.add), reads=["var"], writes=["var2"])
            S.op("act", lambda e: e.activation(out=rstd, in_=rstd, func=AF.Sqrt), reads=["var2"], writes=["sd"])
            S.op("dve", lambda e: e.reciprocal(out=rstd, in_=rstd), reads=["sd"], writes=["rstd"])
            cl = cres
            tmp = [A.at(off_hgh, [2048], F32), A.at(off_hgh + 8192, [2048], F32)]
            mst = [A.at(off_dw, [2048], BF16), A.at(off_dw + 4096, [2048], BF16)]
            A.reset(base_mark)
            hT_b = A.alloc([DC, 16 * 128], BF16)
            xt_b, xn_b, sqj_b, stat_b = build_bufs()
            assert A.mark() <= off_hgh, (A.mark(), off_hgh)

            def ln_chunk(cc):
                s = cc % 2
                S.op("pool", (lambda s, cc: lambda e: e.dma_start(out=cl[s], in_=CS[cc]))(s, cc),
                     reads=[("scr", "cs", cc)], writes=[("cres", s)], dma="cl%d" % s)
                S.op("dve", (lambda s: lambda e: e.tensor_tensor(out=tmp[s], in0=cl[s], in1=mean, op=ALU.subtract))(s),
                     reads=[("cres", s), "mean"], writes=[("tmp", s)])
                S.op("dve", (lambda s: lambda e: e.tensor_tensor(out=tmp[s], in0=tmp[s], in1=rstd, op=ALU.mult))(s),
                     reads=[("tmp", s), "rstd"], writes=[("tmp", s)])
                S.op("act", (lambda s, cc: lambda e: e.activation(out=mst[s], in_=tmp[s], func=AF.Silu, bias=lnb[:, cc:cc + 1],
                                                                scale=lng[:, cc:cc + 1]))(s, cc),
                     reads=[("tmp", s), "small"], writes=[("mst", s)])
                S.op("act", (lambda s, cc: lambda e: e.dma_start(out=MT[8 + cc], in_=mst[s]))(s, cc),
                     reads=[("mst", s)], writes=[("scr", "mt", 8 + cc)], dma="mst%d" % s)

            build_hT(list(range(17, 33)), hT_b, gBm, gBm_keys, xt_b, xn_b, sqj_b, stat_b, [0, 2, 4, 6],
                     hook=lambda j: ln_chunk(j - 1) if 1 <= j <= 8 else None)
            S.barrier()

        if "A" in phases:
            phase_proj(True)
        if stop_after != "A" and "B" in phases:
            phase_proj(False, prebuilt=("A" in phases))

        def phase_attn():
            A.reset(base_mark)
            wo = A.alloc([DC, D], BF16)
            kt = [A.alloc([S_LEN], BF16) for _ in range(2)]
            vv = [A.alloc([32, 128], BF16) for _ in range(2)]
            qt = [A.alloc([2, 2048], BF16) for _ in range(2)]
            ee = [A.alloc([9 * 512], BF16) for _ in range(2)]
            bb = [A.alloc([9 * 512], F32)] * 2
            pbuf = [A.alloc([2, 512], BF16) for _ in range(3)]
            r0 = A.alloc([512], F32)
            r1 = A.alloc([512], F32)
            t0 = A.alloc([512], F32)
            t1 = A.alloc([512], F32)
            osq = A.alloc([512], F32)
            cO = [A.alloc([512], F32) for _ in range(2)]
            cL = [A.alloc([512], F32) for _ in range(2)]
            ohi = A.alloc([512], BF16)
            olo = A.alloc([512], BF16)
            ostg = [A.alloc([2048], BF16) for _ in range(2)]
            OB = [4, 5]
            LB = [6, 7]

            def head_loads(h):
                s = h % 2
                S.op("sp", (lambda s, h: lambda e: e.dma_start(out=kt[s], in_=KT[h]))(s, h), writes=[("kt", s)], dma="kt%d" % s)
                S.op("sp", (lambda s, h: lambda e: e.dma_start(out=vv[s], in_=VS[h]))(s, h), writes=[("vv", s)], dma="vv%d" % s)
                S.op("sp", (lambda s, h: lambda e: e.dma_start(out=qt[s], in_=QT[h].rearrange("c p t -> p c t")))(s, h), writes=[("qt", s)], dma="qt%d" % s)
                S.op("sp", (lambda s, h: lambda e: e.dma_start(out=bb[s], in_=bias_t[h]))(s, h), writes=[("bb", 0)], dma="bb0")

            def head_ee(h):
                s = h % 2
                S.op("act", (lambda s, h: lambda e: e.activation(out=ee[s], in_=bb[s], func=AF.Exp, bias=negc[:, h:h + 1]))(s, h),
                     reads=[("bb", 0), "negc"], writes=[("ee", s)])

            def group_units(i):
                units = []
                for pr in range(i):
                    for t in range(8):
                        if pr == i - 1 and t == 7:
                            continue
                        units.append((8 * pr + t, None))
                if i >= 1:
                    units.append((8 * (i - 1) + 7, 0))
                for t in range(8):
                    units.append((8 * i + t, 1 + t))
                return units

            st_ = dict(slot=0, pu=0)

            def s_op(h, i, kp):
                s = h % 2
                sl = st_["slot"] % 2
                st_["slot"] += 1

                def fn(e):
                    e.matmul(bank(2 * sl), lhsT=kt[s][:, 128 * kp:128 * kp + 128], rhs=qt[s][:, 0, 512 * i:512 * i + 512], start=True, stop=True)
                    return e.matmul(bank(2 * sl + 1), lhsT=kt[s][:, 128 * kp:128 * kp + 128], rhs=qt[s][:, 1, 512 * i:512 * i + 512],
                                    start=True, stop=True)
                S.op("pe", fn, reads=[("kt", s), ("qt", s)], writes=[("ps", 2 * sl), ("ps", 2 * sl + 1)])
                return sl

            groups = [(h, i) for h in range(nheads) for i in range(4)]
            for c in range(DC):
                S.op("pool", (lambda c: lambda e: e.dma_start(out=wo[:, c, :], in_=w_out[128 * c:128 * c + 128, :]))(c),
                     writes=[("wo", c)], dma="wo%d" % (c % 4))
            head_loads(0)
            head_ee(0)
            pre_slot = None
            pending = [None]
            pending15 = [None]
            for gi, (h, i) in enumerate(groups):
                s = h % 2
                if i == 0 and h + 1 < nheads:
                    head_loads(h + 1)
                units = group_units(i)
                nu = len(units)
                cur = pre_slot if pre_slot is not None else s_op(h, i, units[0][0])
                for u in range(nu):
                    kp, nj = units[u]
                    nxt = s_op(h, i, units[u + 1][0]) if u + 1 < nu else None
                    pk = st_["pu"] % 3
                    st_["pu"] += 1
                    pb_ = pbuf[pk]
                    if nj is None:
                        src = psum_t[:, 1024 * cur:1024 * cur + 1024].rearrange("p (c q) -> p c q", q=512)
                        S.op("act", (lambda pb_, src: lambda e: e.activation(out=pb_, in_=src, func=AF.Exp))(pb_, src),
                             reads=[("ps", 2 * cur), ("ps", 2 * cur + 1)], writes=[("pb", pk, 0), ("pb", pk, 1)])
                    else:
                        for c in range(2):
                            S.op("act", (lambda pb_, c, cur: lambda e: e.activation(out=pb_[:, c, :], in_=bank(2 * cur + c), func=AF.Exp))(pb_, c, cur),
                                 reads=[("ps", 2 * cur + c)], writes=[("pb", pk, c)])
                            S.op("dve", (lambda pb_, nj, s, c: lambda e: e.tensor_tensor(out=pb_[:, c, :], in0=pb_[:, c, :],
                                                                                      in1=ee[s][:, 512 * nj:512 * nj + 512], op=ALU.mult))(pb_, nj, s, c),
                                 reads=[("pb", pk, c), ("ee", s)], writes=[("pb", pk, c)])
                    for c in range(2):
                        def pv(e, pb_=pb_, kp=kp, u=u, s=s, nu=nu, c=c):
                            e.matmul(bank(OB[c]), lhsT=vv[s][:, kp, :], rhs=pb_[:, c, :], start=(u == 0), stop=(u == nu - 1))
                            return e.matmul(bank(LB[c]), lhsT=ones_bf, rhs=pb_[:, c, :], start=(u == 0), stop=(u == nu - 1))
                        S.op("pe", pv, reads=[("vv", s), ("pb", pk, c), "ones_bf"], writes=[("ps", OB[c]), ("ps", LB[c])])
                    cur = nxt
                    if u == 2 and pending15[0] is not None:
                        pending15[0]()
                        pending15[0] = None
                    if u == min(nu - 2, 10) and pending[0] is not None:
                        pending[0]()
                        pending[0] = None
                if i == 0 and h + 1 < nheads:
                    head_ee(h + 1)
                if gi + 1 < len(groups):
                    nh_, ni_ = groups[gi + 1]
                    pre_slot = s_op(nh_, ni_, group_units(ni_)[0][0])
                else:
                    pre_slot = None
                S.op("act", lambda e: e.activation(out=cO[0], in_=bank(OB[0]), func=AF.Copy), reads=[("ps", OB[0])], writes=["cO0"])
                S.op("dve", lambda e: e.tensor_copy(out=cL[0], in_=bank(LB[0])), reads=[("ps", LB[0])], writes=["cL0"])
                S.op("act", lambda e: e.activation(out=cO[1], in_=bank(OB[1]), func=AF.Copy), reads=[("ps", OB[1])], writes=["cO1"])
                S.op("dve", lambda e: e.tensor_copy(out=cL[1], in_=bank(LB[1])), reads=[("ps", LB[1])], writes=["cL1"])
                def part15():
                  S.op("act", lambda e: e.activation(out=r0, in_=cL[0], func=AF.Ln), reads=["cL0"], writes=["r0"])
                  S.op("act", lambda e: e.activation(out=r0, in_=r0, func=AF.Exp, scale=-1.0), reads=["r0"], writes=["r0"])
                  S.op("act", lambda e: e.activation(out=r1, in_=cL[1], func=AF.Ln), reads=["cL1"], writes=["r1"])
                  S.op("act", lambda e: e.activation(out=r1, in_=r1, func=AF.Exp, scale=-1.0), reads=["r1"], writes=["r1"])
                  S.op("dve", lambda e: e.tensor_tensor(out=t0, in0=cO[0], in1=r0, op=ALU.mult), reads=["cO0", "r0"], writes=["t0"])
                  S.op("dve", lambda e: e.tensor_tensor(out=t1, in0=cO[1], in1=r1, op=ALU.mult), reads=["cO1", "r1"], writes=["t1"])
                  S.op("dve", lambda e: e.scalar_tensor_tensor(out=t0, in0=t1, scalar=neglam, in1=t0, op0=ALU.mult, op1=ALU.add),
                       reads=["t0", "t1", "neglam"], writes=["t0"])
                  S.op("dve", lambda e: e.tensor_tensor(out=osq, in0=t0, in1=t0, op=ALU.mult), reads=["t0"], writes=["osq"])
                  S.op("dve", lambda e: e.tensor_copy(out=ohi, in_=osq), reads=["osq"], writes=["ohi"])
                  S.op("dve", lambda e: e.tensor_tensor(out=olo, in0=osq, in1=ohi, op=ALU.subtract), reads=["osq", "ohi"], writes=["olo"])


                def part2(s=s, i=i, h=h):
                    NBk = 2 * (st_["slot"] % 2)

                    def nmm(e, NBk=NBk):
                        e.matmul(bank(NBk), lhsT=ones_bf, rhs=ohi, start=True, stop=False)
                        return e.matmul(bank(NBk), lhsT=ones_bf, rhs=olo, start=False, stop=True)
                    S.op("pe", nmm, reads=["ohi", "olo", "ones_bf"], writes=[("ps", NBk)])
                    S.op("dve", (lambda NBk: lambda e: e.tensor_scalar(out=r0, in0=bank(NBk), scalar1=1.0 / 128, scalar2=EPS, op0=ALU.mult, op1=ALU.add))(NBk),
                         reads=[("ps", NBk)], writes=["r0"])
                    S.op("act", lambda e: e.activation(out=r0, in_=r0, func=AF.Ln), reads=["r0"], writes=["r0"])
                    S.op("act", lambda e: e.activation(out=r0, in_=r0, func=AF.Exp, scale=-0.5), reads=["r0"], writes=["r0"])
                    S.op("dve", (lambda s, i: lambda e: e.scalar_tensor_tensor(out=ostg[s][:, 512 * i:512 * i + 512], in0=t0, scalar=subg08, in1=r0,
                                                                              op0=ALU.mult, op1=ALU.mult))(s, i),
                         reads=["t0", "r0", "subg08"], writes=[("ostg", s, i)])
                    if i == 3:
                        S.op("sp", (lambda s, h: lambda e: e.dma_start(out=MT[h], in_=ostg[s]))(s, h),
                             reads=[("ostg", s, i_) for i_ in range(4)], writes=[("scr", "mt", h)], dma="ostg%d" % s)
                if gi + 1 < len(groups):
                    pending15[0] = part15
                    pending[0] = part2
                else:
                    part15()
                    part2()
            S.barrier()

        if stop_after not in ("A", "B") and "C" in phases:
            phase_attn()

        def phase_outproj():
            A.reset(base_mark)
            wo = A.alloc([DC, D], BF16)
            gpost = A.alloc([D], F32)
            mt = [A.alloc([DC, 512], BF16) for _ in range(2)]
            xt = [A.alloc([D], F32) for _ in range(2)]
            x1 = [A.alloc([D], F32) for _ in range(2)]
            xn = [A.alloc([D], BF16) for _ in range(2)]
            sqj = A.alloc([D], BF16)
            h2s = [A.alloc([DC, 512], BF16) for _ in range(2)]
            stat = A.alloc([16], F32)
            S.op("sp", lambda e: e.dma_start(out=gpost, in_=g_post[0]), writes=["gpost"], dma="gpost")
            if "C" not in phases:
                for c in range(DC):
                    S.op("pool", (lambda c: lambda e: e.dma_start(out=wo[:, c, :], in_=w_out[128 * c:128 * c + 128, :]))(c),
                         writes=[("wo", c)], dma="wo%d" % (c % 4))
            wok = [("wo", c) for c in range(DC)]
            MTv = MT.rearrange("c p t -> p c t")
            def d_front(tt):
                blk, tl = tt // 4, tt % 4
                bs = blk % 2
                s = tt % 2
                pb0 = 4 * s
                if tl == 0:
                    S.op("sp", (lambda bs, blk: lambda e: e.dma_start(out=mt[bs], in_=MTv[:, :, 512 * blk:512 * blk + 512]))(bs, blk),
                         writes=[("mt", bs)], dma="mt%d" % bs)
                S.op("sp", (lambda s, tt: lambda e: e.dma_start(out=xt[s], in_=x_in[128 * tt:128 * tt + 128, :]))(s, tt),
                     writes=[("xt", s)], dma="xt%d" % s)

                def fn(e, bs=bs, tl=tl, pb0=pb0):
                    for j in range(4):
                        for c in range(DC):
                            ins = e.matmul(bank(pb0 + j), lhsT=mt[bs][:, c, 128 * tl:128 * tl + 128], rhs=wo[:, c, 512 * j:512 * j + 512],
                                           start=(c == 0), stop=(c == DC - 1))
                    return ins
                S.op("pe", fn, reads=[("mt", bs)] + wok, writes=[("ps", pb0 + j) for j in range(4)])

            def d_back(tt):
                blk, tl = tt // 4, tt % 4
                bs = blk % 2
                s = tt % 2
                pb0 = 4 * s
                pk = [("ps", pb0 + j) for j in range(4)]
                pv_ = psum_t[:, 512 * pb0:512 * pb0 + 2048]
                S.op("act", (lambda s, pv_: lambda e: e.activation(out=sqj, in_=pv_, func=AF.Square,
                                                                  accum_out=stat[:, 8 * s:8 * s + 1]))(s, pv_),
                     reads=pk, writes=["sqj", ("ss", s)])
                rstd_ops(("o", s), stat[:, 8 * s:8 * s + 1], D, stat[:, 8 * s + 2:8 * s + 3], stat[:, 8 * s + 1:8 * s + 2], ("ss", s), ("rs", s))
                S.op("dve", (lambda s, pv_: lambda e: e.scalar_tensor_tensor(out=x1[s], in0=pv_, scalar=stat[:, 8 * s + 2:8 * s + 3],
                                                                            in1=gpost, op0=ALU.mult, op1=ALU.mult))(s, pv_),
                     reads=pk + [("rs", s), "gpost"], writes=[("x1", s)])
                S.op("dve", (lambda s: lambda e: e.tensor_tensor(out=x1[s], in0=x1[s], in1=xt[s], op=ALU.add))(s),
                     reads=[("x1", s), ("xt", s)], writes=[("x1", s)])
                S.op("sp", (lambda s, tt: lambda e: e.dma_start(out=out[128 * tt:128 * tt + 128, :], in_=x1[s]))(s, tt),
                     reads=[("x1", s)], writes=[("out", tt)], dma="x1o%d" % s)
                S.op("act", (lambda s: lambda e: e.activation(out=sqj, in_=x1[s], func=AF.Square,
                                                             accum_out=stat[:, 8 * s + 4:8 * s + 5]))(s),
                     reads=[("x1", s)], writes=["sqj", ("ss2", s)])
                rstd_ops(("o2", s), stat[:, 8 * s + 4:8 * s + 5], D, stat[:, 8 * s + 6:8 * s + 7], stat[:, 8 * s + 5:8 * s + 6], ("ss2", s), ("rs2", s))
                S.op("dve", (lambda s: lambda e: e.tensor_scalar(out=xn[s], in0=x1[s], scalar1=stat[:, 8 * s + 6:8 * s + 7],
                                                                 scalar2=None, op0=ALU.mult))(s),
                     reads=[("x1", s), ("rs2", s)], writes=[("xn", s)])
                pT = psum_t[:, 512 * pb0:512 * pb0 + 1024].bitcast(BF16).rearrange("p (a b) -> p a b", b=128)

                def tr(e, s=s, pT=pT):
                    for dc in range(DC):
                        ins = e.transpose(pT[:, dc, :], xn[s][:, 128 * dc:128 * dc + 128], ident)
                    return ins
                S.op("pe", tr, reads=[("xn", s), "ident"], writes=[("ps", pb0), ("ps", pb0 + 1)])
                S.op("dve", (lambda bs, tl, pT: lambda e: e.tensor_tensor(out=h2s[bs][:, :, 128 * tl:128 * tl + 128], in0=pT, in1=gBf,
                                                                          op=ALU.mult))(bs, tl, pT),
                     reads=[("ps", pb0), ("ps", pb0 + 1)] + gBf_keys, writes=[("h2s", bs, tl)])
                if tl == 3:
                    S.op("sp", (lambda bs, blk: lambda e: e.dma_start(out=H2[:, :, 512 * blk:512 * blk + 512], in_=h2s[bs]))(bs, blk),
                         reads=[("h2s", bs, t_) for t_ in range(4)], writes=[("scr", "h2", blk)], dma="h2s%d" % bs)

            for tt in range(17):
                if tt < 16:
                    d_front(tt)
                if tt > 0:
                    d_back(tt - 1)
            S.barrier()

        if stop_after not in ("A", "B", "C") and "D" in phases:
            phase_outproj()

        def phase_ffn():
            A.reset(base_mark)
            gpost = A.alloc([D], F32)
            h2 = [A.alloc([DC, 512], BF16)] * 2
            actT = A.alloc([FC, 512], BF16)
            wg = [A.alloc([DC, 512], BF16) for _ in range(2)]
            wu = [A.alloc([DC, 512], BF16) for _ in range(2)]
            wd = [A.alloc([D], BF16) for _ in range(5)]
            sgt = [A.alloc([512], F32) for _ in range(2)]
            x1 = [A.alloc([D], F32)] * 2
            y = [A.alloc([D], F32) for _ in range(2)]
            sqj = A.alloc([D], BF16)
            stat = A.alloc([8], F32)
            S.op("sp", lambda e: e.dma_start(out=gpost, in_=g_post[1]), writes=["gpost"], dma="gpost2")
            wdi = 0
            for blk in range(4):
                bs = 0
                S.op("sp", (lambda bs, blk: lambda e: e.dma_start(out=h2[bs], in_=H2[:, :, 512 * blk:512 * blk + 512]))(bs, blk),
                     writes=[("h2", bs)], dma="h2%d" % bs)
                for fc in range(FC):
                    ws = (fc // 4) % 2
                    if fc % 4 == 0:
                        S.op("pool", (lambda ws, fc: lambda e: e.dma_start(out=wg[ws], in_=w_gate_v[:, :, 128 * fc:128 * fc + 512]))(ws, fc),
                             writes=[("wg", ws)], dma="wg%d" % ws)
                        S.op("pool", (lambda ws, fc: lambda e: e.dma_start(out=wu[ws], in_=w_up_v[:, :, 128 * fc:128 * fc + 512]))(ws, fc),
                             writes=[("wu", ws)], dma="wu%d" % ws)
                    pg, pu = 2 * (fc % 2), 2 * (fc % 2) + 1
                    fo = 128 * (fc % 4)

                    def fn(e, wt, pb, bs=bs, fo=fo):
                        for dc in range(DC):
                            ins = e.matmul(bank(pb), lhsT=wt[:, dc, fo:fo + 128], rhs=h2[bs][:, dc, :], start=(dc == 0), stop=(dc == DC - 1))
                        return ins
                    S.op("pe", (lambda fn, ws, pg: lambda e: fn(e, wg[ws], pg))(fn, ws, pg), reads=[("wg", ws), ("h2", bs)], writes=[("ps", pg)])
                    S.op("pe", (lambda fn, ws, pu: lambda e: fn(e, wu[ws], pu))(fn, ws, pu), reads=[("wu", ws), ("h2", bs)], writes=[("ps", pu)])
                    S.op("act", (lambda fc, pg: lambda e: e.activation(out=sgt[fc % 2], in_=bank(pg), func=AF.Silu))(fc, pg),
                         reads=[("ps", pg)], writes=[("sgt", fc % 2)])
                    S.op("dve", (lambda fc, pu: lambda e: e.tensor_tensor(out=actT[:, fc, :], in0=sgt[fc % 2], in1=bank(pu), op=ALU.mult))(fc, pu),
                         reads=[("sgt", fc % 2), ("ps", pu)], writes=[("actT", fc)])
                for half in range(2):
                    for tl in range(2):
                        tt = 4 * blk + 2 * half + tl
                        S.op("act", (lambda tl, tt: lambda e: e.dma_start(out=x1[tl], in_=out[128 * tt:128 * tt + 128, :]))(tl, tt),
                             reads=[("out", tt)], writes=[("x1", 0)], dma="x1i0") if tl == 0 else None
                    for fc in range(FC):
                        ws = wdi % 5
                        wdi += 1
                        S.op("sp", (lambda ws, fc: lambda e: e.dma_start(out=wd[ws], in_=WDB[fc]))(ws, fc),
                             writes=[("wd", ws)], dma="wd%d" % ws, extra_deps=wdb_ops)

                        def fn(e, ws=ws, fc=fc, half=half):
                            for tl in range(2):
                                t = 2 * half + tl
                                for j in range(4):
                                    ins = e.matmul(bank(4 * tl + j), lhsT=actT[:, fc, 128 * t:128 * t + 128], rhs=wd[ws][:, 512 * j:512 * j + 512],
                                                   start=(fc == 0), stop=(fc == FC - 1))
                            return ins
                        S.op("pe", fn, reads=[("wd", ws), ("actT", fc)], writes=[("ps", b_) for b_ in range(8)])
                    for tl in range(2):
                        tt = 4 * blk + 2 * half + tl
                        s = tl
                        pk = [("ps", 4 * tl + j) for j in range(4)]
                        pv_ = psum_t[:, 2048 * tl:2048 * tl + 2048]
                        S.op("act", (lambda s, pv_: lambda e: e.activation(out=sqj, in_=pv_, func=AF.Square, accum_out=stat[:, 4 * s:4 * s + 1]))(s, pv_),
                             reads=pk, writes=["sqj", ("ss", s)])
                        rstd_ops(("f", s), stat[:, 4 * s:4 * s + 1], D, stat[:, 4 * s + 2:4 * s + 3], stat[:, 4 * s + 1:4 * s + 2], ("ss", s), ("rs", s))
                        S.op("dve", (lambda s, pv_: lambda e: e.scalar_tensor_tensor(out=y[s], in0=pv_, scalar=stat[:, 4 * s + 2:4 * s + 3],
                                                                                    in1=gpost, op0=ALU.mult, op1=ALU.mult))(s, pv_),
                             reads=pk + [("rs", s), "gpost"], writes=[("y", s)])
                        S.op("dve", (lambda s: lambda e: e.tensor_tensor(out=y[s], in0=y[s], in1=x1[0], op=ALU.add))(s),
                             reads=[("y", s), ("x1", 0)], writes=[("y", s)])
                        if tl == 0:
                            tt1 = tt + 1
                            S.op("act", (lambda tt1: lambda e: e.dma_start(out=x1[0], in_=out[128 * tt1:128 * tt1 + 128, :]))(tt1),
                                 reads=[("out", tt1)], writes=[("x1", 0)], dma="x1i0")
                        S.op("act", (lambda s, tt: lambda e: e.dma_start(out=out[128 * tt:128 * tt + 128, :], in_=y[s]))(s, tt),
                             reads=[("y", s)], writes=[("out", tt)], dma="yo%d" % s)
            S.barrier()

        if stop_after not in ("A", "B", "C", "D") and "E" in phases:
            phase_ffn()
        S.barrier(final=True)
        S.emit(nc, st)
    return nc


def _t5_bucket_np(n):
    n = np.maximum(n, 0)
    max_exact = 16
    nf = np.maximum(n, 1).astype(np.float32)
    large = max_exact + (np.log(nf / np.float32(max_exact)) / np.float32(math.log(128 / max_exact))
                         * np.float32(32 - max_exact)).astype(np.int32)
    large = np.minimum(large, 31)
    return np.where(n < max_exact, n, large)


def _bias_tiles(rel_bias, p):
    ki = np.arange(128)[:, None, None]
    a = (np.arange(512) // 128)[None, None, :]
    qi = (np.arange(512) % 128)[None, None, :]
    j = np.arange(9)[None, :, None]
    t = j - 1
    dt = np.where(j == 0, 8 * p + a + 1, np.where(t < 4, a - t, 8 * p + a - t))
    n = 128 * dt + qi - ki
    bucket = _t5_bucket_np(n)
    vals = rel_bias[bucket]
    vals = np.where((n >= 0)[..., None], vals, np.float32(MASK))
    return np.ascontiguousarray(np.transpose(vals, (3, 0, 1, 2)).reshape(8, 128, 9 * 512).astype(np.float32))


def _core_inputs(c, inp):
    b, p = c // 2, c % 2
    x = inp["x"][b]
    own = [2 * i + p for i in range(4)]
    oth = [2 * i + 1 - p for i in range(4)]
    halo = np.zeros((128, D), np.float32)
    for i, gb in enumerate(own):
        if gb > 0:
            halo[32 * i:32 * i + 32] = x[512 * gb - 32:512 * gb]
    xc = np.concatenate([x[512 * g:512 * g + 512] for g in own] + [halo] + [x[512 * g:512 * g + 512] for g in oth], axis=0)
    return xc


def _small_params(inp):
    f = lambda a: np.asarray(a, np.float32)
    cols = [
        f(inp["norm_pre_mix"][0]).reshape(16, 128).T,
        f(inp["norm_pre_ffn"][0]).reshape(16, 128).T,
        f(inp["conv_dw_w"][0]).T.reshape(8, 128, NTAP).transpose(1, 0, 2).reshape(128, 8 * NTAP),
        f(inp["conv_dw_b"][0]).reshape(8, 128).T,
        f(inp["conv_ln_g"][0]).reshape(8, 128).T,
        f(inp["conv_ln_b"][0]).reshape(8, 128).T,
        f(inp["subln_g"][0]).reshape(128, 1),
        np.broadcast_to(f(inp["rel_bias"])[31][None, :], (128, 8)),
        np.broadcast_to(np.concatenate([f(inp["lambda_q1"][0]), f(inp["lambda_k1"][0]),
                                        f(inp["lambda_q2"][0]), f(inp["lambda_k2"][0])])[None, :], (128, 256)),
    ]
    return np.ascontiguousarray(np.concatenate(cols, axis=1).astype(np.float32))


_NC_CACHE = {}


def kernel(**inputs):
    inp = {k: np.asarray(v) for k, v in inputs.items()}
    if "nc" not in _NC_CACHE:
        _NC_CACHE["nc"] = build_nc()
    nc = _NC_CACHE["nc"]
    small = _small_params(inp)
    gpost = np.ascontiguousarray(np.stack([np.broadcast_to(inp["norm_post_mix"][0][None, :], (128, D)),
                                           np.broadcast_to(inp["norm_post_ffn"][0][None, :], (128, D))]).astype(np.float32))
    rel = np.asarray(inp["rel_bias"], np.float32)
    bt = [_bias_tiles(rel, 0), _bias_tiles(rel, 1)]
    shared = dict(w_in=np.ascontiguousarray(inp["w_in"][0]), w_out=np.ascontiguousarray(inp["w_out"][0]),
                  w_gate=np.ascontiguousarray(inp["w_gate"][0]), w_up=np.ascontiguousarray(inp["w_up"][0]),
                  w_down=np.ascontiguousarray(inp["w_down"][0]), p_small=small, g_post=gpost)
    in_maps = []
    for c in range(N_CORES):
        m = dict(shared)
        m["x"] = _core_inputs(c, inp)
        m["bias_t"] = bt[c % 2]
        in_maps.append(m)
    res = run_bass_kernel_spmd(nc, in_maps, core_ids=list(range(N_CORES)))
    outp = np.empty((4, S_LEN, D), np.float32)
    for c in range(N_CORES):
        b, p = c // 2, c % 2
        o = res.results[c]["out"]
        for i in range(4):
            gb = 2 * i + p
            outp[b, 512 * gb:512 * gb + 512] = o[512 * i:512 * i + 512]
    return outp
```

```python
import math
import os
from contextlib import ExitStack

import numpy as np
import concourse.bass as bass
import concourse.mybir as mybir
from concourse.bass_utils import run_bass_kernel_spmd

F32 = mybir.dt.float32
BF16 = mybir.dt.bfloat16
AF = mybir.ActivationFunctionType
ALU = mybir.AluOpType

D = 2048
DC = 16
S_LEN = 4096
NH = 8
FF = 5632
FC = 44
IN_W = 5120
EPS = 1e-6
LAM_INIT = 0.2
NTAP = 31
MASK = -1e30
N_CORES = 8


class Sched:
    ENGS = ("pe", "act", "dve", "pool", "sp")

    def __init__(self):
        self.ops = []
        self.last_writer = {}
        self.readers = {}
        self.last_eng = {}
        self.open_dmas = []
        self.bg_dmas = []

    def op(self, eng, fn, reads=(), writes=(), dma=None, extra_deps=(), bg=False):
        deps = set(extra_deps)
        for k in reads:
            w = self.last_writer.get(k)
            if w is not None:
                deps.add(w)
        for k in writes:
            w = self.last_writer.get(k)
            if w is not None:
                deps.add(w)
            for r in self.readers.get(k, ()):
                deps.add(r)
        idx = len(self.ops)
        self.ops.append(dict(eng=eng, fn=fn, deps=deps, dma=dma, cons=False, sig=None))
        for k in reads:
            self.readers.setdefault(k, []).append(idx)
        for k in writes:
            self.last_writer[k] = idx
            self.readers[k] = []
        if fn is not None:
            if dma is None:
                self.last_eng[eng] = idx
            elif bg:
                self.bg_dmas.append(idx)
            else:
                self.open_dmas.append(idx)
        return idx

    def barrier(self, final=False):
        deps = set(self.last_eng.values()) | set(self.open_dmas)
        if final:
            deps |= set(self.bg_dmas)
            self.bg_dmas = []
        self.open_dmas = []
        for e in self.ENGS:
            self.op(e, None, extra_deps=deps)
        self.last_writer = {}
        self.readers = {}

    @staticmethod
    def _needs_sync(P, C):
        if P["dma"] is None and C["dma"] is None and P["eng"] == C["eng"] == "pe":
            return False
        return True

    def emit(self, nc, stack):
        ops = self.ops
        for o in ops:
            for d in o["deps"]:
                if self._needs_sync(ops[d], o):
                    ops[d]["cons"] = True
        counts = {}
        for o in ops:
            if not o["cons"]:
                continue
            name = ("e_" + o["eng"]) if o["dma"] is None else ("d_" + o["dma"])
            inc = 1 if o["dma"] is None else 16
            counts[name] = counts.get(name, 0) + inc
            o["sig"] = (name, counts[name], inc)
        per_eng = {e: [] for e in self.ENGS}
        waited = {e: {} for e in self.ENGS}
        for o in ops:
            need = {}
            for d in o["deps"]:
                P = ops[d]
                if not self._needs_sync(P, o):
                    continue
                name, val, _ = P["sig"]
                if need.get(name, 0) < val:
                    need[name] = val
            waits = []
            for name, val in need.items():
                if waited[o["eng"]].get(name, 0) < val:
                    waited[o["eng"]][name] = val
                    waits.append((name, val))
            per_eng[o["eng"]].append((waits, o["fn"], o["sig"]))
        sems = {}
        for name in counts:
            sems[name] = stack.enter_context(nc.semaphore("s_" + name))
        block = stack.enter_context(nc.Block())

        def make(engname):
            def body(eng):
                for waits, fn, sig in per_eng[engname]:
                    for name, val in waits:
                        eng.wait_ge(sems[name], val)
                    if fn is None:
                        continue
                    ins = fn(eng)
                    if sig is not None:
                        ins.then_inc(sems[sig[0]], sig[2])
            return body

        block.tensor(make("pe"))
        block.scalar(make("act"))
        block.vector(make("dve"))
        block.gpsimd(make("pool"))
        block.sync(make("sp"))
        self.counts = counts


class Arena:
    def __init__(self, tensor, nbytes):
        self.t = tensor
        self.n = nbytes
        self.off = 0

    def alloc(self, free_shape, dt):
        n = 1
        for s in free_shape:
            n *= s
        esz = 4 if dt == F32 else 2
        nb = n * esz
        a = self.off
        self.off = (a + nb + 63) // 64 * 64
        assert self.off <= self.n, ("SBUF arena overflow", self.off, self.n)
        ap = self.t[:, a // 4:(a + nb) // 4]
        if dt != F32:
            ap = ap.bitcast(dt)
        if len(free_shape) == 2:
            ap = ap.rearrange("p (a b) -> p a b", b=free_shape[1])
        elif len(free_shape) == 3:
            ap = ap.rearrange("p (a b c) -> p a b c", b=free_shape[1], c=free_shape[2])
        return ap

    def at(self, off, free_shape, dt):
        save = self.off
        self.off = off
        ap = self.alloc(free_shape, dt)
        self.off = save
        return ap

    def mark(self):
        return self.off

    def reset(self, m):
        self.off = m


def build_nc(debug=False, stop_after=None, phases="ABCDE", nheads=NH):
    nc = bass.Bass("TRN2", target_bir_lowering=False)
    skind = "ExternalOutput" if debug else "Internal"

    def din(name, shape, dt=F32):
        return nc.dram_tensor(name, shape, dt, kind="ExternalInput").ap()

    x_in = din("x", [33 * 128, D])
    w_in = din("w_in", [D, IN_W])
    w_out = din("w_out", [D, D])
    w_gate = din("w_gate", [D, FF])
    w_up = din("w_up", [D, FF])
    w_down = din("w_down", [FF, D])
    p_small = din("p_small", [128, 16 + 16 + 248 + 8 + 8 + 8 + 1 + 8 + 256])
    g_post = din("g_post", [2, 128, D])
    bias_t = din("bias_t", [NH, 128, 9 * 512])
    out = nc.dram_tensor("out", [2048, D], F32, kind="ExternalOutput").ap()

    KT = nc.dram_tensor("KT", [NH, 128, S_LEN], BF16, kind=skind).ap()
    VS = nc.dram_tensor("VS", [NH, 128, 32, 128], BF16, kind=skind).ap()
    QT = nc.dram_tensor("QT", [NH, 2, 128, 2048], BF16, kind=skind).ap()
    CS = nc.dram_tensor("CS", [8, 128, 2048], F32, kind=skind).ap()
    MT = nc.dram_tensor("MT", [16, 128, 2048], BF16, kind=skind).ap()
    H2 = nc.dram_tensor("H2", [128, DC, 2048], BF16, kind=skind).ap()
    WDB = nc.dram_tensor("WDB", [FC, 128, D], BF16, kind="Internal").ap()

    w_in_v = w_in.rearrange("(dc p) c -> p dc c", p=128)
    w_gate_v = w_gate.rearrange("(dc p) c -> p dc c", p=128)
    w_up_v = w_up.rearrange("(dc p) c -> p dc c", p=128)

    st = ExitStack()
    with st:
        ARENA_BYTES = 206 * 1024
        arena_t = st.enter_context(nc.sbuf_tensor("arena", [128, ARENA_BYTES // 4], F32))
        A = Arena(arena_t, ARENA_BYTES)
        psum_t = st.enter_context(nc.psum_tensor("psum", [128, 8 * 512], F32))

        def bank(b, n=512, off=0):
            return psum_t[:, 512 * b + off:512 * b + off + n]

        S = Sched()

        identf = A.alloc([128], F32)
        ident = A.alloc([128], BF16)
        ones_bf = A.alloc([128], BF16)
        ones_f = A.alloc([128], F32)
        small = A.alloc([16 + 16 + 248 + 8 + 8 + 8 + 1 + 8 + 256], F32)
        gBm = A.alloc([DC, 128], F32)
        gBf = A.alloc([DC, 128], F32)
        misc = A.alloc([16], F32)
        o0 = 0
        gpm = small[:, 0:16]
        gpf = small[:, 16:32]
        cw = small[:, 32:280].rearrange("p (c j) -> p c j", j=NTAP)
        cb = small[:, 280:288]
        lng = small[:, 288:296]
        lnb = small[:, 296:304]
        subg = small[:, 304:305]
        relc = small[:, 305:313]
        lamp = small[:, 313:569].rearrange("p (k d) -> p k d", d=64)
        neglam = misc[:, 0:1]
        negc = misc[:, 1:9]
        subg08 = misc[:, 9:10]
        lam_t = misc[:, 10:14]

        S.op("sp", lambda e: e.dma_start(out=small, in_=p_small), writes=["small"], dma="small")
        S.op("pool", lambda e: e.memset(identf, 0.0), writes=["identf"])
        S.op("pool", lambda e: e.affine_select(out=identf, in_=identf, pattern=[[-1, 128]],
                                               compare_op=ALU.not_equal, fill=1.0, base=0,
                                               channel_multiplier=1),
             reads=["identf"], writes=["identf"])
        S.op("dve", lambda e: e.tensor_copy(out=ident, in_=identf), reads=["identf"], writes=["ident"])
        S.op("dve", lambda e: e.memset(ones_bf, 1.0), writes=["ones_bf"])
        S.op("dve", lambda e: e.memset(ones_f, 1.0), writes=["ones_f"])
        for dc in range(DC):
            S.op("dve", (lambda dc: lambda e: e.tensor_scalar(out=gBm[:, dc, :], in0=ones_f, scalar1=gpm[:, dc:dc + 1],
                                                               scalar2=None, op0=ALU.mult))(dc),
                 reads=["small", "ones_f"], writes=[("gBm", dc)])
            S.op("dve", (lambda dc: lambda e: e.tensor_scalar(out=gBf[:, dc, :], in0=ones_f, scalar1=gpf[:, dc:dc + 1],
                                                               scalar2=None, op0=ALU.mult))(dc),
                 reads=["small", "ones_f"], writes=[("gBf", dc)])
        gBm_keys = [("gBm", dc) for dc in range(DC)]
        gBf_keys = [("gBf", dc) for dc in range(DC)]
        lj = A.alloc([2, 64], F32)
        S.op("dve", lambda e: e.tensor_tensor(out=lj[:, 0, :], in0=lamp[:, 0, :], in1=lamp[:, 1, :], op=ALU.mult),
             reads=["small"], writes=["lj0"])
        S.op("dve", lambda e: e.tensor_tensor(out=lj[:, 1, :], in0=lamp[:, 2, :], in1=lamp[:, 3, :], op=ALU.mult),
             reads=["small"], writes=["lj1"])
        S.op("dve", lambda e: e.reduce_sum(out=lam_t[:, 0:1], in_=lj[:, 0, :], axis=mybir.AxisListType.X),
             reads=["lj0"], writes=["lam0"])
        S.op("dve", lambda e: e.reduce_sum(out=lam_t[:, 1:2], in_=lj[:, 1, :], axis=mybir.AxisListType.X),
             reads=["lj1"], writes=["lam1"])
        S.op("act", lambda e: e.activation(out=lam_t[:, 2:4], in_=lam_t[:, 0:2], func=AF.Exp),
             reads=["lam0", "lam1"], writes=["lam2"])
        S.op("dve", lambda e: e.tensor_tensor(out=lam_t[:, 0:1], in0=lam_t[:, 3:4], in1=lam_t[:, 2:3], op=ALU.subtract),
             reads=["lam2"], writes=["lam3"])
        S.op("dve", lambda e: e.tensor_scalar(out=neglam, in0=lam_t[:, 0:1], scalar1=-LAM_INIT, scalar2=None, op0=ALU.add),
             reads=["lam3"], writes=["neglam"])
        S.op("dve", lambda e: e.tensor_scalar(out=negc, in0=relc, scalar1=-1.0, scalar2=None, op0=ALU.mult),
             reads=["small"], writes=["negc"])
        S.op("dve", lambda e: e.tensor_scalar(out=subg08, in0=subg, scalar1=1.0 - LAM_INIT, scalar2=None, op0=ALU.mult),
             reads=["small"], writes=["subg08"])
        base_mark = A.mark()

        def rstd_ops(tag, ss_ap, n, rs_ap, tmp_ap, ss_key, rs_key):
            S.op("dve", lambda e: e.tensor_scalar(out=tmp_ap, in0=ss_ap, scalar1=1.0 / n, scalar2=EPS,
                                                  op0=ALU.mult, op1=ALU.add),
                 reads=[ss_key], writes=[(tag, "ms")])
            S.op("act", lambda e: e.activation(out=tmp_ap, in_=tmp_ap, func=AF.Ln),
                 reads=[(tag, "ms")], writes=[(tag, "sq")])
            S.op("act", lambda e: e.activation(out=rs_ap, in_=tmp_ap, func=AF.Exp, scale=-0.5), reads=[(tag, "sq")], writes=[rs_key])

        def build_hT(tiles, hT, gB, gB_keys, xt, xn, sqj, stat, pt_banks, hook=None):
            ns = len(xt)
            nt = len(tiles)

            def st1(j):
                tt = tiles[j]
                s = j % ns
                S.op("sp", (lambda s, tt: lambda e: e.dma_start(out=xt[s], in_=x_in[128 * tt:128 * tt + 128, :]))(s, tt),
                     writes=[("xt", s)], dma="xt%d" % s)
                S.op("act", (lambda s: lambda e: e.activation(out=sqj, in_=xt[s], func=AF.Square,
                                                             accum_out=stat[:, 4 * s:4 * s + 1]))(s),
                     reads=[("xt", s)], writes=["sqj", ("ss", s)])
                rstd_ops(("h", s), stat[:, 4 * s:4 * s + 1], D, stat[:, 4 * s + 2:4 * s + 3], stat[:, 4 * s + 1:4 * s + 2],
                         ("ss", s), ("rs", s))

            def st2(j):
                s = j % ns
                S.op("dve", (lambda s: lambda e: e.tensor_scalar(out=xn[s], in0=xt[s], scalar1=stat[:, 4 * s + 2:4 * s + 3],
                                                                scalar2=None, op0=ALU.mult))(s),
                     reads=[("xt", s), ("rs", s)], writes=[("xn", s)])
                pb = pt_banks[s]
                pT = psum_t[:, 512 * pb:512 * pb + 1024].bitcast(BF16).rearrange("p (a b) -> p a b", b=128)

                def tr(e, s=s, pT=pT):
                    for dc in range(DC):
                        ins = e.transpose(pT[:, dc, :], xn[s][:, 128 * dc:128 * dc + 128], ident)
                    return ins
                S.op("pe", tr, reads=[("xn", s), "ident"], writes=[("ps", pb), ("ps", pb + 1)])

            def st3(j):
                s = j % ns
                pb = pt_banks[s]
                pT = psum_t[:, 512 * pb:512 * pb + 1024].bitcast(BF16).rearrange("p (a b) -> p a b", b=128)
                S.op("dve", (lambda j, pT: lambda e: e.tensor_tensor(out=hT[:, :, 128 * j:128 * j + 128], in0=pT, in1=gB,
                                                                     op=ALU.mult))(j, pT),
                     reads=[("ps", pb), ("ps", pb + 1)] + gB_keys, writes=[("hT", j)])

            for k in range(nt + 3):
                if k < nt:
                    if hook is not None:
                        hook(k)
                    st1(k)
                if 0 <= k - 1 < nt:
                    st2(k - 1)
                if 0 <= k - 2 < nt:
                    st3(k - 2)

        def load_wchunk(dst, src_view, c0, ncol, key, slot):
            S.op("pool", lambda e: e.dma_start(out=dst, in_=src_view[:, :, c0:c0 + ncol]),
                 writes=[(key, slot)], dma="%s%d" % (key, slot))

        def proj_fm(wt, wkey, hT, hkeys, col0, ps_b):
            def fn(e):
                for dc in range(DC):
                    ins = e.matmul(bank(ps_b), lhsT=wt[:, dc, :], rhs=hT[:, dc, col0:col0 + 512],
                                   start=(dc == 0), stop=(dc == DC - 1))
                return ins
            S.op("pe", fn, reads=[wkey] + hkeys, writes=[("ps", ps_b)])

        wdb_ops = []

        def precast_wd(n):
            for _ in range(n):
                fc = len(wdb_ops)
                if fc >= FC:
                    return
                wdb_ops.append(S.op("pool", (lambda fc: lambda e: e.dma_start(out=WDB[fc], in_=w_down[128 * fc:128 * fc + 128, :]))(fc),
                                    dma="wdb%d" % (fc % 4), bg=True))

        def build_bufs():
            xt = [A.alloc([D], F32) for _ in range(4)]
            xn = [A.alloc([D], BF16) for _ in range(4)]
            sqj = A.alloc([D], BF16)
            stat = A.alloc([16], F32)
            return xt, xn, sqj, stat

        def phase_proj(own, prebuilt=False):
            A.reset(base_mark)
            ntile = 17 if own else 16
            tiles = list(range(0, 17)) if own else list(range(17, 33))
            hT = A.alloc([DC, ntile * 128], BF16)
            m1 = A.mark()
            if not prebuilt:
                xt, xn, sqj, stat = build_bufs()
                build_hT(tiles, hT, gBm, gBm_keys, xt, xn, sqj, stat, [0, 2, 4, 6])
                S.barrier()
            A.reset(m1)
            wc = [A.alloc([DC, 128], BF16) for _ in range(4)]
            wv = [A.alloc([DC, 256], BF16) for _ in range(2)]
            stg = [A.alloc([2048], BF16) for _ in range(2)]
            qz = [A.alloc([2, 2048], BF16) for _ in range(2)] if own else None
            if own:
                for q_ in range(2):
                    S.op("pool", (lambda q_: lambda e: e.memset(qz[q_], 0.0))(q_), writes=[("qz", q_, b_) for b_ in range(4)])
            vstg = [A.alloc([2, 16, 128], BF16) for _ in range(2)]
            hk = lambda blk: [("hT", 4 * blk + t) for t in range(4)]
            pair_half = 0 if own else 1
            chunks = []
            if own:
                chunks += [("q", h) for h in range(NH)]
            chunks += [("k", h) for h in range(NH)]
            pb_rot = 0
            for ci, (kind, h) in enumerate(chunks):
                ws = ci % 4
                col = (0 if kind == "q" else 1024) + 128 * h
                load_wchunk(wc[ws], w_in_v, col, 128, "wc", ws)
                ss_ = ci % 2
                for blk in range(4):
                    pb = 4 + (pb_rot % 4)
                    pb_rot += 1
                    proj_fm(wc[ws], ("wc", ws), hT, hk(blk), 512 * blk, pb)
                    if kind == "q":
                        S.op("act", (lambda ss_, blk, pb: lambda e: e.activation(
                            out=qz[ss_][0:64, 0, 512 * blk:512 * blk + 512], in_=bank(pb)[0:64, :], func=AF.Copy, scale=0.125))(ss_, blk, pb),
                            reads=[("ps", pb), ("qz", ss_, blk)], writes=[("qza", ss_, blk)])
                        S.op("dve", (lambda ss_, blk, pb: lambda e: e.tensor_scalar(
                            out=qz[ss_][64:128, 1, 512 * blk:512 * blk + 512], in0=bank(pb)[64:128, :], scalar1=0.125, scalar2=None,
                            op0=ALU.mult))(ss_, blk, pb),
                            reads=[("ps", pb), ("qz", ss_, blk), ("qza", ss_, blk)], writes=[("qzb", ss_, blk)])
                    else:
                        eng = "act" if blk % 2 == 0 else "dve"
                        if eng == "act":
                            S.op("act", (lambda ss_, blk, pb: lambda e: e.activation(
                                out=stg[ss_][:, 512 * blk:512 * blk + 512], in_=bank(pb), func=AF.Copy))(ss_, blk, pb),
                                reads=[("ps", pb)], writes=[("stg", ss_, blk)])
                        else:
                            S.op("dve", (lambda ss_, blk, pb: lambda e: e.tensor_copy(
                                out=stg[ss_][:, 512 * blk:512 * blk + 512], in_=bank(pb)))(ss_, blk, pb),
                                reads=[("ps", pb)], writes=[("stg", ss_, blk)])
                if kind == "q":
                    S.op("sp", (lambda h, ss_: lambda e: e.dma_start(out=QT[h].rearrange("c p t -> p c t"), in_=qz[ss_]))(h, ss_),
                         reads=[(k_, ss_, b_) for b_ in range(4) for k_ in ("qza", "qzb")],
                         writes=[("scr", kind, h)] + [(k_, ss_, b_) for b_ in range(4) for k_ in ("qza", "qzb")], dma="qzo%d" % ss_)
                    continue
                else:
                    dst = KT[h].rearrange("p (i two c) -> p i two c", two=2, c=512)[:, :, pair_half, :]
                    src = stg[ss_].rearrange("p (i c) -> p i c", c=512)
                S.op("sp", (lambda dst, src: lambda e: e.dma_start(out=dst, in_=src))(dst, src),
                     reads=[("stg", ss_, b_) for b_ in range(4)], writes=[("scr", kind, h)], dma="stgo%d" % ss_)
            for vg in range(4):
                ws = vg % 2
                load_wchunk(wv[ws], w_in_v, 2048 + 256 * vg, 256, "wv", ws)
                for t in range(16):
                    pb = 4 + (pb_rot % 4)
                    pb_rot += 1

                    def fn(e, t=t, pb=pb, ws=ws):
                        for dc in range(DC):
                            ins = e.matmul(bank(pb, 256), lhsT=hT[:, dc, 128 * t:128 * t + 128], rhs=wv[ws][:, dc, :],
                                           start=(dc == 0), stop=(dc == DC - 1))
                        return ins
                    S.op("pe", fn, reads=[("wv", ws), ("hT", t)], writes=[("ps", pb)])
                    src = bank(pb, 256).rearrange("p (a b) -> p a b", b=128)
                    if t % 2 == 0:
                        S.op("act", (lambda ws, t, src: lambda e: e.activation(out=vstg[ws][:, :, t, :], in_=src, func=AF.Copy))(ws, t, src),
                             reads=[("ps", pb)], writes=[("vstg", ws, t)])
                    else:
                        S.op("dve", (lambda ws, t, src: lambda e: e.tensor_copy(out=vstg[ws][:, :, t, :], in_=src))(ws, t, src),
                             reads=[("ps", pb)], writes=[("vstg", ws, t)])
                for hh in range(2):
                    h = 2 * vg + hh
                    dst = VS[h].rearrange("p (i j) e -> p i j e", j=8)[:, :, 4 * pair_half:4 * pair_half + 4, :]
                    src = vstg[ws][:, hh, :, :].rearrange("p (i t) e -> p i t e", t=4)
                    S.op("sp", (lambda dst, src: lambda e: e.dma_start(out=dst, in_=src))(dst, src),
                         reads=[("vstg", ws, t) for t in range(16)], writes=[("scr", "v", h)], dma="vstgo%d_%d" % (ws, hh))
            if not own:
                S.barrier()
                return
            S.barrier()
            A.reset(m1)
            NDVE = 22
            NPE = NTAP - NDVE
            wc = [A.alloc([DC, 128], BF16) for _ in range(4)]
            hg = [A.alloc([4, 544], F32) for _ in range(2)]
            sga = A.alloc([4, 544], F32)
            accA = A.alloc([4, 512], F32)
            off_hgh = A.mark()
            hgh = [A.alloc([4, 544], BF16) for _ in range(2)]
            hgl = [A.alloc([4, 544], BF16) for _ in range(2)]
            off_dw = A.mark()
            dwh = [A.alloc([NPE, 128], BF16) for _ in range(2)]
            dwl = [A.alloc([NPE, 128], BF16) for _ in range(2)]
            cres = [A.alloc([2048], F32) for _ in range(2)]
            sqt = A.alloc([2048], F32)
            sumacc = A.alloc([2048], F32)
            sqacc = A.alloc([2048], F32)

            def fnh(e, wt, pb):
                for dc in range(DC):
                    ins = e.matmul(bank(pb, 128), lhsT=wt[:, dc, :], rhs=hT[:, dc, 2048:2176],
                                   start=(dc == 0), stop=(dc == DC - 1))
                return ins

            def stage_u(cc):
                wa, wg_ = wc[(2 * cc) % 4], wc[(2 * cc + 1) % 4]
                ka, kg = ("wc", (2 * cc) % 4), ("wc", (2 * cc + 1) % 4)
                load_wchunk(wa, w_in_v, 3072 + 128 * cc, 128, "wc", (2 * cc) % 4)
                load_wchunk(wg_, w_in_v, 4096 + 128 * cc, 128, "wc", (2 * cc + 1) % 4)
                precast_wd(6)
                hs = cc % 2
                for blk in range(4):
                    pa, pg = 4 + 2 * (blk % 2), 5 + 2 * (blk % 2)
                    proj_fm(wa, ka, hT, hk(blk), 512 * blk, pa)
                    proj_fm(wg_, kg, hT, hk(blk), 512 * blk, pg)
                    S.op("act", (lambda blk, pg: lambda e: e.activation(out=sga[:, blk, 32:544], in_=bank(pg), func=AF.Sigmoid))(blk, pg),
                         reads=[("ps", pg)], writes=[("sga", blk)])
                    S.op("act", (lambda blk, pa, hs: lambda e: e.activation(out=hg[hs][:, blk, 32:544], in_=bank(pa), func=AF.Copy))(blk, pa, hs),
                         reads=[("ps", pa)], writes=[("hga", hs, blk)])
                S.op("pe", (lambda wa: lambda e: fnh(e, wa, 4))(wa), reads=[ka, ("hT", 16)], writes=[("ps", 4)])
                S.op("pe", (lambda wg_: lambda e: fnh(e, wg_, 5))(wg_), reads=[kg, ("hT", 16)], writes=[("ps", 5)])
                S.op("act", lambda e: e.activation(out=sga[:, :, 0:32], in_=bank(5, 128).rearrange("p (a b) -> p a b", b=32), func=AF.Sigmoid),
                     reads=[("ps", 5)], writes=[("sga", "halo")])
                S.op("act", (lambda hs: lambda e: e.activation(out=hg[hs][:, :, 0:32], in_=bank(4, 128).rearrange("p (a b) -> p a b", b=32),
                                                               func=AF.Copy))(hs),
                     reads=[("ps", 4)], writes=[("hga", hs, "halo")])

            def stage_g(cc):
                hs = cc % 2
                hak = [("hga", hs, blk) for blk in range(4)] + [("hga", hs, "halo")]
                sgk = [("sga", blk) for blk in range(4)] + [("sga", "halo")]
                S.op("dve", (lambda hs: lambda e: e.tensor_tensor(out=hg[hs], in0=hg[hs], in1=sga, op=ALU.mult))(hs),
                     reads=hak + sgk, writes=[("hg", hs)] + hak)
                S.op("act", (lambda hs: lambda e: e.activation(out=hgh[hs], in_=hg[hs], func=AF.Copy))(hs), reads=[("hg", hs)], writes=[("hgh", hs)])
                S.op("dve", (lambda hs: lambda e: e.tensor_tensor(out=hgl[hs], in0=hg[hs], in1=hgh[hs], op=ALU.subtract))(hs),
                     reads=[("hg", hs), ("hgh", hs)], writes=[("hgl", hs)])
                for jj in range(NPE):
                    j = NDVE + jj
                    wj = cw[:, cc, j:j + 1]
                    S.op("act", (lambda hs, jj, wj: lambda e: e.activation(out=dwh[hs][:, jj, :], in_=identf, func=AF.Copy, scale=wj))(hs, jj, wj),
                         reads=["identf", "small"], writes=[("dwh", hs, jj)])
                    S.op("dve", (lambda hs, jj, wj: lambda e: e.scalar_tensor_tensor(out=dwl[hs][:, jj, :], in0=identf, scalar=wj, in1=dwh[hs][:, jj, :],
                                                                                      op0=ALU.mult, op1=ALU.subtract))(hs, jj, wj),
                         reads=["identf", "small", ("dwh", hs, jj)], writes=[("dwl", hs, jj)])

            def stage_t(cc):
                hs = cc % 2
                cs = cc % 2
                for blk in range(4):
                    def pconv(e, hs=hs, blk=blk):
                        n = 0
                        for jj in range(NPE):
                            j = NDVE + jj
                            for (wt, ht) in ((dwh, hgh), (dwh, hgl), (dwl, hgh)):
                                ins = e.matmul(bank(blk), lhsT=wt[hs][:, jj, :], rhs=ht[hs][:, blk, 2 + j:2 + j + 512],
                                               start=(n == 0), stop=(n == 3 * NPE - 1))
                                n += 1
                        return ins
                    S.op("pe", pconv, reads=[("hgh", hs), ("hgl", hs)] + [(k_, hs, jj) for jj in range(NPE) for k_ in ("dwh", "dwl")],
                         writes=[("ps", blk)])
                for j in range(NDVE):
                    src = hg[hs][:, :, 2 + j:2 + j + 512]
                    wj = cw[:, cc, j:j + 1]
                    if j == 0:
                        S.op("dve", (lambda src, wj, cc: lambda e: e.tensor_scalar(out=accA, in0=src, scalar1=wj, scalar2=cb[:, cc:cc + 1],
                                                                                   op0=ALU.mult, op1=ALU.add))(src, wj, cc),
                             reads=[("hg", hs), "small"], writes=["accA"])
                    else:
                        S.op("dve", (lambda src, wj: lambda e: e.scalar_tensor_tensor(out=accA, in0=src, scalar=wj, in1=accA,
                                                                                      op0=ALU.mult, op1=ALU.add))(src, wj),
                             reads=[("hg", hs), "small", "accA"], writes=["accA"])
                S.op("dve", (lambda cs: lambda e: e.tensor_tensor(out=cres[cs], in0=accA.rearrange("p a b -> p (a b)"), in1=psum_t[:, 0:2048], op=ALU.add))(cs),
                     reads=["accA"] + [("ps", b_) for b_ in range(4)], writes=[("cres", cs)])
                S.op("sp", (lambda cs, cc: lambda e: e.dma_start(out=CS[cc], in_=cres[cs]))(cs, cc),
                     reads=[("cres", cs)], writes=[("scr", "cs", cc)], dma="cres%d" % cs)
                S.op("act", (lambda cs: lambda e: e.activation(out=sqt, in_=cres[cs], func=AF.Square))(cs),
                     reads=[("cres", cs)], writes=["sqt"])

            def stage_s(cc):
                cs = cc % 2
                for blk in range(4):
                    for which, srcb, acc, key in ((0, cres[cs], sumacc, ("cres", cs)), (1, sqt, sqacc, "sqt")):
                        pb = 6 + which
                        S.op("pe", (lambda srcb, blk, pb: lambda e: e.matmul(bank(pb), lhsT=ones_f, rhs=srcb[:, 512 * blk:512 * blk + 512],
                                                                           start=True, stop=True))(srcb, blk, pb),
                             reads=[key, "ones_f"], writes=[("ps", pb)])
                        if cc == 0:
                            S.op("dve", (lambda acc, blk, pb: lambda e: e.tensor_copy(out=acc[:, 512 * blk:512 * blk + 512], in_=bank(pb)))(acc, blk, pb),
                                 reads=[("ps", pb)], writes=[("acc", which, blk)])
                        else:
                            S.op("dve", (lambda acc, blk, pb: lambda e: e.tensor_tensor(out=acc[:, 512 * blk:512 * blk + 512],
                                                                                       in0=acc[:, 512 * blk:512 * blk + 512], in1=bank(pb), op=ALU.add))(acc, blk, pb),
                                 reads=[("ps", pb), ("acc", which, blk)], writes=[("acc", which, blk)])

            stage_u(0)
            stage_g(0)
            for cc in range(8):
                if cc + 1 < 8:
                    stage_u(cc + 1)
                stage_t(cc)
                if cc + 1 < 8:
                    stage_g(cc + 1)
                stage_s(cc)
            S.barrier()
            acck = [("acc", w_, b_) for w_ in range(2) for b_ in range(4)]
            mean = sumacc
            rstd = sqacc
            m2 = sqt
            S.op("dve", lambda e: e.tensor_scalar(out=mean, in0=sumacc, scalar1=1.0 / 1024, scalar2=None, op0=ALU.mult),
                 reads=acck, writes=["mean"])
            S.op("dve", lambda e: e.tensor_tensor(out=m2, in0=mean, in1=mean, op=ALU.mult), reads=["mean"], writes=["m2", "sqt"])
            S.op("dve", lambda e: e.scalar_tensor_tensor(out=rstd, in0=sqacc, scalar=1.0 / 1024, in1=m2, op0=ALU.mult, op1=ALU.subtract),
                 reads=acck + ["m2"], writes=["var"])
            S.op("dve", lambda e: e.tensor_scalar(out=rstd, in0=rstd, scalar1=EPS, scalar2=None, op0=ALU.add), reads=["var"], writes=["var2"])
            S.op("act", lambda e: e.activation(out=rstd, in_=rstd, func=AF.Sqrt), reads=["var2"], writes=["sd"])
            S.op("dve", lambda e: e.reciprocal(out=rstd, in_=rstd), reads=["sd"], writes=["rstd"])
            cl = cres
            tmp = [A.at(off_hgh, [2048], F32), A.at(off_hgh + 8192, [2048], F32)]
            mst = [A.at(off_dw, [2048], BF16), A.at(off_dw + 4096, [2048], BF16)]
            A.reset(base_mark)
            hT_b = A.alloc([DC, 16 * 128], BF16)
            xt_b, xn_b, sqj_b, stat_b = build_bufs()
            assert A.mark() <= off_hgh, (A.mark(), off_hgh)

            def ln_chunk(cc):
                s = cc % 2
                S.op("pool", (lambda s, cc: lambda e: e.dma_start(out=cl[s], in_=CS[cc]))(s, cc),
                     reads=[("scr", "cs", cc)], writes=[("cres", s)], dma="cl%d" % s)
                S.op("dve", (lambda s: lambda e: e.tensor_tensor(out=tmp[s], in0=cl[s], in1=mean, op=ALU.subtract))(s),
                     reads=[("cres", s), "mean"], writes=[("tmp", s)])
                S.op("dve", (lambda s: lambda e: e.tensor_tensor(out=tmp[s], in0=tmp[s], in1=rstd, op=ALU.mult))(s),
                     reads=[("tmp", s), "rstd"], writes=[("tmp", s)])
                S.op("act", (lambda s, cc: lambda e: e.activation(out=mst[s], in_=tmp[s], func=AF.Silu, bias=lnb[:, cc:cc + 1],
                                                                scale=lng[:, cc:cc + 1]))(s, cc),
                     reads=[("tmp", s), "small"], writes=[("mst", s)])
                S.op("act", (lambda s, cc: lambda e: e.dma_start(out=MT[8 + cc], in_=mst[s]))(s, cc),
                     reads=[("mst", s)], writes=[("scr", "mt", 8 + cc)], dma="mst%d" % s)

            build_hT(list(range(17, 33)), hT_b, gBm, gBm_keys, xt_b, xn_b, sqj_b, stat_b, [0, 2, 4, 6],
                     hook=lambda j: ln_chunk(j - 1) if 1 <= j <= 8 else None)
            S.barrier()

        if "A" in phases:
            phase_proj(True)
        if stop_after != "A" and "B" in phases:
            phase_proj(False, prebuilt=("A" in phases))

        def phase_attn():
            A.reset(base_mark)
            wo = A.alloc([DC, D], BF16)
            kt = [A.alloc([S_LEN], BF16) for _ in range(2)]
            vv = [A.alloc([32, 128], BF16) for _ in range(2)]
            qt = [A.alloc([2, 2048], BF16) for _ in range(2)]
            ee = [A.alloc([9 * 512], BF16) for _ in range(2)]
            bb = [A.alloc([9 * 512], F32)] * 2
            pbuf = [A.alloc([2, 512], BF16) for _ in range(3)]
            r0 = A.alloc([512], F32)
            r1 = A.alloc([512], F32)
            t0 = A.alloc([512], F32)
            t1 = A.alloc([512], F32)
            osq = A.alloc([512], F32)
            cO = [A.alloc([512], F32) for _ in range(2)]
            cL = [A.alloc([512], F32) for _ in range(2)]
            ohi = A.alloc([512], BF16)
            olo = A.alloc([512], BF16)
            ostg = [A.alloc([2048], BF16) for _ in range(2)]
            OB = [4, 5]
            LB = [6, 7]

            def head_loads(h):
                s = h % 2
                S.op("sp", (lambda s, h: lambda e: e.dma_start(out=kt[s], in_=KT[h]))(s, h), writes=[("kt", s)], dma="kt%d" % s)
                S.op("sp", (lambda s, h: lambda e: e.dma_start(out=vv[s], in_=VS[h]))(s, h), writes=[("vv", s)], dma="vv%d" % s)
                S.op("sp", (lambda s, h: lambda e: e.dma_start(out=qt[s], in_=QT[h].rearrange("c p t -> p c t")))(s, h), writes=[("qt", s)], dma="qt%d" % s)
                S.op("sp", (lambda s, h: lambda e: e.dma_start(out=bb[s], in_=bias_t[h]))(s, h), writes=[("bb", 0)], dma="bb0")

            def head_ee(h):
                s = h % 2
                S.op("act", (lambda s, h: lambda e: e.activation(out=ee[s], in_=bb[s], func=AF.Exp, bias=negc[:, h:h + 1]))(s, h),
                     reads=[("bb", 0), "negc"], writes=[("ee", s)])

            def group_units(i):
                units = []
                for pr in range(i):
                    for t in range(8):
                        if pr == i - 1 and t == 7:
                            continue
                        units.append((8 * pr + t, None))
                if i >= 1:
                    units.append((8 * (i - 1) + 7, 0))
                for t in range(8):
                    units.append((8 * i + t, 1 + t))
                return units

            st_ = dict(slot=0, pu=0)

            def s_op(h, i, kp):
                s = h % 2
                sl = st_["slot"] % 2
                st_["slot"] += 1

                def fn(e):
                    e.matmul(bank(2 * sl), lhsT=kt[s][:, 128 * kp:128 * kp + 128], rhs=qt[s][:, 0, 512 * i:512 * i + 512], start=True, stop=True)
                    return e.matmul(bank(2 * sl + 1), lhsT=kt[s][:, 128 * kp:128 * kp + 128], rhs=qt[s][:, 1, 512 * i:512 * i + 512],
                                    start=True, stop=True)
                S.op("pe", fn, reads=[("kt", s), ("qt", s)], writes=[("ps", 2 * sl), ("ps", 2 * sl + 1)])
                return sl

            groups = [(h, i) for h in range(nheads) for i in range(4)]
            for c in range(DC):
                S.op("pool", (lambda c: lambda e: e.dma_start(out=wo[:, c, :], in_=w_out[128 * c:128 * c + 128, :]))(c),
                     writes=[("wo", c)], dma="wo%d" % (c % 4))
            head_loads(0)
            head_ee(0)
            pre_slot = None
            pending = [None]
            pending15 = [None]
            for gi, (h, i) in enumerate(groups):
                s = h % 2
                if i == 0 and h + 1 < nheads:
                    head_loads(h + 1)
                units = group_units(i)
                nu = len(units)
                cur = pre_slot if pre_slot is not None else s_op(h, i, units[0][0])
                for u in range(nu):
                    kp, nj = units[u]
                    nxt = s_op(h, i, units[u + 1][0]) if u + 1 < nu else None
                    pk = st_["pu"] % 3
                    st_["pu"] += 1
                    pb_ = pbuf[pk]
                    if nj is None:
                        src = psum_t[:, 1024 * cur:1024 * cur + 1024].rearrange("p (c q) -> p c q", q=512)
                        S.op("act", (lambda pb_, src: lambda e: e.activation(out=pb_, in_=src, func=AF.Exp))(pb_, src),
                             reads=[("ps", 2 * cur), ("ps", 2 * cur + 1)], writes=[("pb", pk, 0), ("pb", pk, 1)])
                    else:
                        for c in range(2):
                            S.op("act", (lambda pb_, c, cur: lambda e: e.activation(out=pb_[:, c, :], in_=bank(2 * cur + c), func=AF.Exp))(pb_, c, cur),
                                 reads=[("ps", 2 * cur + c)], writes=[("pb", pk, c)])
                            S.op("dve", (lambda pb_, nj, s, c: lambda e: e.tensor_tensor(out=pb_[:, c, :], in0=pb_[:, c, :],
                                                                                      in1=ee[s][:, 512 * nj:512 * nj + 512], op=ALU.mult))(pb_, nj, s, c),
                                 reads=[("pb", pk, c), ("ee", s)], writes=[("pb", pk, c)])
                    for c in range(2):
                        def pv(e, pb_=pb_, kp=kp, u=u, s=s, nu=nu, c=c):
                            e.matmul(bank(OB[c]), lhsT=vv[s][:, kp, :], rhs=pb_[:, c, :], start=(u == 0), stop=(u == nu - 1))
                            return e.matmul(bank(LB[c]), lhsT=ones_bf, rhs=pb_[:, c, :], start=(u == 0), stop=(u == nu - 1))
                        S.op("pe", pv, reads=[("vv", s), ("pb", pk, c), "ones_bf"], writes=[("ps", OB[c]), ("ps", LB[c])])
                    cur = nxt
                    if u == 2 and pending15[0] is not None:
                        pending15[0]()
                        pending15[0] = None
                    if u == min(nu - 2, 10) and pending[0] is not None:
                        pending[0]()
                        pending[0] = None
                if i == 0 and h + 1 < nheads:
                    head_ee(h + 1)
                if gi + 1 < len(groups):
                    nh_, ni_ = groups[gi + 1]
                    pre_slot = s_op(nh_, ni_, group_units(ni_)[0][0])
                else:
                    pre_slot = None
                S.op("act", lambda e: e.activation(out=cO[0], in_=bank(OB[0]), func=AF.Copy), reads=[("ps", OB[0])], writes=["cO0"])
                S.op("dve", lambda e: e.tensor_copy(out=cL[0], in_=bank(LB[0])), reads=[("ps", LB[0])], writes=["cL0"])
                S.op("act", lambda e: e.activation(out=cO[1], in_=bank(OB[1]), func=AF.Copy), reads=[("ps", OB[1])], writes=["cO1"])
                S.op("dve", lambda e: e.tensor_copy(out=cL[1], in_=bank(LB[1])), reads=[("ps", LB[1])], writes=["cL1"])
                def part15():
                  S.op("act", lambda e: e.activation(out=r0, in_=cL[0], func=AF.Ln), reads=["cL0"], writes=["r0"])
                  S.op("act", lambda e: e.activation(out=r0, in_=r0, func=AF.Exp, scale=-1.0), reads=["r0"], writes=["r0"])
                  S.op("act", lambda e: e.activation(out=r1, in_=cL[1], func=AF.Ln), reads=["cL1"], writes=["r1"])
                  S.op("act", lambda e: e.activation(out=r1, in_=r1, func=AF.Exp, scale=-1.0), reads=["r1"], writes=["r1"])
                  S.op("dve", lambda e: e.tensor_tensor(out=t0, in0=cO[0], in1=r0, op=ALU.mult), reads=["cO0", "r0"], writes=["t0"])
                  S.op("dve", lambda e: e.tensor_tensor(out=t1, in0=cO[1], in1=r1, op=ALU.mult), reads=["cO1", "r1"], writes=["t1"])
                  S.op("dve", lambda e: e.scalar_tensor_tensor(out=t0, in0=t1, scalar=neglam, in1=t0, op0=ALU.mult, op1=ALU.add),
                       reads=["t0", "t1", "neglam"], writes=["t0"])
                  S.op("dve", lambda e: e.tensor_tensor(out=osq, in0=t0, in1=t0, op=ALU.mult), reads=["t0"], writes=["osq"])
                  S.op("dve", lambda e: e.tensor_copy(out=ohi, in_=osq), reads=["osq"], writes=["ohi"])
                  S.op("dve", lambda e: e.tensor_tensor(out=olo, in0=osq, in1=ohi, op=ALU.subtract), reads=["osq", "ohi"], writes=["olo"])


                def part2(s=s, i=i, h=h):
                    NBk = 2 * (st_["slot"] % 2)

                    def nmm(e, NBk=NBk):
                        e.matmul(bank(NBk), lhsT=ones_bf, rhs=ohi, start=True, stop=False)
                        return e.matmul(bank(NBk), lhsT=ones_bf, rhs=olo, start=False, stop=True)
                    S.op("pe", nmm, reads=["ohi", "olo", "ones_bf"], writes=[("ps", NBk)])
                    S.op("dve", (lambda NBk: lambda e: e.tensor_scalar(out=r0, in0=bank(NBk), scalar1=1.0 / 128, scalar2=EPS, op0=ALU.mult, op1=ALU.add))(NBk),
                         reads=[("ps", NBk)], writes=["r0"])
                    S.op("act", lambda e: e.activation(out=r0, in_=r0, func=AF.Ln), reads=["r0"], writes=["r0"])
                    S.op("act", lambda e: e.activation(out=r0, in_=r0, func=AF.Exp, scale=-0.5), reads=["r0"], writes=["r0"])
                    S.op("dve", (lambda s, i: lambda e: e.scalar_tensor_tensor(out=ostg[s][:, 512 * i:512 * i + 512], in0=t0, scalar=subg08, in1=r0,
                                                                              op0=ALU.mult, op1=ALU.mult))(s, i),
                         reads=["t0", "r0", "subg08"], writes=[("ostg", s, i)])
                    if i == 3:
                        S.op("sp", (lambda s, h: lambda e: e.dma_start(out=MT[h], in_=ostg[s]))(s, h),
                             reads=[("ostg", s, i_) for i_ in range(4)], writes=[("scr", "mt", h)], dma="ostg%d" % s)
                if gi + 1 < len(groups):
                    pending15[0] = part15
                    pending[0] = part2
                else:
                    part15()
                    part2()
            S.barrier()

        if stop_after not in ("A", "B") and "C" in phases:
            phase_attn()

        def phase_outproj():
            A.reset(base_mark)
            wo = A.alloc([DC, D], BF16)
            gpost = A.alloc([D], F32)
            mt = [A.alloc([DC, 512], BF16) for _ in range(2)]
            xt = [A.alloc([D], F32) for _ in range(2)]
            x1 = [A.alloc([D], F32) for _ in range(2)]
            xn = [A.alloc([D], BF16) for _ in range(2)]
            sqj = A.alloc([D], BF16)
            h2s = [A.alloc([DC, 512], BF16) for _ in range(2)]
            stat = A.alloc([16], F32)
            S.op("sp", lambda e: e.dma_start(out=gpost, in_=g_post[0]), writes=["gpost"], dma="gpost")
            if "C" not in phases:
                for c in range(DC):
                    S.op("pool", (lambda c: lambda e: e.dma_start(out=wo[:, c, :], in_=w_out[128 * c:128 * c + 128, :]))(c),
                         writes=[("wo", c)], dma="wo%d" % (c % 4))
            wok = [("wo", c) for c in range(DC)]
            MTv = MT.rearrange("c p t -> p c t")
            def d_front(tt):
                blk, tl = tt // 4, tt % 4
                bs = blk % 2
                s = tt % 2
                pb0 = 4 * s
                if tl == 0:
                    S.op("sp", (lambda bs, blk: lambda e: e.dma_start(out=mt[bs], in_=MTv[:, :, 512 * blk:512 * blk + 512]))(bs, blk),
                         writes=[("mt", bs)], dma="mt%d" % bs)
                S.op("sp", (lambda s, tt: lambda e: e.dma_start(out=xt[s], in_=x_in[128 * tt:128 * tt + 128, :]))(s, tt),
                     writes=[("xt", s)], dma="xt%d" % s)

                def fn(e, bs=bs, tl=tl, pb0=pb0):
                    for j in range(4):
                        for c in range(DC):
                            ins = e.matmul(bank(pb0 + j), lhsT=mt[bs][:, c, 128 * tl:128 * tl + 128], rhs=wo[:, c, 512 * j:512 * j + 512],
                                           start=(c == 0), stop=(c == DC - 1))
                    return ins
                S.op("pe", fn, reads=[("mt", bs)] + wok, writes=[("ps", pb0 + j) for j in range(4)])

            def d_back(tt):
                blk, tl = tt // 4, tt % 4
                bs = blk % 2
                s = tt % 2
                pb0 = 4 * s
                pk = [("ps", pb0 + j) for j in range(4)]
                pv_ = psum_t[:, 512 * pb0:512 * pb0 + 2048]
                S.op("act", (lambda s, pv_: lambda e: e.activation(out=sqj, in_=pv_, func=AF.Square,
                                                                  accum_out=stat[:, 8 * s:8 * s + 1]))(s, pv_),
                     reads=pk, writes=["sqj", ("ss", s)])
                rstd_ops(("o", s), stat[:, 8 * s:8 * s + 1], D, stat[:, 8 * s + 2:8 * s + 3], stat[:, 8 * s + 1:8 * s + 2], ("ss", s), ("rs", s))
                S.op("dve", (lambda s, pv_: lambda e: e.scalar_tensor_tensor(out=x1[s], in0=pv_, scalar=stat[:, 8 * s + 2:8 * s + 3],
                                                                            in1=gpost, op0=ALU.mult, op1=ALU.mult))(s, pv_),
                     reads=pk + [("rs", s), "gpost"], writes=[("x1", s)])
                S.op("dve", (lambda s: lambda e: e.tensor_tensor(out=x1[s], in0=x1[s], in1=xt[s], op=ALU.add))(s),
                     reads=[("x1", s), ("xt", s)], writes=[("x1", s)])
                S.op("sp", (lambda s, tt: lambda e: e.dma_start(out=out[128 * tt:128 * tt + 128, :], in_=x1[s]))(s, tt),
                     reads=[("x1", s)], writes=[("out", tt)], dma="x1o%d" % s)
                S.op("act", (lambda s: lambda e: e.activation(out=sqj, in_=x1[s], func=AF.Square,
                                                             accum_out=stat[:, 8 * s + 4:8 * s + 5]))(s),
                     reads=[("x1", s)], writes=["sqj", ("ss2", s)])
                rstd_ops(("o2", s), stat[:, 8 * s + 4:8 * s + 5], D, stat[:, 8 * s + 6:8 * s + 7], stat[:, 8 * s + 5:8 * s + 6], ("ss2", s), ("rs2", s))
                S.op("dve", (lambda s: lambda e: e.tensor_scalar(out=xn[s], in0=x1[s], scalar1=stat[:, 8 * s + 6:8 * s + 7],
                                                                 scalar2=None, op0=ALU.mult))(s),
                     reads=[("x1", s), ("rs2", s)], writes=[("xn", s)])
                pT = psum_t[:, 512 * pb0:512 * pb0 + 1024].bitcast(BF16).rearrange("p (a b) -> p a b", b=128)

                def tr(e, s=s, pT=pT):
                    for dc in range(DC):
                        ins = e.transpose(pT[:, dc, :], xn[s][:, 128 * dc:128 * dc + 128], ident)
                    return ins
                S.op("pe", tr, reads=[("xn", s), "ident"], writes=[("ps", pb0), ("ps", pb0 + 1)])
                S.op("dve", (lambda bs, tl, pT: lambda e: e.tensor_tensor(out=h2s[bs][:, :, 128 * tl:128 * tl + 128], in0=pT, in1=gBf,
                                                                          op=ALU.mult))(bs, tl, pT),
                     reads=[("ps", pb0), ("ps", pb0 + 1)] + gBf_keys, writes=[("h2s", bs, tl)])
                if tl == 3:
                    S.op("sp", (lambda bs, blk: lambda e: e.dma_start(out=H2[:, :, 512 * blk:512 * blk + 512], in_=h2s[bs]))(bs, blk),
                         reads=[("h2s", bs, t_) for t_ in range(4)], writes=[("scr", "h2", blk)], dma="h2s%d" % bs)

            for tt in range(17):
                if tt < 16:
                    d_front(tt)
                if tt > 0:
                    d_back(tt - 1)
            S.barrier()

        if stop_after not in ("A", "B", "C") and "D" in phases:
            phase_outproj()

        def phase_ffn():
            A.reset(base_mark)
            gpost = A.alloc([D], F32)
            h2 = [A.alloc([DC, 512], BF16)] * 2
            actT = A.alloc([FC, 512], BF16)
            wg = [A.alloc([DC, 512], BF16) for _ in range(2)]
            wu = [A.alloc([DC, 512], BF16) for _ in range(2)]
            wd = [A.alloc([D], BF16) for _ in range(5)]
            sgt = [A.alloc([512], F32) for _ in range(2)]
            x1 = [A.alloc([D], F32)] * 2
            y = [A.alloc([D], F32) for _ in range(2)]
            sqj = A.alloc([D], BF16)
            stat = A.alloc([8], F32)
            S.op("sp", lambda e: e.dma_start(out=gpost, in_=g_post[1]), writes=["gpost"], dma="gpost2")
            wdi = 0
            for blk in range(4):
                bs = 0
                S.op("sp", (lambda bs, blk: lambda e: e.dma_start(out=h2[bs], in_=H2[:, :, 512 * blk:512 * blk + 512]))(bs, blk),
                     writes=[("h2", bs)], dma="h2%d" % bs)
                for fc in range(FC):
                    ws = (fc // 4) % 2
                    if fc % 4 == 0:
                        S.op("pool", (lambda ws, fc: lambda e: e.dma_start(out=wg[ws], in_=w_gate_v[:, :, 128 * fc:128 * fc + 512]))(ws, fc),
                             writes=[("wg", ws)], dma="wg%d" % ws)
                        S.op("pool", (lambda ws, fc: lambda e: e.dma_start(out=wu[ws], in_=w_up_v[:, :, 128 * fc:128 * fc + 512]))(ws, fc),
                             writes=[("wu", ws)], dma="wu%d" % ws)
                    pg, pu = 2 * (fc % 2), 2 * (fc % 2) + 1
                    fo = 128 * (fc % 4)

                    def fn(e, wt, pb, bs=bs, fo=fo):
                        for dc in range(DC):
                            ins = e.matmul(bank(pb), lhsT=wt[:, dc, fo:fo + 128], rhs=h2[bs][:, dc, :], start=(dc == 0), stop=(dc == DC - 1))
                        return ins
                    S.op("pe", (lambda fn, ws, pg: lambda e: fn(e, wg[ws], pg))(fn, ws, pg), reads=[("wg", ws), ("h2", bs)], writes=[("ps", pg)])
                    S.op("pe", (lambda fn, ws, pu: lambda e: fn(e, wu[ws], pu))(fn, ws, pu), reads=[("wu", ws), ("h2", bs)], writes=[("ps", pu)])
                    S.op("act", (lambda fc, pg: lambda e: e.activation(out=sgt[fc % 2], in_=bank(pg), func=AF.Silu))(fc, pg),
                         reads=[("ps", pg)], writes=[("sgt", fc % 2)])
                    S.op("dve", (lambda fc, pu: lambda e: e.tensor_tensor(out=actT[:, fc, :], in0=sgt[fc % 2], in1=bank(pu), op=ALU.mult))(fc, pu),
                         reads=[("sgt", fc % 2), ("ps", pu)], writes=[("actT", fc)])
                for half in range(2):
                    for tl in range(2):
                        tt = 4 * blk + 2 * half + tl
                        S.op("act", (lambda tl, tt: lambda e: e.dma_start(out=x1[tl], in_=out[128 * tt:128 * tt + 128, :]))(tl, tt),
                             reads=[("out", tt)], writes=[("x1", 0)], dma="x1i0") if tl == 0 else None
                    for fc in range(FC):
                        ws = wdi % 5
                        wdi += 1
                        S.op("sp", (lambda ws, fc: lambda e: e.dma_start(out=wd[ws], in_=WDB[fc]))(ws, fc),
                             writes=[("wd", ws)], dma="wd%d" % ws, extra_deps=wdb_ops)

                        for tl in range(2):
                            def fn(e, ws=ws, fc=fc, half=half, tl=tl):
                                t = 2 * half + tl
                                for j in range(4):
                                    ins = e.matmul(bank(4 * tl + j), lhsT=actT[:, fc, 128 * t:128 * t + 128], rhs=wd[ws][:, 512 * j:512 * j + 512],
                                                   start=(fc == 0), stop=(fc == FC - 1))
                                return ins
                            S.op("pe", fn, reads=[("wd", ws), ("actT", fc)], writes=[("ps", 4 * tl + b_) for b_ in range(4)])
                    for tl in range(2):
                        s = tl
                        pk = [("ps", 4 * tl + j) for j in range(4)]
                        pv_ = psum_t[:, 2048 * tl:2048 * tl + 2048]
                        S.op("act", (lambda s, pv_: lambda e: e.activation(out=sqj, in_=pv_, func=AF.Square, accum_out=stat[:, 4 * s:4 * s + 1]))(s, pv_),
                             reads=pk, writes=["sqj", ("ss", s)])
                        rstd_ops(("f", s), stat[:, 4 * s:4 * s + 1], D, stat[:, 4 * s + 2:4 * s + 3], stat[:, 4 * s + 1:4 * s + 2], ("ss", s), ("rs", s))
                        S.op("dve", (lambda s, pv_: lambda e: e.scalar_tensor_tensor(out=y[s], in0=pv_, scalar=stat[:, 4 * s + 2:4 * s + 3],
                                                                                    in1=gpost, op0=ALU.mult, op1=ALU.mult))(s, pv_),
                             reads=pk + [("rs", s), "gpost"], writes=[("y", s)])
                    for tl in range(2):
                        tt = 4 * blk + 2 * half + tl
                        s = tl
                        S.op("dve", (lambda s: lambda e: e.tensor_tensor(out=y[s], in0=y[s], in1=x1[0], op=ALU.add))(s),
                             reads=[("y", s), ("x1", 0)], writes=[("y", s)])
                        if tl == 0:
                            tt1 = tt + 1
                            S.op("act", (lambda tt1: lambda e: e.dma_start(out=x1[0], in_=out[128 * tt1:128 * tt1 + 128, :]))(tt1),
                                 reads=[("out", tt1)], writes=[("x1", 0)], dma="x1i0")
                        S.op("act", (lambda s, tt: lambda e: e.dma_start(out=out[128 * tt:128 * tt + 128, :], in_=y[s]))(s, tt),
                             reads=[("y", s)], writes=[("out", tt)], dma="yo%d" % s)
            S.barrier()

        if stop_after not in ("A", "B", "C", "D") and "E" in phases:
            phase_ffn()
        S.barrier(final=True)
        S.emit(nc, st)
    return nc


def _t5_bucket_np(n):
    n = np.maximum(n, 0)
    max_exact = 16
    nf = np.maximum(n, 1).astype(np.float32)
    large = max_exact + (np.log(nf / np.float32(max_exact)) / np.float32(math.log(128 / max_exact))
                         * np.float32(32 - max_exact)).astype(np.int32)
    large = np.minimum(large, 31)
    return np.where(n < max_exact, n, large)


def _bias_tiles(rel_bias, p):
    ki = np.arange(128)[:, None, None]
    a = (np.arange(512) // 128)[None, None, :]
    qi = (np.arange(512) % 128)[None, None, :]
    j = np.arange(9)[None, :, None]
    t = j - 1
    dt = np.where(j == 0, 8 * p + a + 1, np.where(t < 4, a - t, 8 * p + a - t))
    n = 128 * dt + qi - ki
    bucket = _t5_bucket_np(n)
    vals = rel_bias[bucket]
    vals = np.where((n >= 0)[..., None], vals, np.float32(MASK))
    return np.ascontiguousarray(np.transpose(vals, (3, 0, 1, 2)).reshape(8, 128, 9 * 512).astype(np.float32))


def _core_inputs(c, inp):
    b, p = c // 2, c % 2
    x = inp["x"][b]
    own = [2 * i + p for i in range(4)]
    oth = [2 * i + 1 - p for i in range(4)]
    halo = np.zeros((128, D), np.float32)
    for i, gb in enumerate(own):
        if gb > 0:
            halo[32 * i:32 * i + 32] = x[512 * gb - 32:512 * gb]
    xc = np.concatenate([x[512 * g:512 * g + 512] for g in own] + [halo] + [x[512 * g:512 * g + 512] for g in oth], axis=0)
    return xc


def _small_params(inp):
    f = lambda a: np.asarray(a, np.float32)
    cols = [
        f(inp["norm_pre_mix"][0]).reshape(16, 128).T,
        f(inp["norm_pre_ffn"][0]).reshape(16, 128).T,
        f(inp["conv_dw_w"][0]).T.reshape(8, 128, NTAP).transpose(1, 0, 2).reshape(128, 8 * NTAP),
        f(inp["conv_dw_b"][0]).reshape(8, 128).T,
        f(inp["conv_ln_g"][0]).reshape(8, 128).T,
        f(inp["conv_ln_b"][0]).reshape(8, 128).T,
        f(inp["subln_g"][0]).reshape(128, 1),
        np.broadcast_to(f(inp["rel_bias"])[31][None, :], (128, 8)),
        np.broadcast_to(np.concatenate([f(inp["lambda_q1"][0]), f(inp["lambda_k1"][0]),
                                        f(inp["lambda_q2"][0]), f(inp["lambda_k2"][0])])[None, :], (128, 256)),
    ]
    return np.ascontiguousarray(np.concatenate(cols, axis=1).astype(np.float32))


_NC_CACHE = {}


def kernel(**inputs):
    inp = {k: np.asarray(v) for k, v in inputs.items()}
    if "nc" not in _NC_CACHE:
        _NC_CACHE["nc"] = build_nc()
    nc = _NC_CACHE["nc"]
    small = _small_params(inp)
    gpost = np.ascontiguousarray(np.stack([np.broadcast_to(inp["norm_post_mix"][0][None, :], (128, D)),
                                           np.broadcast_to(inp["norm_post_ffn"][0][None, :], (128, D))]).astype(np.float32))
    rel = np.asarray(inp["rel_bias"], np.float32)
    bt = [_bias_tiles(rel, 0), _bias_tiles(rel, 1)]
    shared = dict(w_in=np.ascontiguousarray(inp["w_in"][0]), w_out=np.ascontiguousarray(inp["w_out"][0]),
                  w_gate=np.ascontiguousarray(inp["w_gate"][0]), w_up=np.ascontiguousarray(inp["w_up"][0]),
                  w_down=np.ascontiguousarray(inp["w_down"][0]), p_small=small, g_post=gpost)
    in_maps = []
    for c in range(N_CORES):
        m = dict(shared)
        m["x"] = _core_inputs(c, inp)
        m["bias_t"] = bt[c % 2]
        in_maps.append(m)
    res = run_bass_kernel_spmd(nc, in_maps, core_ids=list(range(N_CORES)))
    outp = np.empty((4, S_LEN, D), np.float32)
    for c in range(N_CORES):
        b, p = c // 2, c % 2
        o = res.results[c]["out"]
        for i in range(4):
            gb = 2 * i + p
            outp[b, 512 * gb:512 * gb + 512] = o[512 * i:512 * i + 512]
    return outp
```
